# Optimizing a Trainium2 kernel written in Bass

```python
import math
import jax, jax.numpy as jnp
from jax import lax
import numpy as np

D_MODEL = 1024
BATCH = 16
SEQ = 2048
DEPTH = 4

CHUNK = 64
Q_BLOCK = 128
N_A = DEPTH // 2
N_B = DEPTH - N_A
EXPAND_A = 2
E_A = EXPAND_A * D_MODEL
POOL_WINDOWS = (2, 4, 8, 16)
N_POOL_GROUPS = len(POOL_WINDOWS)
G_A = E_A // N_POOL_GROUPS
N_HEADS_B = D_MODEL // 128
HEAD_DIM_B = 64
V_DIM_B = 2 * HEAD_DIM_B
QK_B = N_HEADS_B * 2 * HEAD_DIM_B
E_B = N_HEADS_B * V_DIM_B
EPS = 1e-6
SUBLN_EPS = 1e-5

kernel_name = "hybrid_pool_diffattn_yoco_trunk"


def lambda_init_fn(layer_idx):
    return 0.8 - 0.6 * math.exp(-0.3 * layer_idx)


def rms_norm(x, g, eps=EPS):
    xf = x.astype(jnp.float32)
    y = xf * lax.rsqrt(jnp.mean(xf * xf, axis=-1, keepdims=True) + eps)
    return (y * g.astype(jnp.float32)).astype(x.dtype)


def modulate(h, shift, scale):
    return h * (1.0 + scale[:, None, :]) + shift[:, None, :]


def pool_mixer(h, w_in, w_group, ch_scale, w_out):
    B, S, _ = h.shape
    u, z = jnp.split(h @ w_in, 2, axis=-1)
    uf = u.astype(jnp.float32).reshape(B, S, N_POOL_GROUPS, G_A)
    cs = jnp.cumsum(uf, axis=1)
    cs = jnp.concatenate([jnp.zeros_like(cs[:, :1]), cs], axis=1)
    t = jnp.arange(S)
    pooled = []
    for g, w in enumerate(POOL_WINDOWS):
        lo = jnp.maximum(t + 1 - w, 0)
        cnt = jnp.minimum(t + 1, w).astype(jnp.float32)
        win_sum = cs[:, 1:, g] - jnp.take(cs[:, :, g], lo, axis=1)
        pooled.append(win_sum / cnt[None, :, None])
    pooled = jnp.stack(pooled, axis=2) - uf
    mixed = jnp.einsum('bsgi,gio->bsgo', pooled.astype(u.dtype), w_group)
    mixed = mixed.reshape(B, S, E_A) * ch_scale
    return (mixed * jax.nn.silu(z)) @ w_out


def shared_kv(x, kv_norm, kv_shift, kv_scale, w_kv):
    B, S, _ = x.shape
    hk = modulate(rms_norm(x, kv_norm), kv_shift, kv_scale)
    kv = hk @ w_kv
    k = kv[..., :QK_B].reshape(B, S, N_HEADS_B, 2, HEAD_DIM_B)
    v = kv[..., QK_B:].reshape(B, S, N_HEADS_B, V_DIM_B)
    return k, v


def diff_attention(h, k, v, w_in, lam_vec, subln_g, w_out, lam_init):
    B, S, _ = h.shape
    q, z = jnp.split(h @ w_in, 2, axis=-1)
    q = q.reshape(B, S, N_HEADS_B, 2, HEAD_DIM_B)
    lv = lam_vec.astype(jnp.float32)
    lam = jnp.exp(jnp.sum(lv[0] * lv[1])) - jnp.exp(jnp.sum(lv[2] * lv[3])) + lam_init
    n_blk = S // Q_BLOCK
    qb = q.reshape(B, n_blk, Q_BLOCK, N_HEADS_B, 2, HEAD_DIM_B).transpose(1, 0, 2, 3, 4, 5)
    key_chunk = jnp.arange(S) // CHUNK
    sm_scale = HEAD_DIM_B ** -0.5

    def block(args):
        q_i, i = args
        s = jnp.einsum('bqhcd,bkhcd->bhcqk', q_i, k).astype(jnp.float32) * sm_scale
        q_chunk = (i * Q_BLOCK + jnp.arange(Q_BLOCK)) // CHUNK
        mask = key_chunk[None, :] <= q_chunk[:, None]
        p = jax.nn.softmax(jnp.where(mask, s, -jnp.inf), axis=-1)
        a = p[:, :, 0] - lam * p[:, :, 1]
        return jnp.einsum('bhqk,bkhv->bqhv', a.astype(v.dtype), v)

    o = lax.map(block, (qb, jnp.arange(n_blk)))
    o = o.transpose(1, 0, 2, 3, 4).reshape(B, S, N_HEADS_B, V_DIM_B)
    o = rms_norm(o, subln_g, SUBLN_EPS) * (1.0 - lam_init)
    y = o.reshape(B, S, E_B) * jax.nn.silu(z)
    return y @ w_out


def setup_inputs(seed: int = 0) -> dict:
    key = jax.random.key(seed)
    ks = jax.random.split(key, 20)
    f32 = jnp.float32
    D = D_MODEL

    def nrm(k, shape, s):
        return jax.random.normal(k, shape, f32) * s

    return {
        "x": nrm(ks[0], (BATCH, SEQ, D), 1.0),
        "c": nrm(ks[1], (BATCH, D), 1.0),
        "ada_w": nrm(ks[2], (DEPTH, D, 3 * D), 0.5 * D ** -0.5),
        "ada_b": nrm(ks[3], (DEPTH, 3 * D), 0.02),
        "norm_pre": 1.0 + nrm(ks[4], (DEPTH, D), 0.05),
        "norm_post": 1.0 + nrm(ks[5], (DEPTH, D), 0.05),
        "a_w_in": nrm(ks[6], (N_A, D, 2 * E_A), D ** -0.5),
        "a_w_group": nrm(ks[7], (N_A, N_POOL_GROUPS, G_A, G_A), G_A ** -0.5),
        "a_scale": 1.0 + nrm(ks[8], (N_A, E_A), 0.1),
        "a_w_out": nrm(ks[9], (N_A, E_A, D), E_A ** -0.5),
        "kv_norm": 1.0 + nrm(ks[10], (D,), 0.05),
        "kv_ada_w": nrm(ks[11], (D, 2 * D), 0.5 * D ** -0.5),
        "kv_ada_b": nrm(ks[12], (2 * D,), 0.02),
        "w_kv": nrm(ks[13], (D, QK_B + E_B), D ** -0.5),
        "b_w_in": nrm(ks[14], (N_B, D, QK_B + E_B), D ** -0.5),
        "b_lambda": nrm(ks[15], (N_B, 4, HEAD_DIM_B), 0.1),
        "b_subln": 1.0 + nrm(ks[16], (N_B, V_DIM_B), 0.05),
        "b_w_out": nrm(ks[17], (N_B, E_B, D), E_B ** -0.5),
    }


def reference(x, c, ada_w, ada_b, norm_pre, norm_post, a_w_in, a_w_group, a_scale,
              a_w_out, kv_norm, kv_ada_w, kv_ada_b, w_kv, b_w_in, b_lambda, b_subln,
              b_w_out):
    cond = jax.nn.silu(c)
    k = v = None
    for l in range(DEPTH):
        shift, scale, gate = jnp.split(cond @ ada_w[l] + ada_b[l], 3, axis=-1)
        h = modulate(rms_norm(x, norm_pre[l]), shift, scale)
        if l < N_A:
            y = pool_mixer(h, a_w_in[l], a_w_group[l], a_scale[l], a_w_out[l])
        else:
            if l == N_A:
                kv_shift, kv_scale = jnp.split(cond @ kv_ada_w + kv_ada_b, 2, axis=-1)
                k, v = shared_kv(x, kv_norm, kv_shift, kv_scale, w_kv)
            j = l - N_A
            y = diff_attention(h, k, v, b_w_in[j], b_lambda[j], b_subln[j], b_w_out[j],
                               lambda_init_fn(l))
        x = x + gate[:, None, :] * rms_norm(y, norm_post[l])
    return x
```

```python
import numpy as np
import concourse.bass as bass
import concourse.mybir as mybir
from concourse.bass_utils import run_bass_kernel_spmd

F32 = mybir.dt.float32
BF16 = mybir.dt.bfloat16
AF = mybir.ActivationFunctionType
ALU = mybir.AluOpType
AX = mybir.AxisListType

P = 128
D = 1024
S = 2048
NB = 16
NT = 4
DEPTH = 4
NCORES = 8
SEQ_PER_CORE = 2
EPS = 1e-6
SUBLN_EPS = 1e-5
POOL_W = (2, 4, 8, 16)

C_CT = 0
C_NPRE = 16
C_NPOST = 48
C_KVN = 80
C_ADAB = 88
C_KVADAB = 280
C_ASC = 312
C_SUBLN = 344
C_LAMB = 346
NCOLS = 864
NMATS = 128 + 8 * 144


def lambda_init_fn(layer_idx):
    import math
    return 0.8 - 0.6 * math.exp(-0.3 * layer_idx)


class Chan:
    def __init__(self, sem, step):
        self.sem = sem
        self.step = step
        self.count = 0


class Op:
    __slots__ = ("eng", "fn", "deps", "chan", "val", "need", "dma")

    def __init__(self, eng, fn, chan, dma):
        self.eng = eng
        self.fn = fn
        self.chan = chan
        self.dma = dma
        self.deps = []
        self.val = None
        self.need = dma


class Sched:
    ENGS = ("pe", "act", "dve", "pool", "sp")

    def __init__(self, nc):
        self.nc = nc
        self.ops = []
        self.streams = {e: [] for e in self.ENGS}
        self.lastw = {}
        self.readers = {}
        self.echan = {}
        for e in ("pe", "act", "dve", "pool"):
            self.echan[e] = Chan(nc.alloc_semaphore(name="e_" + e), 1)

    def new_chan(self, name):
        return Chan(self.nc.alloc_semaphore(name=name), 16)

    def op(self, eng, fn, R=(), W=(), chan=None, extra=()):
        dma = chan is not None
        o = Op(eng, fn, chan if dma else self.echan.get(eng), dma)
        deps = set(extra)
        for r in R:
            w = self.lastw.get(r)
            if w is not None:
                deps.add(w)
        for w_ in W:
            w = self.lastw.get(w_)
            if w is not None:
                deps.add(w)
            for rd in self.readers.get(w_, ()):
                deps.add(rd)
        for r in R:
            self.readers.setdefault(r, []).append(o)
        for w_ in W:
            self.lastw[w_] = o
            self.readers[w_] = []
        deps.discard(o)
        while any(d.fn is None for d in deps):
            nd = set()
            for d in deps:
                if d.fn is None:
                    nd.update(d.deps)
                else:
                    nd.add(d)
            deps = nd
        if eng == "pe":
            deps = [d for d in deps if not (d.eng == "pe" and not d.dma)]
        o.deps = list(deps)
        self.ops.append(o)
        self.streams[eng].append(o)
        return o

    def barrier(self):
        last = {}
        for o in self.ops:
            if o.fn is not None:
                last[o.chan] = o
        tails = list(last.values())
        for e in self.ENGS:
            w = Op(e, None, None, False)
            w.deps = [t for t in tails if not (t.eng == e and not t.dma and e == "pe")]
            self.ops.append(w)
            self.streams[e].append(w)
        self.lastw = {}
        self.readers = {}

    def emit(self):
        for o in self.ops:
            for d in o.deps:
                d.need = True
        for o in self.ops:
            if o.fn is not None and o.need:
                o.chan.count += o.chan.step
                o.val = o.chan.count
        nc = self.nc
        with nc.Block() as block:
            decos = {"pe": block.tensor, "act": block.scalar, "dve": block.vector,
                     "pool": block.gpsimd, "sp": block.sync}
            for name in self.ENGS:
                stream = self.streams[name]

                def body(e, stream=stream):
                    seen = {}
                    for o in stream:
                        waits = {}
                        for d in o.deps:
                            if waits.get(d.chan, 0) < d.val:
                                waits[d.chan] = d.val
                        for ch, v in waits.items():
                            if seen.get(ch, 0) < v:
                                e.wait_ge(ch.sem, v)
                                seen[ch] = v
                        if o.fn is not None:
                            ins = o.fn(e)
                            if o.val is not None:
                                ins.then_inc(o.chan.sem, o.chan.step)

                decos[name](body)


class Builder:
    def __init__(self, n_layers=DEPTH):
        self.n_layers = n_layers
        nc = bass.Bass("TRN2", target_bir_lowering=False)
        self.nc = nc
        self.s = Sched(nc)
        dt = nc.dram_tensor
        self.x_d = dt("x", [SEQ_PER_CORE, S, D], F32, kind="ExternalInput").ap()
        self.cols_d = dt("cols", [P, NCOLS], F32, kind="ExternalInput").ap()
        self.mats_d = dt("mats", [P, NMATS], F32, kind="ExternalInput").ap()
        self.ada_w_d = dt("ada_w", [DEPTH, D, 3 * D], F32, kind="ExternalInput").ap()
        self.a_w_in_d = dt("a_w_in", [2, D, 4096], F32, kind="ExternalInput").ap()
        self.a_w_group_d = dt("a_w_group", [2, 2048, 512], F32, kind="ExternalInput").ap()
        self.a_w_out_d = dt("a_w_out", [2, 2048, D], F32, kind="ExternalInput").ap()
        self.kv_ada_w_d = dt("kv_ada_w", [D, 2048], F32, kind="ExternalInput").ap()
        self.w_kv_d = dt("w_kv", [D, 2048], F32, kind="ExternalInput").ap()
        self.b_w_in_d = dt("b_w_in", [2, D, 2048], F32, kind="ExternalInput").ap()
        self.b_w_out_d = dt("b_w_out", [2, D, D], F32, kind="ExternalInput").ap()
        self.out_d = dt("out", [SEQ_PER_CORE, S, D], F32, kind="ExternalOutput").ap()
        self.sc = {}
        self.sc_src = {}

        def scr(name, src, rows, cols):
            self.sc[name] = dt("sc_" + name, [rows, cols], BF16).ap()
            self.sc_src[name] = (src, rows, cols)

        for l in range(2):
            scr(f"ain{l}", self.a_w_in_d[l], D, 4096)
            scr(f"ag{l}", self.a_w_group_d[l], 2048, 512)
            scr(f"aout{l}", self.a_w_out_d[l], 2048, D)
        scr("wkv", self.w_kv_d, D, 2048)
        for l in range(1, DEPTH):
            scr(f"ada{l}", self.ada_w_d[l], D, 3 * D)
        scr("kvada", self.kv_ada_w_d, D, 2048)
        for j in range(2):
            scr(f"bin{j}", self.b_w_in_d[j], D, 2048)
            scr(f"bout{j}", self.b_w_out_d[j], D, D)
        self.conv_res = {}
        self.conv_q = []
        self.mod_q = []
        self.ps_pool = 8
        self.deferred = None
        self.next_A = None
        self.next_gate = None
        self.fifo2 = []
        self.alloc()

    def alloc(self):
        nc = self.nc
        off = 0

        def take(n):
            nonlocal off
            o = off
            off += (n + 63) // 64 * 64
            return o

        o_x = take(NB * D * 4)
        o_kv = take(65536)
        o_flex = take(44032)
        o_cols = take(NCOLS * 4)
        o_identf = take(512)
        o_identb = take(256)
        o_onesb = take(256)
        o_onesf = take(512)
        o_bands = take(8 * 144 * 2)
        o_mod = take((4 * 48 + 32) * 4)
        o_small = take(1024)
        o_abf = take(2048)
        o_akv = take(2048)
        o_gf = take(4096)
        o_htm = take(4096)
        o_hT = take(8192)
        o_tmp = take(4096)
        o_junk = take(2048)
        o_mask = take(384)
        total = off
        assert total <= nc.sbuf_bytes_remaining, (total, nc.sbuf_bytes_remaining)
        self.arena = nc.alloc_sbuf_tensor("arena", [P, total // 2], BF16)

        def view(o, dtype, *shape):
            n = 1
            for d_ in shape:
                n *= d_
            esz = 4 if dtype == F32 else 2
            ap = self.arena[:, o // 2: o // 2 + n * esz // 2]
            if dtype == F32:
                ap = ap.bitcast(F32)
            if len(shape) == 2:
                ap = ap.rearrange("p (a b) -> p a b", a=shape[0])
            elif len(shape) == 3:
                ap = ap.rearrange("p (a b c) -> p a b c", a=shape[0], b=shape[1])
            return ap

        self.view = view
        self.x_sb = view(o_x, F32, NB, D)
        self.stage = view(o_x, F32, 8 * 144)
        self.kT = view(o_kv, BF16, 8, S)
        self.V = view(o_kv + 32768, BF16, NB, D)
        self.u_tm = view(o_kv, BF16, 4, 2048)
        self.szA = view(o_kv + 16384, BF16, 16, 512)
        self.gT = view(o_kv + 32768, BF16, 16, 512)
        self.pooledT = [view(o_kv + 49152 + i * 4096, BF16, 4, 512) for i in range(2)]
        self.halo = view(o_kv + 57344, BF16, 2048)
        self.sigA = [view(o_kv + 61440 + i * 2048, F32, 512) for i in range(2)]
        self.o_flex = o_flex
        self.ring_slots = {"A": [o_flex + i * 8192 for i in range(5)],
                           "B": [o_flex + i * 8192 for i in range(2)]}
        ob = o_flex + 16384
        self.yT = view(ob, BF16, 8, 512)
        self.q_h = [[view(ob + 8192 + (i * 2 + c) * 1024, BF16, 512) for c in range(2)]
                    for i in range(2)]
        self.pT = [[view(ob + 12288 + (i * 2 + c) * 1024, BF16, 512) for c in range(2)]
                   for i in range(2)]
        self.fsc = [view(ob + 16384 + i * 2048, F32, 512) for i in range(4)]
        self.f01 = view(ob + 16384, F32, 1024)
        self.sq = view(ob + 24576, BF16, 512)
        self.szb = [view(ob + 25600 + i * 1024, BF16, 512) for i in range(2)]
        assert ob + 27648 <= o_flex + 44032
        self.cols = view(o_cols, F32, NCOLS)
        self.ident_f = view(o_identf, F32, 128)
        self.ident_b = view(o_identb, BF16, 128)
        self.ones_b = view(o_onesb, BF16, 128)
        self.ones_f = view(o_onesf, F32, 128)
        self.bands = view(o_bands, BF16, 8, 144)
        self.modcol = [view(o_mod + l * 192, F32, 48) for l in range(4)]
        self.modkv = view(o_mod + 768, F32, 32)
        small = view(o_small, F32, 256)
        self.small = small
        self.A_bf = view(o_abf, BF16, D)
        self.Akv_bf = view(o_akv, BF16, D)
        self.G_f = view(o_gf, F32, D)
        self.h_tm = [view(o_htm + i * 2048, BF16, D) for i in range(2)]
        self.hT = view(o_hT, BF16, 8, 512)
        self.tmp = view(o_tmp, F32, D)
        self.junk = view(o_junk, BF16, D)
        self.maskrow = view(o_mask, BF16, 192)
        self.ps_all = nc.alloc_psum_tensor("ps_all", [P, 4096], F32)
        self.ps = [self.ps_all[:, i * 512:(i + 1) * 512] for i in range(8)]
        self.rot = 0
        self.ring_n = 0
        self.ring_mode = "A"
        self.ring_ch = [self.s.new_chan(f"ring{i}") for i in range(5)]
        self.ring_ch_sw = [self.s.new_chan(f"ringsw{i}") for i in range(5)]
        self.x_ch = [self.s.new_chan(f"xch{i}") for i in range(NB)]
        self.x_ch_hw = [self.s.new_chan(f"xchhw{i}") for i in range(NB)]
        self.const_ch = self.s.new_chan("constch")
        self.conv_ch = {}

    def bank(self, pool=8):
        b = self.rot % pool
        self.rot += 1
        return b

    def ring_get(self, name, r0, nk, dram_view, shape, split=None, dtype=BF16, eng="sp"):
        slots = self.ring_slots[self.ring_mode]
        slot = self.ring_n % len(slots)
        self.ring_n += 1
        v = self.view(slots[slot], dtype, *shape)
        res = ("ring", slot)
        if split is None:
            self.s.op(eng, lambda e, o=v, i=dram_view: e.dma_start(out=o, in_=i),
                      R=self.conv_res.get(name, []), W=[res],
                      chan=(self.ring_ch if eng == "sp" else self.ring_ch_sw)[slot])
        else:
            subs = []
            for i_, (ov, iv) in enumerate(split(v)):
                sr = ("ringpart", slot, i_)
                subs.append(self.s.op("sp", lambda e, o=ov, i=iv: e.dma_start(out=o, in_=i),
                                      R=self.conv_res[name], W=[sr], chan=self.ring_ch[slot],
                                      extra=[d for d in [self.s.lastw.get(res)] if d is not None]
                                      + list(self.s.readers.get(res, ()))))
            j = Op("none", None, None, False)
            j.deps = subs
            self.s.lastw[res] = j
            self.s.readers[res] = []
        return v, res

    def slab(self, name, r0, nk, c0, ncol):
        src = self.sc[name][r0:r0 + nk * P, c0:c0 + ncol].rearrange("(k p) n -> p k n", p=P)
        return self.ring_get((name, c0) if (name, c0) in self.conv_res else name, r0, nk, src, (nk, ncol))

    def mm(self, out, lhsT, rhs, start, stop, R, W):
        return self.s.op(
            "pe",
            lambda e, o=out, l=lhsT, r=rhs, a=start, b=stop: e.matmul(
                o, lhsT=l, rhs=r, start=a, stop=b, skip_group_check=True),
            R=R, W=W)

    def act(self, out, in_, func, R, W, bias=None, scale=None, accum=None):
        kw = {}
        if bias is not None:
            kw["bias"] = bias
        if scale is not None:
            kw["scale"] = scale
        if accum is not None:
            kw["accum_out"] = accum
        return self.s.op("act", lambda e, o=out, i=in_, f=func, kw=kw: e.activation(
            out=o, in_=i, func=f, **kw), R=R, W=W)

    def dve(self, fn, R, W):
        return self.s.op("dve", fn, R=R, W=W)

    def copy(self, eng, out, in_, R, W):
        if eng == "act":
            return self.act(out, in_, AF.Copy, R, W)
        return self.s.op(eng, lambda e, o=out, i=in_: e.tensor_copy(out=o, in_=i), R=R, W=W)

    def convert_weights(self, names, defer=False):
        for name in names:
            src, rows, cols = self.sc_src[name]
            res = []
            if name.startswith("ain"):
                for c0 in [2048, 2560, 3072, 3584, 0, 512, 1024, 1536]:
                    ch = self.s.new_chan(f"cv_{name}_{c0}")
                    rs = ("conv", name, c0)
                    self.conv_q.append(lambda o=self.sc[name][:, c0:c0 + 512], s_=src[:, c0:c0 + 512], rs=rs, ch=ch:
                                       self.s.op("pool", lambda e: e.dma_start(out=o, in_=s_), R=[], W=[rs], chan=ch))
                    self.conv_res[(name, c0)] = [rs]
                    res.append(rs)
            else:
                ch = self.s.new_chan("cv_" + name)
                step = max(128, (1 << 19) // cols)
                for i, r0 in enumerate(range(0, rows, step)):
                    r1 = min(rows, r0 + step)
                    rs = ("conv", name, i)
                    self.conv_q.append(lambda o=self.sc[name][r0:r1, :], s_=src[r0:r1, :], rs=rs, ch=ch:
                                       self.s.op("pool", lambda e: e.dma_start(out=o, in_=s_), R=[], W=[rs], chan=ch))
                    res.append(rs)
            self.conv_res[name] = res
        if not defer:
            self.issue_conv(1.0)

    def issue_conv(self, frac):
        n = int(round(len(self.conv_q) * frac)) if frac < 1.0 else len(self.conv_q)
        for _ in range(n):
            self.conv_q.pop(0)()

    def prologue(self):
        s = self.s
        sm = self.small
        s.op("sp", lambda e: e.dma_start(out=self.cols, in_=self.cols_d), W=["cols"],
             chan=self.s.new_chan("constch1"))
        s.op("sp", lambda e: e.dma_start(out=self.ident_f, in_=self.mats_d[:, 0:128]),
             W=["ident_f"], chan=self.s.new_chan("constch2"))
        self.copy("act", self.ident_b, self.ident_f, ["ident_f"], ["ident_b"])
        s.op("pool", lambda e: e.memset(self.ones_b, 1.0), W=["ones_b"])
        s.op("pool", lambda e: e.memset(self.ones_f, 1.0), W=["ones_f"])
        c = self.cols[:, C_CT:C_CT + 16]
        t0 = sm[:, 0:16]
        self.act(t0, c, AF.Exp, ["cols"], ["ctmp"], scale=-1.0)
        self.dve(lambda e: e.tensor_scalar_add(out=t0, in0=t0, scalar1=1.0), ["ctmp"], ["ctmp"])
        self.dve(lambda e: e.reciprocal(out=t0, in_=t0), ["ctmp"], ["ctmp"])
        self.condT = self.view_small_bf16()
        self.dve(lambda e: e.tensor_tensor(out=self.condT, in0=t0, in1=c, op=ALU.mult),
                 ["ctmp", "cols"], ["condT"])
        for j in range(2):
            lb = self.cols[:, C_LAMB + j * 256: C_LAMB + (j + 1) * 256]
            pr = sm[:, 128:192]
            for h_ in range(2):
                self.dve(lambda e, a=lb[:, h_ * 128:h_ * 128 + 64], b=lb[:, h_ * 128 + 64:h_ * 128 + 128]:
                         e.tensor_tensor(out=pr, in0=a, in1=b, op=ALU.mult), ["cols"], ["lprod"])
                self.dve(lambda e, o=sm[:, 114 + h_:115 + h_]: e.reduce_sum(out=o, in_=pr, axis=AX.X),
                         ["lprod"], [("lsum", h_)])
                self.act(sm[:, 116 + h_:117 + h_], sm[:, 114 + h_:115 + h_], AF.Exp,
                         [("lsum", h_)], [("lexp", h_)])
            lam = sm[:, 108 + j:109 + j]
            self.dve(lambda e, o=lam: e.tensor_tensor(out=o, in0=sm[:, 116:117], in1=sm[:, 117:118],
                                                      op=ALU.subtract),
                     [("lexp", 0), ("lexp", 1)], [("lam", j)])
            li = lambda_init_fn(2 + j)
            self.dve(lambda e, o=lam, li=li: e.tensor_scalar_add(out=o, in0=o, scalar1=float(li)),
                     [("lam", j)], [("lam", j)])
            self.dve(lambda e, o=sm[:, 110 + j:111 + j], i=lam: e.tensor_scalar_mul(out=o, in0=i, scalar1=-1.0),
                     [("lam", j)], [("neglam", j)])
            self.dve(lambda e, o=sm[:, 112 + j:113 + j], i=self.cols[:, C_SUBLN + j:C_SUBLN + j + 1], li=li:
                     e.tensor_scalar_mul(out=o, in0=i, scalar1=float(1.0 - li)), ["cols"], [("gsub", j)])

    def modcols(self, l, run_now=False):
        name = f"ada{l}" if l < DEPTH else "kvada"
        nch = 24 if l < DEPTH else 16
        dst = self.modcol[l] if l < DEPTH else self.modkv
        bcol = (self.cols[:, C_ADAB + l * 48:C_ADAB + (l + 1) * 48] if l < DEPTH
                else self.cols[:, C_KVADAB:C_KVADAB + 32])
        srcw = self.ada_w_d[l] if l < DEPTH else self.kv_ada_w_d

        def step(sl):
            b = self.bank(self.ps_pool)
            bres = ("ps", b)
            if l == 0:
                src = srcw[:, sl * 512:(sl + 1) * 512].rearrange("(k p) n -> p k n", p=P)
                w, wres = self.ring_get(None, 0, 8, src, (8, 512), eng="pool")
            else:
                w, wres = self.slab(name, 0, 8, sl * 512, 512)
            for cc in range(4):
                for k in range(8):
                    self.mm(self.ps[b][:, 2 * cc:2 * cc + 2], w[:, k, cc * 128:(cc + 1) * 128],
                            self.condT[:, 2 * k:2 * k + 2], k == 0, k == 7,
                            [wres, "condT"], [bres])
            self.dve(lambda e, o=dst[:, 8 * sl:8 * sl + 8], i=self.ps[b][:, 0:8], bc=bcol[:, 8 * sl:8 * sl + 8]:
                     e.tensor_tensor(out=o, in0=i, in1=bc, op=ALU.add), [bres, "cols"], [("mod", l)])

        for sl in range(nch // 4):
            self.mod_q.append(lambda sl=sl: step(sl))
        if run_now:
            self.mod_flush()

    def mod_step(self):
        if self.mod_q:
            self.mod_q.pop(0)()

    def mod_flush(self):
        while self.mod_q:
            self.mod_q.pop(0)()

    def view_small_bf16(self):
        v = self.small[:, 120:128].bitcast(BF16)
        return v

    def bcast_tile(self, col8, dst, dst_res, evac_eng):
        diag = self.tmp
        for k in range(8):
            self.dve(lambda e, o=diag[:, k * 128:(k + 1) * 128], sc=col8[:, k:k + 1]:
                     e.tensor_scalar_mul(out=o, in0=self.ident_f, scalar1=sc),
                     ["ident_f", "col8"], [("tmpk", k)])
        for half in range(2):
            b = self.bank(self.ps_pool)
            self.mm(self.ps[b][:, :], self.ones_f, diag[:, half * 512:(half + 1) * 512], True, True,
                    ["ones_f"] + [("tmpk", k) for k in range(half * 4, half * 4 + 4)], [("ps", b)])
            self.copy(evac_eng, dst[:, half * 512:(half + 1) * 512], self.ps[b][:, :], [("ps", b)],
                      [dst_res])

    def modA(self, l, s_):
        sm = self.small
        mc = self.modcol[l]
        mc3 = mc.rearrange("p (j s) -> p j s", s=2)
        shift = mc3[:, 0:8, s_]
        scale = mc3[:, 8:16, s_]
        gate = mc3[:, 16:24, s_]
        acol = sm[:, 84:92]
        gcol = sm[:, 92:100]
        npre = self.cols[:, C_NPRE + l * 8:C_NPRE + (l + 1) * 8]
        npost = self.cols[:, C_NPOST + l * 8:C_NPOST + (l + 1) * 8]
        self.shift_col = shift
        if l == 2:
            kc3 = self.modkv.rearrange("p (j s) -> p j s", s=2)
            kcol = sm[:, 100:108]
            kvn = self.cols[:, C_KVN:C_KVN + 8]
            self.dve(lambda e: e.scalar_tensor_tensor(out=kcol, in0=kc3[:, 8:16, s_], scalar=1.0, in1=kvn,
                                                      op0=ALU.add, op1=ALU.mult),
                     [("mod", 4), "cols"] + [("tmpk", k) for k in range(8)], ["col8"])
            self.bcast_tile(kcol, self.Akv_bf, "Akv_bf", "act")
            self.kvshift_col = kc3[:, 0:8, s_]
        self.dve(lambda e: e.scalar_tensor_tensor(out=acol, in0=scale, scalar=1.0, in1=npre,
                                                  op0=ALU.add, op1=ALU.mult),
                 [("mod", l), "cols"] + [("tmpk", k) for k in range(8)], ["col8"])
        self.bcast_tile(acol, self.A_bf, "A_bf", "act")

        def gate_tile():
            self.dve(lambda e: e.tensor_tensor(out=gcol, in0=gate, in1=npost, op=ALU.mult),
                     [("mod", l), "cols"] + [("tmpk", k) for k in range(8)], ["col8"])
            self.bcast_tile(gcol, self.G_f, "G_f", "act")

        return gate_tile

    def call_next_A(self):
        if self.next_A is not None:
            f = self.next_A
            self.next_A = None
            f()

    def run_deferred(self):
        if self.deferred is not None:
            d = self.deferred
            self.deferred = None
            d()

    def flush_done(self, keep):
        while len(self.done_fifo) > keep:
            gb = self.done_fifo.pop(0)
            if self.block_done is not None:
                self.block_done(gb)

    def sumsq_block(self, gb):
        self.act(self.junk, self.x_sb[:, gb, :], AF.Square, [("x", gb)], ["junk", ("ss", gb)],
                 accum=self.small[:, 16 + gb:17 + gb])

    def prenorm_stats(self, have_ss=False):
        sm = self.small
        if not have_ss:
            for gb in range(NB):
                self.sumsq_block(gb)
        self.act(sm[:, 32:48], sm[:, 16:32], AF.Ln, [("ss", gb) for gb in range(NB)], ["ln16"],
                 scale=1.0 / D, bias=self.eps_col(EPS))
        self.act(sm[:, 48:64], sm[:, 32:48], AF.Exp, ["ln16"], ["rstd16"], scale=-0.5)

    def eps_col(self, eps):
        return self.small[:, 200:201] if eps == EPS else self.small[:, 201:202]

    def make_h(self, t, gain, gain_res, shiftcol, shift_res):
        sm = self.small
        banks = [self.bank() for _ in range(4)]
        for tb in range(4):
            gb = t * 4 + tb
            hb = self.h_tm[tb % 2]
            hres = ("h_tm", tb % 2)
            self.dve(lambda e, o=hb, x_=self.x_sb[:, gb, :], r=sm[:, 48 + gb:49 + gb]:
                     e.scalar_tensor_tensor(out=o, in0=x_, scalar=r, in1=gain, op0=ALU.mult, op1=ALU.mult),
                     [("x", gb), "rstd16", gain_res], [hres])
            for k in range(8):
                b = banks[k // 2]
                dst = self.ps[b].bitcast(BF16)[:, (k % 2) * 512 + tb * 128:(k % 2) * 512 + (tb + 1) * 128]
                self.s.op("pe", lambda e, o=dst, i=hb[:, k * 128:(k + 1) * 128]:
                          e.transpose(o, i, self.ident_b), R=[hres, "ident_b"], W=[("ps", b)])
        for k in range(8):
            b = banks[k // 2]
            src = self.ps[b].bitcast(BF16)[:, (k % 2) * 512:(k % 2) * 512 + 512]
            self.act(self.hT[:, k, :], src, AF.Identity, [("ps", b), shift_res], [("hT", k)],
                     bias=shiftcol[:, k:k + 1])

    def postnorm_update(self, t, tb, b0):
        sm = self.small
        gb = t * 4 + tb
        b1 = b0 + 1
        for hf, b in enumerate((b0, b1)):
            self.act(self.junk[:, hf * 512:(hf + 1) * 512], self.ps[b][:, :], AF.Square, [("ps", b)],
                     ["junk", ("ssy", hf)], accum=sm[:, 64 + hf:65 + hf])
        self.dve(lambda e: e.tensor_tensor(out=sm[:, 72:73], in0=sm[:, 64:65], in1=sm[:, 65:66], op=ALU.add),
                 [("ssy", 0), ("ssy", 1)], ["sst"])
        self.act(sm[:, 76:77], sm[:, 72:73], AF.Ln, ["sst"], ["lny"], scale=1.0 / D, bias=self.eps_col(EPS))
        self.act(sm[:, 80:81], sm[:, 76:77], AF.Exp, ["lny"], ["rstdy"], scale=-0.5)
        for hf, b in enumerate((b0, b1)):
            self.dve(lambda e, o=self.tmp[:, hf * 512:(hf + 1) * 512], i=self.ps[b][:, :],
                     g=self.G_f[:, hf * 512:(hf + 1) * 512]:
                     e.scalar_tensor_tensor(out=o, in0=i, scalar=sm[:, 80:81], in1=g, op0=ALU.mult, op1=ALU.mult),
                     [("ps", b), "rstdy", "G_f"], [("tmpk", 4 * hf + q) for q in range(4)])
        self.s.op("pool", lambda e, x_=self.x_sb[:, gb, :]: e.tensor_tensor(out=x_, in0=x_, in1=self.tmp, op=ALU.add),
                  R=[("tmpk", k) for k in range(8)] + [("x", gb)], W=[("x", gb)])
        if self.block_store is not None:
            self.block_store(gb)
        self.done_fifo.append(gb)
        self.flush_done(self.done_lag)

    def sigmoid_from_psum(self, zps, zres, scratch, sres):
        self.act(scratch, zps, AF.Exp, [zres], [sres], scale=-1.0)
        self.dve(lambda e: e.tensor_scalar_add(out=scratch, in0=scratch, scalar1=1.0), [sres], [sres])
        self.dve(lambda e: e.reciprocal(out=scratch, in_=scratch), [sres], [sres])

    def layerA(self, l, s_):
        self.ring_mode = "A"
        self.ps_pool = 8
        self.prenorm_stats(self.have_ss)
        self.make_h(0, self.A_bf, "A_bf", self.shift_col, ("mod", l))
        for t in range(NT):
            for zs in range(4):
                w, wres = self.slab(f"ain{l}", 0, 8, 2048 + zs * 512, 512)
                for cc in range(4):
                    j = zs * 4 + cc
                    b = self.bank()
                    for k in range(8):
                        self.mm(self.ps[b][:, :], w[:, k, cc * 128:(cc + 1) * 128], self.hT[:, k, :],
                                k == 0, k == 7, [wres, ("hT", k)], [("ps", b)])
                    self.act(self.szA[:, j, :], self.ps[b][:, :], AF.Silu, [("ps", b)], [("szA", j)])
                self.run_deferred()
                self.mod_step()
            if t > 0:
                self.s.op("pool", lambda e: e.tensor_copy(out=self.halo, in_=self.u_tm[:, 3, :]),
                          R=[("u_tm", 3, g) for g in range(4)], W=["halo"])
            for g in range(4):
                w, wres = self.slab(f"ain{l}", 0, 8, g * 512, 512)
                for tb in range(4):
                    b = self.bank()
                    for k in range(8):
                        self.mm(self.ps[b][:, :], self.hT[:, k, tb * 128:(tb + 1) * 128], w[:, k, :],
                                k == 0, k == 7, [wres, ("hT", k)], [("ps", b)])
                    self.copy("act" if tb % 2 == 0 else "dve", self.u_tm[:, tb, g * 512:(g + 1) * 512],
                              self.ps[b][:, :], [("ps", b)], [("u_tm", tb, g)])
            if t + 1 < NT:
                self.make_h(t + 1, self.A_bf, "A_bf", self.shift_col, ("mod", l))
            else:
                self.call_next_A()
            def pool_group(g):
                pg = self.pooledT[g % 2]
                pres = ("pooledT", g % 2)
                band = self.bands[:, g, :]
                bandf = self.bands[:, 4 + g, :]
                for c in range(4):
                    cc = g * 4 + c
                    b = self.bank()
                    first = True
                    if t > 0:
                        self.mm(self.ps[b][:, 0:16], self.halo[:, cc * 128:(cc + 1) * 128], band[:, 128:144],
                                True, False, ["halo", "bands"], [("ps", b)])
                        first = False
                    for tb in range(4):
                        ncol = 144 if tb < 3 else 128
                        bm = bandf if (t == 0 and tb == 0) else band
                        self.mm(self.ps[b][:, tb * 128:tb * 128 + ncol], self.u_tm[:, tb, cc * 128:(cc + 1) * 128],
                                bm[:, 0:ncol], first, tb == 3, [("u_tm", tb, g), "bands"], [("ps", b)])
                        first = False
                    self.copy("dve", pg[:, c, :], self.ps[b][:, :], [("ps", b)], [pres])

            def wg_group(g):
                pg = self.pooledT[g % 2]
                pres = ("pooledT", g % 2)
                w, wres = self.slab(f"ag{l}", g * 512, 4, 0, 512)
                for c in range(4):
                    cc = g * 4 + c
                    b = self.bank()
                    for k in range(4):
                        self.mm(self.ps[b][:, :], w[:, k, c * 128:(c + 1) * 128], pg[:, k, :],
                                k == 0, k == 3, [wres, pres], [("ps", b)])
                    self.dve(lambda e, o=self.gT[:, cc, :], i=self.ps[b][:, :],
                             sc=self.cols[:, C_ASC + l * 16 + cc:C_ASC + l * 16 + cc + 1], z=self.szA[:, cc, :]:
                             e.scalar_tensor_tensor(out=o, in0=i, scalar=sc, in1=z, op0=ALU.mult, op1=ALU.mult),
                             [("ps", b), "cols", ("szA", cc)], [("gT", cc)])

            pool_group(0)
            pool_group(1)
            wg_group(0)
            pool_group(2)
            wg_group(1)
            pool_group(3)
            wg_group(2)
            wg_group(3)
            for half in range(2):
                ws = [self.slab(f"aout{l}", kh * 1024, 8, half * 512, 512) for kh in range(2)]
                for tb in range(4):
                    b = 2 * tb + half
                    for kc in range(16):
                        w, wres = ws[kc // 8]
                        self.mm(self.ps[b][:, :], self.gT[:, kc, tb * 128:(tb + 1) * 128], w[:, kc % 8, :],
                                kc == 0, kc == 15, [wres, ("gT", kc)], [("ps", b)])
                    if half == 1:
                        self.postnorm_update(t, tb, 2 * tb)
            self.rot = 0
            self.issue_conv({0: 0.25, 1: 0.45, 2: 1.0, 3: 1.0}[t])
            if t == 1 and self.after_tile1 is not None:
                self.after_tile1()

    def layerB(self, l, s_):
        j = l - 2
        sm = self.small
        self.ring_mode = "B"
        self.ps_pool = 4
        self.ring_n = 0
        for i_ in range(2):
            self.s.op("pool", lambda e, o=self.q_h[i_][0][64:128, :]: e.memset(o, 0.0), W=["qzero"])
            self.s.op("pool", lambda e, o=self.q_h[i_][1][0:64, :]: e.memset(o, 0.0), W=["qzero"])
        self.prenorm_stats(self.have_ss)
        neglam = sm[:, 110 + j:111 + j]
        gsub = sm[:, 112 + j:113 + j]
        for t in range(NT):
            if l == 2:
                if t == 0:
                    self.make_h(t, self.Akv_bf, "Akv_bf", self.kvshift_col, ("mod", 4))
                for hs in range(2):
                    w, wres = self.slab("wkv", 0, 8, hs * 512, 512)
                    for hh in range(4):
                        hd = hs * 4 + hh
                        b = self.bank()
                        for k in range(8):
                            self.mm(self.ps[b][:, :], w[:, k, hh * 128:(hh + 1) * 128], self.hT[:, k, :],
                                    k == 0, k == 7, [wres, ("hT", k)], [("ps", b)])
                        self.copy("act", self.kT[:, hd, t * 512:(t + 1) * 512], self.ps[b][:, :],
                                  [("ps", b)], [("kT", hd, t)])
                for hs in range(2):
                    w, wres = self.slab("wkv", 0, 8, 1024 + hs * 512, 512)
                    for tb in range(4):
                        b = self.bank()
                        for k in range(8):
                            self.mm(self.ps[b][:, :], self.hT[:, k, tb * 128:(tb + 1) * 128], w[:, k, :],
                                    k == 0, k == 7, [wres, ("hT", k)], [("ps", b)])
                        self.copy("dve", self.V[:, t * 4 + tb, hs * 512:(hs + 1) * 512], self.ps[b][:, :],
                                  [("ps", b)], [("V", t * 4 + tb)])
            if l == 2 or t == 0:
                self.make_h(t, self.A_bf, "A_bf", self.shift_col, ("mod", l))
            if t == NT - 1:
                self.call_next_A()
            pending = None
            nkb = 4 * t + 4

            def head_pre(hd):
                srcq = self.sc[f"bin{j}"][:, hd * 128:(hd + 1) * 128].rearrange("(k p) e -> p k e", p=P)
                srcz = self.sc[f"bin{j}"][:, 1024 + hd * 128:1024 + (hd + 1) * 128].rearrange("(k p) e -> p k e", p=P)
                w, wres = self.ring_get(f"bin{j}", 0, 8, None, (8, 2, 128),
                                        split=lambda v, a=srcq, b_=srcz: [(v[:, :, 0, :], a), (v[:, :, 1, :], b_)])
                qb = self.q_h[hd % 2]
                qres = ("q_h", hd % 2)
                b = self.bank(4)
                for k in range(8):
                    self.mm(self.ps[b][:, :], w[:, k, 0, :], self.hT[:, k, :], k == 0, k == 7,
                            [wres, ("hT", k)], [("ps", b)])
                self.copy("dve", qb[0][0:64, :], self.ps[b][0:64, :], [("ps", b), "qzero"], [qres])
                self.copy("dve", qb[1][64:128, :], self.ps[b][64:128, :], [("ps", b), "qzero"], [qres])
                zb = self.bank(4)
                for k in range(8):
                    self.mm(self.ps[zb][:, :], w[:, k, 1, :], self.hT[:, k, :], k == 0, k == 7,
                            [wres, ("hT", k)], [("ps", zb)])
                self.copy("dve", self.szb[hd % 2], self.ps[zb][:, :], [("ps", zb)], [("szb", hd % 2)])
                self.run_deferred()
                self.mod_step()

            head_pre(0)
            for hd in range(8):
                qb = self.q_h[hd % 2]
                qres = ("q_h", hd % 2)
                szb = self.szb[hd % 2]
                szr = ("szb", hd % 2)
                sgt = self.tmp[:, (hd % 2) * 512:(hd % 2) * 512 + 512]
                sgr = [("tmpk", 4 * (hd % 2) + q_) for q_ in range(4)]

                def emit_S(kb):
                    c0 = 0 if kb < 4 * t else (kb - 4 * t) * 128
                    diag = kb >= 4 * t
                    pbuf = self.pT[kb % 2]
                    for c in range(2):
                        b = self.bank(4)
                        self.mm(self.ps[b][:, c0:512], self.kT[:, hd, kb * 128:(kb + 1) * 128],
                                qb[c][:, c0:512], True, not diag,
                                [("kT", hd, kb // 4), qres], [("ps", b)])
                        if diag:
                            self.mm(self.ps[b][:, c0:c0 + 64], self.maskrow[0:1, 0:128], self.maskrow[0:1, 128:192],
                                    False, True, ["maskrow"], [("ps", b)])
                        self.act(pbuf[c][:, c0:512], self.ps[b][:, c0:512], AF.Exp, [("ps", b)],
                                 [("pT", kb % 2, c)], scale=0.125)

                emit_S(0)
                for kb in range(nkb):
                    if kb + 1 < nkb:
                        emit_S(kb + 1)
                    c0 = 0 if kb < 4 * t else (kb - 4 * t) * 128
                    pbuf = self.pT[kb % 2]
                    for c in range(2):
                        pres = ("pT", kb % 2, c)
                        vh = self.V[:, kb, hd * 128:(hd + 1) * 128]
                        for which, (ob, lhs_full) in enumerate(((4 + c, vh), (6 + c, self.ones_b))):
                            R_ = [pres, ("V", kb) if which == 0 else "ones_b"]
                            self.mm(self.ps[ob][:, c0:512], lhs_full, pbuf[c][:, c0:512], kb == 0, kb == nkb - 1,
                                    R_, [("ps", ob)])
                    if kb == 0:
                        self.act(sgt, szb, AF.Exp, [szr], sgr, scale=-1.0)
                    elif kb == 1:
                        self.act(sgt, sgt, AF.Ln, sgr, sgr, bias=self.small[:, 202:203])
                    elif kb == 2:
                        if pending is not None:
                            pending()
                            pending = None
                        self.act(sgt, sgt, AF.Exp, sgr, sgr, scale=-1.0)
                        self.dve(lambda e, o=szb, g_=sgt: e.tensor_tensor(out=o, in0=o, in1=g_, op=ALU.mult),
                                 [szr] + sgr, [szr])
                if pending is not None:
                    pending()
                    pending = None
                stage1b = self.head_epilogue(hd, neglam, gsub)
                if hd < 7:
                    head_pre(hd + 1)
                if hd == 6 and t + 1 < NT:
                    if l == 2:
                        self.make_h(t + 1, self.Akv_bf, "Akv_bf", self.kvshift_col, ("mod", 4))
                    else:
                        self.make_h(t + 1, self.A_bf, "A_bf", self.shift_col, ("mod", l))
                pending = stage1b()
            wsl = [self.slab(f"bout{j}", 0, 8, half * 512, 512) for half in range(2)]
            pair = {0: 0, 1: 2, 2: 6, 3: 4}
            for tb in range(4):
                if tb == 2:
                    self.rot = 4
                    pending(8)
                    pending = None
                for half in range(2):
                    w, wres = wsl[half]
                    b = pair[tb] + half
                    for e_ in range(7):
                        self.mm(self.ps[b][:, :], self.yT[:, e_, tb * 128:(tb + 1) * 128], w[:, e_, :],
                                e_ == 0, False, [wres, ("yT", e_)], [("ps", b)])
            for tb in range(4):
                for half in range(2):
                    w, wres = wsl[half]
                    b = pair[tb] + half
                    self.mm(self.ps[b][:, :], self.yT[:, 7, tb * 128:(tb + 1) * 128], w[:, 7, :],
                            False, True, [wres, ("yT", 7)], [("ps", b)])
                self.postnorm_update(t, tb, pair[tb])
            self.issue_conv({0: 0.25, 1: 0.45, 2: 1.0, 3: 1.0}[t])
            if t == 1 and self.after_tile1 is not None:
                self.after_tile1()
            self.rot = 0

    def head_epilogue(self, hd, neglam, gsub):
        f0, f1, f2, f3 = self.fsc
        f01 = self.f01
        o0, o1 = self.ps[4], self.ps[5]
        l01 = self.ps_all[:, 3072:4096]
        self.act(f01, l01, AF.Ln, [("ps", 6), ("ps", 7)], [("f", 0), ("f", 1)])
        self.copy("dve", f2, o0[:, :], [("ps", 4)], [("f", 2)])
        self.copy("dve", f3, o1[:, :], [("ps", 5)], [("f", 3)])
        self.act(f01, f01, AF.Exp, [("f", 0), ("f", 1)], [("f", 0), ("f", 1)], scale=-1.0)

        def stage1b():
            self.dve(lambda e: e.tensor_tensor(out=f2, in0=f2, in1=f0, op=ALU.mult),
                     [("f", 2), ("f", 0)], [("f", 2)])
            self.dve(lambda e: e.scalar_tensor_tensor(out=f3, in0=f3, scalar=neglam, in1=f1,
                                                      op0=ALU.mult, op1=ALU.mult),
                     [("f", 3), ("f", 1), ("neglam", 0), ("neglam", 1)], [("f", 3)])
            self.dve(lambda e: e.tensor_tensor(out=f2, in0=f2, in1=f3, op=ALU.add),
                     [("f", 2), ("f", 3)], [("f", 2)])
            self.dve(lambda e: e.tensor_tensor(out=self.sq, in0=f2, in1=f2, op=ALU.mult), [("f", 2)], ["sq"])

            def stage2(pool=4):
                mb = self.bank(pool)
                self.mm(self.ps[mb][:, :], self.ones_b, self.sq, True, True, ["ones_b", "sq"], [("ps", mb)])
                self.act(f0, self.ps[mb][:, :], AF.Ln, [("ps", mb)], [("f", 0)], scale=1.0 / 128,
                         bias=self.eps_col(SUBLN_EPS))
                self.act(f0, f0, AF.Exp, [("f", 0)], [("f", 0)], scale=-0.5)
                self.dve(lambda e: e.scalar_tensor_tensor(out=f2, in0=f2, scalar=gsub, in1=f0,
                                                          op0=ALU.mult, op1=ALU.mult),
                         [("f", 2), ("f", 0), ("gsub", 0), ("gsub", 1)], [("f", 2)])
                self.dve(lambda e: e.tensor_tensor(out=self.yT[:, hd, :], in0=f2, in1=self.szb[hd % 2], op=ALU.mult),
                         [("f", 2), ("szb", hd % 2)], [("yT", hd)])

            return stage2

        return stage1b

    def build(self):
        s = self.s
        s.op("sp", lambda e: e.dma_start(out=self.stage, in_=self.mats_d[:, 128:NMATS]),
             W=[("x", 0), ("x", 1)], chan=self.const_ch)
        self.copy("dve", self.bands.rearrange("p a b -> p (a b)"), self.stage,
                  [("x", 0), ("x", 1)], ["bands"])
        self.convert_weights(["ain0", "ag0", "aout0", "ada1"])
        s.op("pool", lambda e: e.memset(self.small[:, 200:201], EPS), W=["epsc"])
        s.op("pool", lambda e: e.memset(self.small[:, 201:202], SUBLN_EPS), W=["epsc2"])
        s.op("pool", lambda e: e.memset(self.small[:, 202:203], 1.0), W=["onec"])
        s.op("pool", lambda e: e.memset(self.maskrow[0:1, 0:64], 0.0), W=["maskrow"])
        s.op("pool", lambda e: e.memset(self.maskrow[0:1, 64:128], 1.0), W=["maskrow"])
        s.op("pool", lambda e: e.memset(self.maskrow[0:1, 128:192], -30000.0), W=["maskrow"])
        stores = []
        self.have_ss = False
        self.block_done = None
        for gb in range(NB):
            s.op("sp", lambda e, o=self.x_sb[:, gb, :], i=self.x_d[0, gb * P:(gb + 1) * P, :]:
                 e.dma_start(out=o, in_=i), R=[], W=[("x", gb)], chan=self.x_ch_hw[gb])
        self.prologue()
        self.modcols(0, run_now=True)
        self.next_gate = self.modA(0, 0)
        stages = [(s_, l) for s_ in range(SEQ_PER_CORE) for l in range(self.n_layers)]
        for si, (s_, l) in enumerate(stages):
            self.deferred = self.next_gate
            self.next_gate = None
            if si + 1 < len(stages):
                def nxtA(ns=stages[si + 1]):
                    self.mod_flush()
                    self.next_gate = self.modA(ns[1], ns[0])
                self.next_A = nxtA
            self.after_tile0 = None
            self.after_tile1 = None
            if s_ == 0:
                nxt = {0: ["ada2", "kvada", "ain1", "ag1", "aout1"], 1: ["ada3", "wkv", "bin0", "bout0"],
                       2: ["bin1", "bout1"]}.get(l)
                if nxt:
                    self.convert_weights(nxt, defer=True)
                if l + 1 < self.n_layers:
                    self.modcols(l + 1)
                    if l + 1 == 2:
                        self.modcols(DEPTH)
            last = l + 1 == self.n_layers
            self.done_fifo = []
            self.fifo2 = []
            self.block_store = None
            if not last:
                self.block_done = self.sumsq_block
                self.done_lag = 1
            else:
                def store(gb, s_=s_):
                    stores.append(s.op(
                        "pool", lambda e, i=self.x_sb[:, gb, :], o=self.out_d[s_, gb * P:(gb + 1) * P, :]:
                        e.dma_start(out=o, in_=i), R=[("x", gb)], W=[], chan=self.x_ch[gb]))
                self.block_store = store
                if s_ + 1 < SEQ_PER_CORE:
                    def load_next(gb, s_=s_):
                        s.op("pool", lambda e, o=self.x_sb[:, gb, :], i=self.x_d[s_ + 1, gb * P:(gb + 1) * P, :]:
                             e.dma_start(out=o, in_=i), R=[], W=[("x", gb)], chan=self.x_ch[gb])
                        self.fifo2.append(gb)
                        while len(self.fifo2) > 4:
                            self.sumsq_block(self.fifo2.pop(0))
                    self.block_done = load_next
                else:
                    self.block_done = None
                self.done_lag = 4
            if l < 2:
                self.layerA(l, s_)
            else:
                if l == 2:
                    s.barrier()
                self.layerB(l, s_)
            self.flush_done(0)
            while self.fifo2:
                self.sumsq_block(self.fifo2.pop(0))
            self.run_deferred()
            self.call_next_A()
            self.mod_flush()
            self.have_ss = True
            if self.n_layers > 2 and l + 1 == self.n_layers and s_ + 1 < SEQ_PER_CORE:
                s.barrier()
        fin = Op("pool", None, None, False)
        fin.deps = stores
        s.ops.append(fin)
        s.streams["pool"].append(fin)
        s.emit()
        return self.nc


def band_mats():
    m = np.zeros((P, 8, 144), np.float32)
    tin = np.arange(P)[:, None]
    for g, w in enumerate(POOL_W):
        for first in range(2):
            a = np.zeros((P, 144), np.float32)
            jj = np.arange(128)[None, :]
            dlt = jj - tin
            cnt = np.minimum(jj + 1, w) if first else np.full_like(jj, w)
            a[:, :128] = np.where((dlt >= 0) & (dlt < w), 1.0 / cnt, 0.0) - (dlt == 0)
            j2 = np.arange(16)[None, :]
            d2 = 128 + j2 - tin
            a[:, 128:] = np.where((d2 >= 0) & (d2 < w), 1.0 / w, 0.0)
            m[:, first * 4 + g, :] = a
    return m


def make_cols(core, c, ada_b, norm_pre, norm_post, a_scale, kv_norm, kv_ada_b, b_lambda, b_subln):
    cols = np.zeros((P, NCOLS), np.float32)
    cc = np.asarray(c[core * 2:core * 2 + 2], np.float32)
    cols[:, C_CT:C_CT + 16] = cc.reshape(2, 8, P).transpose(2, 1, 0).reshape(P, 16)
    cols[:, C_NPRE:C_NPRE + 32] = np.asarray(norm_pre).reshape(4, 8, P).transpose(2, 0, 1).reshape(P, 32)
    cols[:, C_NPOST:C_NPOST + 32] = np.asarray(norm_post).reshape(4, 8, P).transpose(2, 0, 1).reshape(P, 32)
    cols[:, C_KVN:C_KVN + 8] = np.asarray(kv_norm).reshape(8, P).T
    ab = np.asarray(ada_b).reshape(4, 24, P).transpose(2, 0, 1)
    cols[:, C_ADAB:C_ADAB + 192] = np.repeat(ab[:, :, :, None], 2, axis=3).reshape(P, 192)
    kb = np.asarray(kv_ada_b).reshape(16, P).T
    cols[:, C_KVADAB:C_KVADAB + 32] = np.repeat(kb[:, :, None], 2, axis=2).reshape(P, 32)
    cols[:, C_ASC:C_ASC + 32] = np.asarray(a_scale).reshape(2, 16, P).transpose(2, 0, 1).reshape(P, 32)
    cols[:, C_SUBLN:C_SUBLN + 2] = np.asarray(b_subln).reshape(2, P).T
    cols[:, C_LAMB:C_LAMB + 512] = np.broadcast_to(np.asarray(b_lambda).reshape(1, 512), (P, 512))
    return cols


_NC_CACHE = {}


def kernel(x, c, ada_w, ada_b, norm_pre, norm_post, a_w_in, a_w_group, a_scale, a_w_out,
           kv_norm, kv_ada_w, kv_ada_b, w_kv, b_w_in, b_lambda, b_subln, b_w_out, _n_layers=DEPTH):
    f = lambda a: np.ascontiguousarray(np.asarray(a, dtype=np.float32))
    x = f(x)
    mats = np.concatenate([np.eye(P, dtype=np.float32), band_mats().reshape(P, 8 * 144)], axis=1)
    shared = {
        "mats": mats, "ada_w": f(ada_w), "a_w_in": f(a_w_in),
        "a_w_group": f(a_w_group).reshape(2, 2048, 512), "a_w_out": f(a_w_out),
        "kv_ada_w": f(kv_ada_w), "w_kv": f(w_kv), "b_w_in": f(b_w_in), "b_w_out": f(b_w_out),
    }
    in_maps = []
    for core in range(NCORES):
        m = dict(shared)
        m["x"] = x[core * 2:core * 2 + 2]
        m["cols"] = make_cols(core, f(c), f(ada_b), f(norm_pre), f(norm_post), f(a_scale),
                              f(kv_norm), f(kv_ada_b), f(b_lambda), f(b_subln))
        in_maps.append(m)
    nc = Builder(_n_layers).build()
    res = run_bass_kernel_spmd(nc, in_maps, core_ids=list(range(NCORES)))
    out = np.concatenate([np.asarray(r["out"]) for r in res.results], axis=0)
    return out.astype(np.float32)
```

```python
import numpy as np
import concourse.bass as bass
import concourse.mybir as mybir
from concourse.bass_utils import run_bass_kernel_spmd

F32 = mybir.dt.float32
BF16 = mybir.dt.bfloat16
AF = mybir.ActivationFunctionType
ALU = mybir.AluOpType
AX = mybir.AxisListType

P = 128
D = 1024
S = 2048
NB = 16
NT = 4
DEPTH = 4
NCORES = 8
SEQ_PER_CORE = 2
EPS = 1e-6
SUBLN_EPS = 1e-5
POOL_W = (2, 4, 8, 16)

C_CT = 0
C_NPRE = 16
C_NPOST = 48
C_KVN = 80
C_ADAB = 88
C_KVADAB = 280
C_ASC = 312
C_SUBLN = 344
C_LAMB = 346
NCOLS = 864
NMATS = 128 + 8 * 144


def lambda_init_fn(layer_idx):
    import math
    return 0.8 - 0.6 * math.exp(-0.3 * layer_idx)


class Chan:
    def __init__(self, sem, step):
        self.sem = sem
        self.step = step
        self.count = 0


class Op:
    __slots__ = ("eng", "fn", "deps", "chan", "val", "need", "dma")

    def __init__(self, eng, fn, chan, dma):
        self.eng = eng
        self.fn = fn
        self.chan = chan
        self.dma = dma
        self.deps = []
        self.val = None
        self.need = dma


class Sched:
    ENGS = ("pe", "act", "dve", "pool", "sp")

    def __init__(self, nc):
        self.nc = nc
        self.ops = []
        self.streams = {e: [] for e in self.ENGS}
        self.lastw = {}
        self.readers = {}
        self.echan = {}
        for e in ("pe", "act", "dve", "pool"):
            self.echan[e] = Chan(nc.alloc_semaphore(name="e_" + e), 1)

    def new_chan(self, name):
        return Chan(self.nc.alloc_semaphore(name=name), 16)

    def op(self, eng, fn, R=(), W=(), chan=None, extra=()):
        dma = chan is not None
        o = Op(eng, fn, chan if dma else self.echan.get(eng), dma)
        deps = set(extra)
        for r in R:
            w = self.lastw.get(r)
            if w is not None:
                deps.add(w)
        for w_ in W:
            w = self.lastw.get(w_)
            if w is not None:
                deps.add(w)
            for rd in self.readers.get(w_, ()):
                deps.add(rd)
        for r in R:
            self.readers.setdefault(r, []).append(o)
        for w_ in W:
            self.lastw[w_] = o
            self.readers[w_] = []
        deps.discard(o)
        while any(d.fn is None for d in deps):
            nd = set()
            for d in deps:
                if d.fn is None:
                    nd.update(d.deps)
                else:
                    nd.add(d)
            deps = nd
        if eng == "pe":
            deps = [d for d in deps if not (d.eng == "pe" and not d.dma)]
        o.deps = list(deps)
        self.ops.append(o)
        self.streams[eng].append(o)
        return o

    def barrier(self):
        last = {}
        for o in self.ops:
            if o.fn is not None:
                last[o.chan] = o
        tails = list(last.values())
        for e in self.ENGS:
            w = Op(e, None, None, False)
            w.deps = [t for t in tails if not (t.eng == e and not t.dma and e == "pe")]
            self.ops.append(w)
            self.streams[e].append(w)
        self.lastw = {}
        self.readers = {}

    def emit(self):
        for o in self.ops:
            for d in o.deps:
                d.need = True
        for o in self.ops:
            if o.fn is not None and o.need:
                o.chan.count += o.chan.step
                o.val = o.chan.count
        nc = self.nc
        with nc.Block() as block:
            decos = {"pe": block.tensor, "act": block.scalar, "dve": block.vector,
                     "pool": block.gpsimd, "sp": block.sync}
            for name in self.ENGS:
                stream = self.streams[name]

                def body(e, stream=stream):
                    seen = {}
                    for o in stream:
                        waits = {}
                        for d in o.deps:
                            if waits.get(d.chan, 0) < d.val:
                                waits[d.chan] = d.val
                        for ch, v in waits.items():
                            if seen.get(ch, 0) < v:
                                e.wait_ge(ch.sem, v)
                                seen[ch] = v
                        if o.fn is not None:
                            ins = o.fn(e)
                            if o.val is not None:
                                ins.then_inc(o.chan.sem, o.chan.step)

                decos[name](body)


class Builder:
    def __init__(self, n_layers=DEPTH):
        self.n_layers = n_layers
        nc = bass.Bass("TRN2", target_bir_lowering=False)
        self.nc = nc
        self.s = Sched(nc)
        dt = nc.dram_tensor
        self.x_d = dt("x", [SEQ_PER_CORE, S, D], F32, kind="ExternalInput").ap()
        self.cols_d = dt("cols", [P, NCOLS], F32, kind="ExternalInput").ap()
        self.mats_d = dt("mats", [P, NMATS], F32, kind="ExternalInput").ap()
        self.ada_w_d = dt("ada_w", [DEPTH, D, 3 * D], F32, kind="ExternalInput").ap()
        self.a_w_in_d = dt("a_w_in", [2, D, 4096], F32, kind="ExternalInput").ap()
        self.a_w_group_d = dt("a_w_group", [2, 2048, 512], F32, kind="ExternalInput").ap()
        self.a_w_out_d = dt("a_w_out", [2, 2048, D], F32, kind="ExternalInput").ap()
        self.kv_ada_w_d = dt("kv_ada_w", [D, 2048], F32, kind="ExternalInput").ap()
        self.w_kv_d = dt("w_kv", [D, 2048], F32, kind="ExternalInput").ap()
        self.b_w_in_d = dt("b_w_in", [2, D, 2048], F32, kind="ExternalInput").ap()
        self.b_w_out_d = dt("b_w_out", [2, D, D], F32, kind="ExternalInput").ap()
        self.out_d = dt("out", [SEQ_PER_CORE, S, D], F32, kind="ExternalOutput").ap()
        self.sc = {}
        self.sc_src = {}

        def scr(name, src, rows, cols):
            self.sc[name] = dt("sc_" + name, [rows, cols], BF16).ap()
            self.sc_src[name] = (src, rows, cols)

        for l in range(2):
            scr(f"ain{l}", self.a_w_in_d[l], D, 4096)
            scr(f"ag{l}", self.a_w_group_d[l], 2048, 512)
            scr(f"aout{l}", self.a_w_out_d[l], 2048, D)
        scr("wkv", self.w_kv_d, D, 2048)
        for l in range(1, DEPTH):
            scr(f"ada{l}", self.ada_w_d[l], D, 3 * D)
        scr("kvada", self.kv_ada_w_d, D, 2048)
        for j in range(2):
            scr(f"bin{j}", self.b_w_in_d[j], D, 2048)
            scr(f"bout{j}", self.b_w_out_d[j], D, D)
        self.conv_res = {}
        self.conv_q = []
        self.mod_q = []
        self.ps_pool = 8
        self.deferred = None
        self.next_A = None
        self.next_gate = None
        self.fifo2 = []
        self.alloc()

    def alloc(self):
        nc = self.nc
        off = 0

        def take(n):
            nonlocal off
            o = off
            off += (n + 63) // 64 * 64
            return o

        o_x = take(NB * D * 4)
        o_kv = take(65536)
        o_flex = take(44032)
        o_cols = take(NCOLS * 4)
        o_identf = take(512)
        o_identb = take(256)
        o_onesb = take(256)
        o_onesf = take(512)
        o_bands = take(8 * 144 * 2)
        o_mod = take((4 * 48 + 32) * 4)
        o_small = take(1024)
        o_abf = take(2048)
        o_akv = take(2048)
        o_gf = take(4096)
        o_htm = take(4096)
        o_hT = take(8192)
        o_tmp = take(4096)
        o_junk = take(2048)
        o_mask = take(384)
        total = off
        assert total <= nc.sbuf_bytes_remaining, (total, nc.sbuf_bytes_remaining)
        self.arena = nc.alloc_sbuf_tensor("arena", [P, total // 2], BF16)

        def view(o, dtype, *shape):
            n = 1
            for d_ in shape:
                n *= d_
            esz = 4 if dtype == F32 else 2
            ap = self.arena[:, o // 2: o // 2 + n * esz // 2]
            if dtype == F32:
                ap = ap.bitcast(F32)
            if len(shape) == 2:
                ap = ap.rearrange("p (a b) -> p a b", a=shape[0])
            elif len(shape) == 3:
                ap = ap.rearrange("p (a b c) -> p a b c", a=shape[0], b=shape[1])
            return ap

        self.view = view
        self.x_sb = view(o_x, F32, NB, D)
        self.stage = view(o_x, F32, 8 * 144)
        self.kT = view(o_kv, BF16, 8, S)
        self.V = view(o_kv + 32768, BF16, NB, D)
        self.u_tm = view(o_kv, BF16, 4, 2048)
        self.szA = view(o_kv + 16384, BF16, 16, 512)
        self.gT = view(o_kv + 32768, BF16, 16, 512)
        self.pooledT = [view(o_kv + 49152 + i * 4096, BF16, 4, 512) for i in range(2)]
        self.halo = view(o_kv + 57344, BF16, 2048)
        self.sigA = [view(o_kv + 61440 + i * 2048, F32, 512) for i in range(2)]
        self.o_flex = o_flex
        self.ring_slots = {"A": [o_flex + i * 8192 for i in range(5)],
                           "B": [o_flex + i * 8192 for i in range(2)]}
        ob = o_flex + 16384
        self.yT = view(ob, BF16, 8, 512)
        self.q_h = [[view(ob + 8192 + (i * 2 + c) * 1024, BF16, 512) for c in range(2)]
                    for i in range(2)]
        self.pT = [[view(ob + 12288 + (i * 2 + c) * 1024, BF16, 512) for c in range(2)]
                   for i in range(2)]
        self.fsc = [view(ob + 16384 + i * 2048, F32, 512) for i in range(4)]
        self.f01 = view(ob + 16384, F32, 1024)
        self.sq = view(ob + 24576, BF16, 512)
        self.szb = [view(ob + 25600 + i * 1024, BF16, 512) for i in range(2)]
        assert ob + 27648 <= o_flex + 44032
        self.cols = view(o_cols, F32, NCOLS)
        self.ident_f = view(o_identf, F32, 128)
        self.ident_b = view(o_identb, BF16, 128)
        self.ones_b = view(o_onesb, BF16, 128)
        self.ones_f = view(o_onesf, F32, 128)
        self.bands = view(o_bands, BF16, 8, 144)
        self.modcol = [view(o_mod + l * 192, F32, 48) for l in range(4)]
        self.modkv = view(o_mod + 768, F32, 32)
        small = view(o_small, F32, 256)
        self.small = small
        self.A_bf = view(o_abf, BF16, D)
        self.Akv_bf = view(o_akv, BF16, D)
        self.G_f = view(o_gf, F32, D)
        self.h_tm = [view(o_htm + i * 2048, BF16, D) for i in range(2)]
        self.hT = view(o_hT, BF16, 8, 512)
        self.tmp = view(o_tmp, F32, D)
        self.junk = view(o_junk, BF16, D)
        self.maskrow = view(o_mask, BF16, 192)
        self.ps_all = nc.alloc_psum_tensor("ps_all", [P, 4096], F32)
        self.ps = [self.ps_all[:, i * 512:(i + 1) * 512] for i in range(8)]
        self.rot = 0
        self.ring_n = 0
        self.ring_mode = "A"
        self.ring_ch = [self.s.new_chan(f"ring{i}") for i in range(5)]
        self.ring_ch_sw = [self.s.new_chan(f"ringsw{i}") for i in range(5)]
        self.x_ch = [self.s.new_chan(f"xch{i}") for i in range(NB)]
        self.x_ch_hw = [self.s.new_chan(f"xchhw{i}") for i in range(NB)]
        self.const_ch = self.s.new_chan("constch")
        self.conv_ch = {}

    def bank(self, pool=8):
        b = self.rot % pool
        self.rot += 1
        return b

    def ring_get(self, name, r0, nk, dram_view, shape, split=None, dtype=BF16, eng="sp"):
        slots = self.ring_slots[self.ring_mode]
        slot = self.ring_n % len(slots)
        self.ring_n += 1
        v = self.view(slots[slot], dtype, *shape)
        res = ("ring", slot)
        if split is None:
            self.s.op(eng, lambda e, o=v, i=dram_view: e.dma_start(out=o, in_=i),
                      R=self.conv_res.get(name, []), W=[res],
                      chan=(self.ring_ch if eng == "sp" else self.ring_ch_sw)[slot])
        else:
            subs = []
            for i_, (ov, iv) in enumerate(split(v)):
                sr = ("ringpart", slot, i_)
                subs.append(self.s.op("sp", lambda e, o=ov, i=iv: e.dma_start(out=o, in_=i),
                                      R=self.conv_res[name], W=[sr], chan=self.ring_ch[slot],
                                      extra=[d for d in [self.s.lastw.get(res)] if d is not None]
                                      + list(self.s.readers.get(res, ()))))
            j = Op("none", None, None, False)
            j.deps = subs
            self.s.lastw[res] = j
            self.s.readers[res] = []
        return v, res

    def slab(self, name, r0, nk, c0, ncol):
        src = self.sc[name][r0:r0 + nk * P, c0:c0 + ncol].rearrange("(k p) n -> p k n", p=P)
        return self.ring_get((name, c0) if (name, c0) in self.conv_res else name, r0, nk, src, (nk, ncol))

    def mm(self, out, lhsT, rhs, start, stop, R, W):
        return self.s.op(
            "pe",
            lambda e, o=out, l=lhsT, r=rhs, a=start, b=stop: e.matmul(
                o, lhsT=l, rhs=r, start=a, stop=b, skip_group_check=True),
            R=R, W=W)

    def act(self, out, in_, func, R, W, bias=None, scale=None, accum=None):
        kw = {}
        if bias is not None:
            kw["bias"] = bias
        if scale is not None:
            kw["scale"] = scale
        if accum is not None:
            kw["accum_out"] = accum
        return self.s.op("act", lambda e, o=out, i=in_, f=func, kw=kw: e.activation(
            out=o, in_=i, func=f, **kw), R=R, W=W)

    def dve(self, fn, R, W):
        return self.s.op("dve", fn, R=R, W=W)

    def copy(self, eng, out, in_, R, W):
        if eng == "act":
            return self.act(out, in_, AF.Copy, R, W)
        return self.s.op(eng, lambda e, o=out, i=in_: e.tensor_copy(out=o, in_=i), R=R, W=W)

    def convert_weights(self, names, defer=False):
        for name in names:
            src, rows, cols = self.sc_src[name]
            res = []
            if name.startswith("ain"):
                for c0 in [2048, 2560, 3072, 3584, 0, 512, 1024, 1536]:
                    ch = self.s.new_chan(f"cv_{name}_{c0}")
                    rs = ("conv", name, c0)
                    self.conv_q.append(lambda o=self.sc[name][:, c0:c0 + 512], s_=src[:, c0:c0 + 512], rs=rs, ch=ch:
                                       self.s.op("pool", lambda e: e.dma_start(out=o, in_=s_), R=[], W=[rs], chan=ch))
                    self.conv_res[(name, c0)] = [rs]
                    res.append(rs)
            else:
                ch = self.s.new_chan("cv_" + name)
                step = max(128, (1 << 19) // cols)
                for i, r0 in enumerate(range(0, rows, step)):
                    r1 = min(rows, r0 + step)
                    rs = ("conv", name, i)
                    self.conv_q.append(lambda o=self.sc[name][r0:r1, :], s_=src[r0:r1, :], rs=rs, ch=ch:
                                       self.s.op("pool", lambda e: e.dma_start(out=o, in_=s_), R=[], W=[rs], chan=ch))
                    res.append(rs)
            self.conv_res[name] = res
        if not defer:
            self.issue_conv(1.0)

    def issue_conv(self, frac):
        n = int(round(len(self.conv_q) * frac)) if frac < 1.0 else len(self.conv_q)
        for _ in range(n):
            self.conv_q.pop(0)()

    def prologue(self):
        s = self.s
        sm = self.small
        s.op("sp", lambda e: e.dma_start(out=self.cols, in_=self.cols_d), W=["cols"],
             chan=self.s.new_chan("constch1"))
        s.op("sp", lambda e: e.dma_start(out=self.ident_f, in_=self.mats_d[:, 0:128]),
             W=["ident_f"], chan=self.s.new_chan("constch2"))
        self.copy("act", self.ident_b, self.ident_f, ["ident_f"], ["ident_b"])
        s.op("pool", lambda e: e.memset(self.ones_b, 1.0), W=["ones_b"])
        s.op("pool", lambda e: e.memset(self.ones_f, 1.0), W=["ones_f"])
        c = self.cols[:, C_CT:C_CT + 16]
        t0 = sm[:, 0:16]
        self.act(t0, c, AF.Exp, ["cols"], ["ctmp"], scale=-1.0)
        self.dve(lambda e: e.tensor_scalar_add(out=t0, in0=t0, scalar1=1.0), ["ctmp"], ["ctmp"])
        self.dve(lambda e: e.reciprocal(out=t0, in_=t0), ["ctmp"], ["ctmp"])
        self.condT = self.view_small_bf16()
        self.dve(lambda e: e.tensor_tensor(out=self.condT, in0=t0, in1=c, op=ALU.mult),
                 ["ctmp", "cols"], ["condT"])
        for j in range(2):
            lb = self.cols[:, C_LAMB + j * 256: C_LAMB + (j + 1) * 256]
            pr = sm[:, 128:192]
            for h_ in range(2):
                self.dve(lambda e, a=lb[:, h_ * 128:h_ * 128 + 64], b=lb[:, h_ * 128 + 64:h_ * 128 + 128]:
                         e.tensor_tensor(out=pr, in0=a, in1=b, op=ALU.mult), ["cols"], ["lprod"])
                self.dve(lambda e, o=sm[:, 114 + h_:115 + h_]: e.reduce_sum(out=o, in_=pr, axis=AX.X),
                         ["lprod"], [("lsum", h_)])
                self.act(sm[:, 116 + h_:117 + h_], sm[:, 114 + h_:115 + h_], AF.Exp,
                         [("lsum", h_)], [("lexp", h_)])
            lam = sm[:, 108 + j:109 + j]
            self.dve(lambda e, o=lam: e.tensor_tensor(out=o, in0=sm[:, 116:117], in1=sm[:, 117:118],
                                                      op=ALU.subtract),
                     [("lexp", 0), ("lexp", 1)], [("lam", j)])
            li = lambda_init_fn(2 + j)
            self.dve(lambda e, o=lam, li=li: e.tensor_scalar_add(out=o, in0=o, scalar1=float(li)),
                     [("lam", j)], [("lam", j)])
            self.dve(lambda e, o=sm[:, 110 + j:111 + j], i=lam: e.tensor_scalar_mul(out=o, in0=i, scalar1=-1.0),
                     [("lam", j)], [("neglam", j)])
            self.dve(lambda e, o=sm[:, 112 + j:113 + j], i=self.cols[:, C_SUBLN + j:C_SUBLN + j + 1], li=li:
                     e.tensor_scalar_mul(out=o, in0=i, scalar1=float(1.0 - li)), ["cols"], [("gsub", j)])

    def modcols(self, l, run_now=False):
        name = f"ada{l}" if l < DEPTH else "kvada"
        nch = 24 if l < DEPTH else 16
        dst = self.modcol[l] if l < DEPTH else self.modkv
        bcol = (self.cols[:, C_ADAB + l * 48:C_ADAB + (l + 1) * 48] if l < DEPTH
                else self.cols[:, C_KVADAB:C_KVADAB + 32])
        srcw = self.ada_w_d[l] if l < DEPTH else self.kv_ada_w_d

        def step(sl):
            b = self.bank(self.ps_pool)
            bres = ("ps", b)
            if l == 0:
                src = srcw[:, sl * 512:(sl + 1) * 512].rearrange("(k p) n -> p k n", p=P)
                w, wres = self.ring_get(None, 0, 8, src, (8, 512), eng="pool")
            else:
                w, wres = self.slab(name, 0, 8, sl * 512, 512)
            for cc in range(4):
                for k in range(8):
                    self.mm(self.ps[b][:, 2 * cc:2 * cc + 2], w[:, k, cc * 128:(cc + 1) * 128],
                            self.condT[:, 2 * k:2 * k + 2], k == 0, k == 7,
                            [wres, "condT"], [bres])
            self.dve(lambda e, o=dst[:, 8 * sl:8 * sl + 8], i=self.ps[b][:, 0:8], bc=bcol[:, 8 * sl:8 * sl + 8]:
                     e.tensor_tensor(out=o, in0=i, in1=bc, op=ALU.add), [bres, "cols"], [("mod", l)])

        for sl in range(nch // 4):
            self.mod_q.append(lambda sl=sl: step(sl))
        if run_now:
            self.mod_flush()

    def mod_step(self):
        if self.mod_q:
            self.mod_q.pop(0)()

    def mod_flush(self):
        while self.mod_q:
            self.mod_q.pop(0)()

    def view_small_bf16(self):
        v = self.small[:, 120:128].bitcast(BF16)
        return v

    def bcast_tile(self, col8, dst, dst_res, evac_eng):
        diag = self.tmp
        for k in range(8):
            self.dve(lambda e, o=diag[:, k * 128:(k + 1) * 128], sc=col8[:, k:k + 1]:
                     e.tensor_scalar_mul(out=o, in0=self.ident_f, scalar1=sc),
                     ["ident_f", "col8"], [("tmpk", k)])
        for half in range(2):
            b = self.bank(self.ps_pool)
            self.mm(self.ps[b][:, :], self.ones_f, diag[:, half * 512:(half + 1) * 512], True, True,
                    ["ones_f"] + [("tmpk", k) for k in range(half * 4, half * 4 + 4)], [("ps", b)])
            self.copy(evac_eng, dst[:, half * 512:(half + 1) * 512], self.ps[b][:, :], [("ps", b)],
                      [dst_res])

    def modA(self, l, s_):
        sm = self.small
        mc = self.modcol[l]
        mc3 = mc.rearrange("p (j s) -> p j s", s=2)
        shift = mc3[:, 0:8, s_]
        scale = mc3[:, 8:16, s_]
        gate = mc3[:, 16:24, s_]
        acol = sm[:, 84:92]
        gcol = sm[:, 92:100]
        npre = self.cols[:, C_NPRE + l * 8:C_NPRE + (l + 1) * 8]
        npost = self.cols[:, C_NPOST + l * 8:C_NPOST + (l + 1) * 8]
        self.shift_col = shift
        if l == 2:
            kc3 = self.modkv.rearrange("p (j s) -> p j s", s=2)
            kcol = sm[:, 100:108]
            kvn = self.cols[:, C_KVN:C_KVN + 8]
            self.dve(lambda e: e.scalar_tensor_tensor(out=kcol, in0=kc3[:, 8:16, s_], scalar=1.0, in1=kvn,
                                                      op0=ALU.add, op1=ALU.mult),
                     [("mod", 4), "cols"] + [("tmpk", k) for k in range(8)], ["col8"])
            self.bcast_tile(kcol, self.Akv_bf, "Akv_bf", "act")
            self.kvshift_col = kc3[:, 0:8, s_]
        self.dve(lambda e: e.scalar_tensor_tensor(out=acol, in0=scale, scalar=1.0, in1=npre,
                                                  op0=ALU.add, op1=ALU.mult),
                 [("mod", l), "cols"] + [("tmpk", k) for k in range(8)], ["col8"])
        self.bcast_tile(acol, self.A_bf, "A_bf", "act")

        def gate_tile():
            self.dve(lambda e: e.tensor_tensor(out=gcol, in0=gate, in1=npost, op=ALU.mult),
                     [("mod", l), "cols"] + [("tmpk", k) for k in range(8)], ["col8"])
            self.bcast_tile(gcol, self.G_f, "G_f", "act")

        return gate_tile

    def call_next_A(self):
        if self.next_A is not None:
            f = self.next_A
            self.next_A = None
            f()

    def run_deferred(self):
        if self.deferred is not None:
            d = self.deferred
            self.deferred = None
            d()

    def flush_done(self, keep):
        while len(self.done_fifo) > keep:
            gb = self.done_fifo.pop(0)
            if self.block_done is not None:
                self.block_done(gb)

    def sumsq_block(self, gb):
        self.act(self.junk, self.x_sb[:, gb, :], AF.Square, [("x", gb)], ["junk", ("ss", gb)],
                 accum=self.small[:, 16 + gb:17 + gb])

    def prenorm_stats(self, have_ss=False):
        sm = self.small
        if not have_ss:
            for gb in range(NB):
                self.sumsq_block(gb)
        self.act(sm[:, 32:48], sm[:, 16:32], AF.Ln, [("ss", gb) for gb in range(NB)], ["ln16"],
                 scale=1.0 / D, bias=self.eps_col(EPS))
        self.act(sm[:, 48:64], sm[:, 32:48], AF.Exp, ["ln16"], ["rstd16"], scale=-0.5)

    def eps_col(self, eps):
        return self.small[:, 200:201] if eps == EPS else self.small[:, 201:202]

    def make_h(self, t, gain, gain_res, shiftcol, shift_res):
        sm = self.small
        banks = [self.bank() for _ in range(4)]
        for tb in range(4):
            gb = t * 4 + tb
            hb = self.h_tm[tb % 2]
            hres = ("h_tm", tb % 2)
            self.dve(lambda e, o=hb, x_=self.x_sb[:, gb, :], r=sm[:, 48 + gb:49 + gb]:
                     e.scalar_tensor_tensor(out=o, in0=x_, scalar=r, in1=gain, op0=ALU.mult, op1=ALU.mult),
                     [("x", gb), "rstd16", gain_res], [hres])
            for k in range(8):
                b = banks[k // 2]
                dst = self.ps[b].bitcast(BF16)[:, (k % 2) * 512 + tb * 128:(k % 2) * 512 + (tb + 1) * 128]
                self.s.op("pe", lambda e, o=dst, i=hb[:, k * 128:(k + 1) * 128]:
                          e.transpose(o, i, self.ident_b), R=[hres, "ident_b"], W=[("ps", b)])
        for k in range(8):
            b = banks[k // 2]
            src = self.ps[b].bitcast(BF16)[:, (k % 2) * 512:(k % 2) * 512 + 512]
            self.act(self.hT[:, k, :], src, AF.Identity, [("ps", b), shift_res], [("hT", k)],
                     bias=shiftcol[:, k:k + 1])

    def postnorm_update(self, t, tb, b0):
        sm = self.small
        gb = t * 4 + tb
        b1 = b0 + 1
        for hf, b in enumerate((b0, b1)):
            self.act(self.junk[:, hf * 512:(hf + 1) * 512], self.ps[b][:, :], AF.Square, [("ps", b)],
                     ["junk", ("ssy", hf)], accum=sm[:, 64 + hf:65 + hf])
        self.dve(lambda e: e.tensor_tensor(out=sm[:, 72:73], in0=sm[:, 64:65], in1=sm[:, 65:66], op=ALU.add),
                 [("ssy", 0), ("ssy", 1)], ["sst"])
        self.act(sm[:, 76:77], sm[:, 72:73], AF.Ln, ["sst"], ["lny"], scale=1.0 / D, bias=self.eps_col(EPS))
        self.act(sm[:, 80:81], sm[:, 76:77], AF.Exp, ["lny"], ["rstdy"], scale=-0.5)
        for hf, b in enumerate((b0, b1)):
            self.dve(lambda e, o=self.tmp[:, hf * 512:(hf + 1) * 512], i=self.ps[b][:, :],
                     g=self.G_f[:, hf * 512:(hf + 1) * 512]:
                     e.scalar_tensor_tensor(out=o, in0=i, scalar=sm[:, 80:81], in1=g, op0=ALU.mult, op1=ALU.mult),
                     [("ps", b), "rstdy", "G_f"], [("tmpk", 4 * hf + q) for q in range(4)])
        self.s.op("pool", lambda e, x_=self.x_sb[:, gb, :]: e.tensor_tensor(out=x_, in0=x_, in1=self.tmp, op=ALU.add),
                  R=[("tmpk", k) for k in range(8)] + [("x", gb)], W=[("x", gb)])
        if self.block_store is not None:
            self.block_store(gb)
        self.done_fifo.append(gb)
        self.flush_done(self.done_lag)

    def sigmoid_from_psum(self, zps, zres, scratch, sres):
        self.act(scratch, zps, AF.Exp, [zres], [sres], scale=-1.0)
        self.dve(lambda e: e.tensor_scalar_add(out=scratch, in0=scratch, scalar1=1.0), [sres], [sres])
        self.dve(lambda e: e.reciprocal(out=scratch, in_=scratch), [sres], [sres])

    def layerA(self, l, s_):
        self.ring_mode = "A"
        self.ps_pool = 8
        self.prenorm_stats(self.have_ss)
        self.make_h(0, self.A_bf, "A_bf", self.shift_col, ("mod", l))
        for t in range(NT):
            for zs in range(4):
                w, wres = self.slab(f"ain{l}", 0, 8, 2048 + zs * 512, 512)
                for cc in range(4):
                    j = zs * 4 + cc
                    b = self.bank()
                    for k in range(8):
                        self.mm(self.ps[b][:, :], w[:, k, cc * 128:(cc + 1) * 128], self.hT[:, k, :],
                                k == 0, k == 7, [wres, ("hT", k)], [("ps", b)])
                    self.act(self.szA[:, j, :], self.ps[b][:, :], AF.Silu, [("ps", b)], [("szA", j)])
                self.run_deferred()
                self.mod_step()
            if t > 0:
                self.s.op("pool", lambda e: e.tensor_copy(out=self.halo, in_=self.u_tm[:, 3, :]),
                          R=[("u_tm", 3, g) for g in range(4)], W=["halo"])
            for g in range(4):
                w, wres = self.slab(f"ain{l}", 0, 8, g * 512, 512)
                for tb in range(4):
                    b = self.bank()
                    for k in range(8):
                        self.mm(self.ps[b][:, :], self.hT[:, k, tb * 128:(tb + 1) * 128], w[:, k, :],
                                k == 0, k == 7, [wres, ("hT", k)], [("ps", b)])
                    self.copy("act" if tb % 2 == 0 else "dve", self.u_tm[:, tb, g * 512:(g + 1) * 512],
                              self.ps[b][:, :], [("ps", b)], [("u_tm", tb, g)])
            if t + 1 < NT:
                self.make_h(t + 1, self.A_bf, "A_bf", self.shift_col, ("mod", l))
            else:
                self.call_next_A()
            def pool_group(g):
                pg = self.pooledT[g % 2]
                pres = ("pooledT", g % 2)
                band = self.bands[:, g, :]
                bandf = self.bands[:, 4 + g, :]
                for c in range(4):
                    cc = g * 4 + c
                    b = self.bank()
                    first = True
                    if t > 0:
                        self.mm(self.ps[b][:, 0:16], self.halo[:, cc * 128:(cc + 1) * 128], band[:, 128:144],
                                True, False, ["halo", "bands"], [("ps", b)])
                        first = False
                    for tb in range(4):
                        ncol = 144 if tb < 3 else 128
                        bm = bandf if (t == 0 and tb == 0) else band
                        self.mm(self.ps[b][:, tb * 128:tb * 128 + ncol], self.u_tm[:, tb, cc * 128:(cc + 1) * 128],
                                bm[:, 0:ncol], first, tb == 3, [("u_tm", tb, g), "bands"], [("ps", b)])
                        first = False
                    self.copy("dve", pg[:, c, :], self.ps[b][:, :], [("ps", b)], [pres])

            def wg_group(g):
                pg = self.pooledT[g % 2]
                pres = ("pooledT", g % 2)
                w, wres = self.slab(f"ag{l}", g * 512, 4, 0, 512)
                for c in range(4):
                    cc = g * 4 + c
                    b = self.bank()
                    for k in range(4):
                        self.mm(self.ps[b][:, :], w[:, k, c * 128:(c + 1) * 128], pg[:, k, :],
                                k == 0, k == 3, [wres, pres], [("ps", b)])
                    self.dve(lambda e, o=self.gT[:, cc, :], i=self.ps[b][:, :],
                             sc=self.cols[:, C_ASC + l * 16 + cc:C_ASC + l * 16 + cc + 1], z=self.szA[:, cc, :]:
                             e.scalar_tensor_tensor(out=o, in0=i, scalar=sc, in1=z, op0=ALU.mult, op1=ALU.mult),
                             [("ps", b), "cols", ("szA", cc)], [("gT", cc)])

            pool_group(0)
            pool_group(1)
            wg_group(0)
            pool_group(2)
            wg_group(1)
            pool_group(3)
            wg_group(2)
            wg_group(3)
            for half in range(2):
                ws = [self.slab(f"aout{l}", kh * 1024, 8, half * 512, 512) for kh in range(2)]
                for tb in range(4):
                    b = 2 * tb + half
                    for kc in range(16):
                        w, wres = ws[kc // 8]
                        self.mm(self.ps[b][:, :], self.gT[:, kc, tb * 128:(tb + 1) * 128], w[:, kc % 8, :],
                                kc == 0, kc == 15, [wres, ("gT", kc)], [("ps", b)])
                    if half == 1:
                        self.postnorm_update(t, tb, 2 * tb)
            self.rot = 0
            self.issue_conv({0: 0.25, 1: 0.45, 2: 1.0, 3: 1.0}[t])
            if t == 1 and self.after_tile1 is not None:
                self.after_tile1()

    def layerB(self, l, s_):
        j = l - 2
        sm = self.small
        self.ring_mode = "B"
        self.ps_pool = 4
        self.ring_n = 0
        for i_ in range(2):
            self.s.op("pool", lambda e, o=self.q_h[i_][0][64:128, :]: e.memset(o, 0.0), W=["qzero"])
            self.s.op("pool", lambda e, o=self.q_h[i_][1][0:64, :]: e.memset(o, 0.0), W=["qzero"])
        self.prenorm_stats(self.have_ss)
        neglam = sm[:, 110 + j:111 + j]
        gsub = sm[:, 112 + j:113 + j]
        for t in range(NT):
            if l == 2:
                if t == 0:
                    self.make_h(t, self.Akv_bf, "Akv_bf", self.kvshift_col, ("mod", 4))
                for hs in range(2):
                    w, wres = self.slab("wkv", 0, 8, hs * 512, 512)
                    for hh in range(4):
                        hd = hs * 4 + hh
                        b = self.bank()
                        for k in range(8):
                            self.mm(self.ps[b][:, :], w[:, k, hh * 128:(hh + 1) * 128], self.hT[:, k, :],
                                    k == 0, k == 7, [wres, ("hT", k)], [("ps", b)])
                        self.copy("dve", self.kT[:, hd, t * 512:(t + 1) * 512], self.ps[b][:, :],
                                  [("ps", b)], [("kT", hd, t)])
                for hs in range(2):
                    w, wres = self.slab("wkv", 0, 8, 1024 + hs * 512, 512)
                    for tb in range(4):
                        b = self.bank()
                        for k in range(8):
                            self.mm(self.ps[b][:, :], self.hT[:, k, tb * 128:(tb + 1) * 128], w[:, k, :],
                                    k == 0, k == 7, [wres, ("hT", k)], [("ps", b)])
                        self.copy("dve", self.V[:, t * 4 + tb, hs * 512:(hs + 1) * 512], self.ps[b][:, :],
                                  [("ps", b)], [("V", t * 4 + tb)])
            if l == 2 or t == 0:
                self.make_h(t, self.A_bf, "A_bf", self.shift_col, ("mod", l))
            if t == NT - 1:
                self.call_next_A()
            pending = None
            nkb = 4 * t + 4

            def head_pre(hd):
                srcq = self.sc[f"bin{j}"][:, hd * 128:(hd + 1) * 128].rearrange("(k p) e -> p k e", p=P)
                srcz = self.sc[f"bin{j}"][:, 1024 + hd * 128:1024 + (hd + 1) * 128].rearrange("(k p) e -> p k e", p=P)
                w, wres = self.ring_get(f"bin{j}", 0, 8, None, (8, 2, 128),
                                        split=lambda v, a=srcq, b_=srcz: [(v[:, :, 0, :], a), (v[:, :, 1, :], b_)])
                qb = self.q_h[hd % 2]
                qres = ("q_h", hd % 2)
                b = self.bank(4)
                for k in range(8):
                    self.mm(self.ps[b][:, :], w[:, k, 0, :], self.hT[:, k, :], k == 0, k == 7,
                            [wres, ("hT", k)], [("ps", b)])
                self.copy("dve", qb[0][0:64, :], self.ps[b][0:64, :], [("ps", b), "qzero"], [qres])
                self.copy("dve", qb[1][64:128, :], self.ps[b][64:128, :], [("ps", b), "qzero"], [qres])
                zb = self.bank(4)
                for k in range(8):
                    self.mm(self.ps[zb][:, :], w[:, k, 1, :], self.hT[:, k, :], k == 0, k == 7,
                            [wres, ("hT", k)], [("ps", zb)])
                self.copy("dve", self.szb[hd % 2], self.ps[zb][:, :], [("ps", zb)], [("szb", hd % 2)])
                self.run_deferred()
                self.mod_step()

            head_pre(0)
            for hd in range(8):
                qb = self.q_h[hd % 2]
                qres = ("q_h", hd % 2)
                szb = self.szb[hd % 2]
                szr = ("szb", hd % 2)
                sgt = self.tmp[:, (hd % 2) * 512:(hd % 2) * 512 + 512]
                sgr = [("tmpk", 4 * (hd % 2) + q_) for q_ in range(4)]

                def emit_S(kb):
                    c0 = 0 if kb < 4 * t else (kb - 4 * t) * 128
                    diag = kb >= 4 * t
                    pbuf = self.pT[kb % 2]
                    for c in range(2):
                        b = self.bank(4)
                        self.mm(self.ps[b][:, c0:512], self.kT[:, hd, kb * 128:(kb + 1) * 128],
                                qb[c][:, c0:512], True, not diag,
                                [("kT", hd, kb // 4), qres], [("ps", b)])
                        if diag:
                            self.mm(self.ps[b][:, c0:c0 + 64], self.maskrow[0:1, 0:128], self.maskrow[0:1, 128:192],
                                    False, True, ["maskrow"], [("ps", b)])
                        self.act(pbuf[c][:, c0:512], self.ps[b][:, c0:512], AF.Exp, [("ps", b)],
                                 [("pT", kb % 2, c)], scale=0.125)

                emit_S(0)
                for kb in range(nkb):
                    if kb + 1 < nkb:
                        emit_S(kb + 1)
                    c0 = 0 if kb < 4 * t else (kb - 4 * t) * 128
                    pbuf = self.pT[kb % 2]
                    for c in range(2):
                        pres = ("pT", kb % 2, c)
                        vh = self.V[:, kb, hd * 128:(hd + 1) * 128]
                        for which, (ob, lhs_full) in enumerate(((4 + c, vh), (6 + c, self.ones_b))):
                            R_ = [pres, ("V", kb) if which == 0 else "ones_b"]
                            self.mm(self.ps[ob][:, c0:512], lhs_full, pbuf[c][:, c0:512], kb == 0, kb == nkb - 1,
                                    R_, [("ps", ob)])
                    if kb == 0:
                        self.act(sgt, szb, AF.Exp, [szr], sgr, scale=-1.0)
                    elif kb == 1:
                        self.act(sgt, sgt, AF.Ln, sgr, sgr, bias=self.small[:, 202:203])
                    elif kb == 2:
                        if pending is not None:
                            pending()
                            pending = None
                    elif kb == 3:
                        self.act(sgt, sgt, AF.Exp, sgr, sgr, scale=-1.0)
                        self.dve(lambda e, o=szb, g_=sgt: e.tensor_tensor(out=o, in0=o, in1=g_, op=ALU.mult),
                                 [szr] + sgr, [szr])
                if pending is not None:
                    pending()
                    pending = None
                stage1b = self.head_epilogue(hd, neglam, gsub)
                if hd < 7:
                    head_pre(hd + 1)
                if hd == 6 and t + 1 < NT:
                    if l == 2:
                        self.make_h(t + 1, self.Akv_bf, "Akv_bf", self.kvshift_col, ("mod", 4))
                    else:
                        self.make_h(t + 1, self.A_bf, "A_bf", self.shift_col, ("mod", l))
                pending = stage1b()
            wsl = [self.slab(f"bout{j}", 0, 8, half * 512, 512) for half in range(2)]
            pair = {0: 0, 1: 2, 2: 6, 3: 4}
            for tb in range(4):
                if tb == 2:
                    self.rot = 4
                    pending(8)
                    pending = None
                for half in range(2):
                    w, wres = wsl[half]
                    b = pair[tb] + half
                    for e_ in range(7):
                        self.mm(self.ps[b][:, :], self.yT[:, e_, tb * 128:(tb + 1) * 128], w[:, e_, :],
                                e_ == 0, False, [wres, ("yT", e_)], [("ps", b)])
            for tb in range(4):
                for half in range(2):
                    w, wres = wsl[half]
                    b = pair[tb] + half
                    self.mm(self.ps[b][:, :], self.yT[:, 7, tb * 128:(tb + 1) * 128], w[:, 7, :],
                            False, True, [wres, ("yT", 7)], [("ps", b)])
                self.postnorm_update(t, tb, pair[tb])
            self.issue_conv({0: 0.25, 1: 0.45, 2: 1.0, 3: 1.0}[t])
            if t == 1 and self.after_tile1 is not None:
                self.after_tile1()
            self.rot = 0

    def head_epilogue(self, hd, neglam, gsub):
        f0, f1, f2, f3 = self.fsc
        f01 = self.f01
        o0, o1 = self.ps[4], self.ps[5]
        l01 = self.ps_all[:, 3072:4096]
        self.act(f01, l01, AF.Ln, [("ps", 6), ("ps", 7)], [("f", 0), ("f", 1)])
        self.copy("dve", f2, o0[:, :], [("ps", 4)], [("f", 2)])
        self.copy("dve", f3, o1[:, :], [("ps", 5)], [("f", 3)])
        self.act(f01, f01, AF.Exp, [("f", 0), ("f", 1)], [("f", 0), ("f", 1)], scale=-1.0)

        def stage1b():
            self.dve(lambda e: e.tensor_tensor(out=f2, in0=f2, in1=f0, op=ALU.mult),
                     [("f", 2), ("f", 0)], [("f", 2)])
            self.dve(lambda e: e.scalar_tensor_tensor(out=f3, in0=f3, scalar=neglam, in1=f1,
                                                      op0=ALU.mult, op1=ALU.mult),
                     [("f", 3), ("f", 1), ("neglam", 0), ("neglam", 1)], [("f", 3)])
            self.dve(lambda e: e.tensor_tensor(out=f2, in0=f2, in1=f3, op=ALU.add),
                     [("f", 2), ("f", 3)], [("f", 2)])
            self.dve(lambda e: e.tensor_tensor(out=self.sq, in0=f2, in1=f2, op=ALU.mult), [("f", 2)], ["sq"])

            def stage2(pool=4):
                mb = self.bank(pool)
                self.mm(self.ps[mb][:, :], self.ones_b, self.sq, True, True, ["ones_b", "sq"], [("ps", mb)])
                self.act(f0, self.ps[mb][:, :], AF.Ln, [("ps", mb)], [("f", 0)], scale=1.0 / 128,
                         bias=self.eps_col(SUBLN_EPS))
                self.act(f0, f0, AF.Exp, [("f", 0)], [("f", 0)], scale=-0.5)
                self.dve(lambda e: e.scalar_tensor_tensor(out=f2, in0=f2, scalar=gsub, in1=f0,
                                                          op0=ALU.mult, op1=ALU.mult),
                         [("f", 2), ("f", 0), ("gsub", 0), ("gsub", 1)], [("f", 2)])
                self.dve(lambda e: e.tensor_tensor(out=self.yT[:, hd, :], in0=f2, in1=self.szb[hd % 2], op=ALU.mult),
                         [("f", 2), ("szb", hd % 2)], [("yT", hd)])

            return stage2

        return stage1b

    def build(self):
        s = self.s
        s.op("sp", lambda e: e.dma_start(out=self.stage, in_=self.mats_d[:, 128:NMATS]),
             W=[("x", 0), ("x", 1)], chan=self.const_ch)
        self.copy("dve", self.bands.rearrange("p a b -> p (a b)"), self.stage,
                  [("x", 0), ("x", 1)], ["bands"])
        self.convert_weights(["ain0", "ag0", "aout0", "ada1"])
        s.op("pool", lambda e: e.memset(self.small[:, 200:201], EPS), W=["epsc"])
        s.op("pool", lambda e: e.memset(self.small[:, 201:202], SUBLN_EPS), W=["epsc2"])
        s.op("pool", lambda e: e.memset(self.small[:, 202:203], 1.0), W=["onec"])
        s.op("pool", lambda e: e.memset(self.maskrow[0:1, 0:64], 0.0), W=["maskrow"])
        s.op("pool", lambda e: e.memset(self.maskrow[0:1, 64:128], 1.0), W=["maskrow"])
        s.op("pool", lambda e: e.memset(self.maskrow[0:1, 128:192], -30000.0), W=["maskrow"])
        stores = []
        self.have_ss = False
        self.block_done = None
        for gb in range(NB):
            s.op("sp", lambda e, o=self.x_sb[:, gb, :], i=self.x_d[0, gb * P:(gb + 1) * P, :]:
                 e.dma_start(out=o, in_=i), R=[], W=[("x", gb)], chan=self.x_ch_hw[gb])
        self.prologue()
        self.modcols(0, run_now=True)
        self.next_gate = self.modA(0, 0)
        stages = [(s_, l) for s_ in range(SEQ_PER_CORE) for l in range(self.n_layers)]
        for si, (s_, l) in enumerate(stages):
            self.deferred = self.next_gate
            self.next_gate = None
            if si + 1 < len(stages):
                def nxtA(ns=stages[si + 1]):
                    self.mod_flush()
                    self.next_gate = self.modA(ns[1], ns[0])
                self.next_A = nxtA
            self.after_tile0 = None
            self.after_tile1 = None
            if s_ == 0:
                nxt = {0: ["ada2", "kvada", "ain1", "ag1", "aout1"], 1: ["ada3", "wkv", "bin0", "bout0"],
                       2: ["bin1", "bout1"]}.get(l)
                if nxt:
                    self.convert_weights(nxt, defer=True)
                if l + 1 < self.n_layers:
                    self.modcols(l + 1)
                    if l + 1 == 2:
                        self.modcols(DEPTH)
            last = l + 1 == self.n_layers
            self.done_fifo = []
            self.fifo2 = []
            self.block_store = None
            if not last:
                self.block_done = self.sumsq_block
                self.done_lag = 1
            else:
                def store(gb, s_=s_):
                    stores.append(s.op(
                        "pool", lambda e, i=self.x_sb[:, gb, :], o=self.out_d[s_, gb * P:(gb + 1) * P, :]:
                        e.dma_start(out=o, in_=i), R=[("x", gb)], W=[], chan=self.x_ch[gb]))
                self.block_store = store
                if s_ + 1 < SEQ_PER_CORE:
                    def load_next(gb, s_=s_):
                        s.op("pool", lambda e, o=self.x_sb[:, gb, :], i=self.x_d[s_ + 1, gb * P:(gb + 1) * P, :]:
                             e.dma_start(out=o, in_=i), R=[], W=[("x", gb)], chan=self.x_ch[gb])
                        self.fifo2.append(gb)
                        while len(self.fifo2) > 4:
                            self.sumsq_block(self.fifo2.pop(0))
                    self.block_done = load_next
                else:
                    self.block_done = None
                self.done_lag = 4
            if l < 2:
                self.layerA(l, s_)
            else:
                if l == 2:
                    s.barrier()
                self.layerB(l, s_)
            self.flush_done(0)
            while self.fifo2:
                self.sumsq_block(self.fifo2.pop(0))
            self.run_deferred()
            self.call_next_A()
            self.mod_flush()
            self.have_ss = True
            if self.n_layers > 2 and l + 1 == self.n_layers and s_ + 1 < SEQ_PER_CORE:
                s.barrier()
        fin = Op("pool", None, None, False)
        fin.deps = stores
        s.ops.append(fin)
        s.streams["pool"].append(fin)
        s.emit()
        return self.nc


def band_mats():
    m = np.zeros((P, 8, 144), np.float32)
    tin = np.arange(P)[:, None]
    for g, w in enumerate(POOL_W):
        for first in range(2):
            a = np.zeros((P, 144), np.float32)
            jj = np.arange(128)[None, :]
            dlt = jj - tin
            cnt = np.minimum(jj + 1, w) if first else np.full_like(jj, w)
            a[:, :128] = np.where((dlt >= 0) & (dlt < w), 1.0 / cnt, 0.0) - (dlt == 0)
            j2 = np.arange(16)[None, :]
            d2 = 128 + j2 - tin
            a[:, 128:] = np.where((d2 >= 0) & (d2 < w), 1.0 / w, 0.0)
            m[:, first * 4 + g, :] = a
    return m


def make_cols(core, c, ada_b, norm_pre, norm_post, a_scale, kv_norm, kv_ada_b, b_lambda, b_subln):
    cols = np.zeros((P, NCOLS), np.float32)
    cc = np.asarray(c[core * 2:core * 2 + 2], np.float32)
    cols[:, C_CT:C_CT + 16] = cc.reshape(2, 8, P).transpose(2, 1, 0).reshape(P, 16)
    cols[:, C_NPRE:C_NPRE + 32] = np.asarray(norm_pre).reshape(4, 8, P).transpose(2, 0, 1).reshape(P, 32)
    cols[:, C_NPOST:C_NPOST + 32] = np.asarray(norm_post).reshape(4, 8, P).transpose(2, 0, 1).reshape(P, 32)
    cols[:, C_KVN:C_KVN + 8] = np.asarray(kv_norm).reshape(8, P).T
    ab = np.asarray(ada_b).reshape(4, 24, P).transpose(2, 0, 1)
    cols[:, C_ADAB:C_ADAB + 192] = np.repeat(ab[:, :, :, None], 2, axis=3).reshape(P, 192)
    kb = np.asarray(kv_ada_b).reshape(16, P).T
    cols[:, C_KVADAB:C_KVADAB + 32] = np.repeat(kb[:, :, None], 2, axis=2).reshape(P, 32)
    cols[:, C_ASC:C_ASC + 32] = np.asarray(a_scale).reshape(2, 16, P).transpose(2, 0, 1).reshape(P, 32)
    cols[:, C_SUBLN:C_SUBLN + 2] = np.asarray(b_subln).reshape(2, P).T
    cols[:, C_LAMB:C_LAMB + 512] = np.broadcast_to(np.asarray(b_lambda).reshape(1, 512), (P, 512))
    return cols


_NC_CACHE = {}


def kernel(x, c, ada_w, ada_b, norm_pre, norm_post, a_w_in, a_w_group, a_scale, a_w_out,
           kv_norm, kv_ada_w, kv_ada_b, w_kv, b_w_in, b_lambda, b_subln, b_w_out, _n_layers=DEPTH):
    f = lambda a: np.ascontiguousarray(np.asarray(a, dtype=np.float32))
    x = f(x)
    mats = np.concatenate([np.eye(P, dtype=np.float32), band_mats().reshape(P, 8 * 144)], axis=1)
    shared = {
        "mats": mats, "ada_w": f(ada_w), "a_w_in": f(a_w_in),
        "a_w_group": f(a_w_group).reshape(2, 2048, 512), "a_w_out": f(a_w_out),
        "kv_ada_w": f(kv_ada_w), "w_kv": f(w_kv), "b_w_in": f(b_w_in), "b_w_out": f(b_w_out),
    }
    in_maps = []
    for core in range(NCORES):
        m = dict(shared)
        m["x"] = x[core * 2:core * 2 + 2]
        m["cols"] = make_cols(core, f(c), f(ada_b), f(norm_pre), f(norm_post), f(a_scale),
                              f(kv_norm), f(kv_ada_b), f(b_lambda), f(b_subln))
        in_maps.append(m)
    nc = Builder(_n_layers).build()
    res = run_bass_kernel_spmd(nc, in_maps, core_ids=list(range(NCORES)))
    out = np.concatenate([np.asarray(r["out"]) for r in res.results], axis=0)
    return out.astype(np.float32)
```

```python
import numpy as np
import concourse.bass as bass
import concourse.mybir as mybir
from concourse.bass_utils import run_bass_kernel_spmd

F32 = mybir.dt.float32
BF16 = mybir.dt.bfloat16
AF = mybir.ActivationFunctionType
ALU = mybir.AluOpType
AX = mybir.AxisListType

P = 128
D = 1024
S = 2048
NB = 16
NT = 4
DEPTH = 4
NCORES = 8
SEQ_PER_CORE = 2
EPS = 1e-6
SUBLN_EPS = 1e-5
POOL_W = (2, 4, 8, 16)

C_CT = 0
C_NPRE = 16
C_NPOST = 48
C_KVN = 80
C_ADAB = 88
C_KVADAB = 280
C_ASC = 312
C_SUBLN = 344
C_LAMB = 346
NCOLS = 864
NMATS = 128 + 8 * 144


def lambda_init_fn(layer_idx):
    import math
    return 0.8 - 0.6 * math.exp(-0.3 * layer_idx)


class Chan:
    def __init__(self, sem, step):
        self.sem = sem
        self.step = step
        self.count = 0


class Op:
    __slots__ = ("eng", "fn", "deps", "chan", "val", "need", "dma")

    def __init__(self, eng, fn, chan, dma):
        self.eng = eng
        self.fn = fn
        self.chan = chan
        self.dma = dma
        self.deps = []
        self.val = None
        self.need = dma


class Sched:
    ENGS = ("pe", "act", "dve", "pool", "sp")

    def __init__(self, nc):
        self.nc = nc
        self.ops = []
        self.streams = {e: [] for e in self.ENGS}
        self.lastw = {}
        self.readers = {}
        self.echan = {}
        for e in ("pe", "act", "dve", "pool"):
            self.echan[e] = Chan(nc.alloc_semaphore(name="e_" + e), 1)

    def new_chan(self, name):
        return Chan(self.nc.alloc_semaphore(name=name), 16)

    def op(self, eng, fn, R=(), W=(), chan=None, extra=()):
        dma = chan is not None
        o = Op(eng, fn, chan if dma else self.echan.get(eng), dma)
        deps = set(extra)
        for r in R:
            w = self.lastw.get(r)
            if w is not None:
                deps.add(w)
        for w_ in W:
            w = self.lastw.get(w_)
            if w is not None:
                deps.add(w)
            for rd in self.readers.get(w_, ()):
                deps.add(rd)
        for r in R:
            self.readers.setdefault(r, []).append(o)
        for w_ in W:
            self.lastw[w_] = o
            self.readers[w_] = []
        deps.discard(o)
        while any(d.fn is None for d in deps):
            nd = set()
            for d in deps:
                if d.fn is None:
                    nd.update(d.deps)
                else:
                    nd.add(d)
            deps = nd
        if eng == "pe":
            deps = [d for d in deps if not (d.eng == "pe" and not d.dma)]
        o.deps = list(deps)
        self.ops.append(o)
        self.streams[eng].append(o)
        return o

    def barrier(self):
        last = {}
        for o in self.ops:
            if o.fn is not None:
                last[o.chan] = o
        tails = list(last.values())
        for e in self.ENGS:
            w = Op(e, None, None, False)
            w.deps = [t for t in tails if not (t.eng == e and not t.dma and e == "pe")]
            self.ops.append(w)
            self.streams[e].append(w)
        self.lastw = {}
        self.readers = {}

    def emit(self):
        for o in self.ops:
            for d in o.deps:
                d.need = True
        for o in self.ops:
            if o.fn is not None and o.need:
                o.chan.count += o.chan.step
                o.val = o.chan.count
        nc = self.nc
        with nc.Block() as block:
            decos = {"pe": block.tensor, "act": block.scalar, "dve": block.vector,
                     "pool": block.gpsimd, "sp": block.sync}
            for name in self.ENGS:
                stream = self.streams[name]

                def body(e, stream=stream):
                    seen = {}
                    for o in stream:
                        waits = {}
                        for d in o.deps:
                            if waits.get(d.chan, 0) < d.val:
                                waits[d.chan] = d.val
                        for ch, v in waits.items():
                            if seen.get(ch, 0) < v:
                                e.wait_ge(ch.sem, v)
                                seen[ch] = v
                        if o.fn is not None:
                            ins = o.fn(e)
                            if o.val is not None:
                                ins.then_inc(o.chan.sem, o.chan.step)

                decos[name](body)


class Builder:
    def __init__(self, n_layers=DEPTH):
        self.n_layers = n_layers
        nc = bass.Bass("TRN2", target_bir_lowering=False)
        self.nc = nc
        self.s = Sched(nc)
        dt = nc.dram_tensor
        self.x_d = dt("x", [SEQ_PER_CORE, S, D], F32, kind="ExternalInput").ap()
        self.cols_d = dt("cols", [P, NCOLS], F32, kind="ExternalInput").ap()
        self.mats_d = dt("mats", [P, NMATS], F32, kind="ExternalInput").ap()
        self.ada_w_d = dt("ada_w", [DEPTH, D, 3 * D], F32, kind="ExternalInput").ap()
        self.a_w_in_d = dt("a_w_in", [2, D, 4096], F32, kind="ExternalInput").ap()
        self.a_w_group_d = dt("a_w_group", [2, 2048, 512], F32, kind="ExternalInput").ap()
        self.a_w_out_d = dt("a_w_out", [2, 2048, D], F32, kind="ExternalInput").ap()
        self.kv_ada_w_d = dt("kv_ada_w", [D, 2048], F32, kind="ExternalInput").ap()
        self.w_kv_d = dt("w_kv", [D, 2048], F32, kind="ExternalInput").ap()
        self.b_w_in_d = dt("b_w_in", [2, D, 2048], F32, kind="ExternalInput").ap()
        self.b_w_out_d = dt("b_w_out", [2, D, D], F32, kind="ExternalInput").ap()
        self.out_d = dt("out", [SEQ_PER_CORE, S, D], F32, kind="ExternalOutput").ap()
        self.sc = {}
        self.sc_src = {}

        def scr(name, src, rows, cols):
            self.sc[name] = dt("sc_" + name, [rows, cols], BF16).ap()
            self.sc_src[name] = (src, rows, cols)

        for l in range(2):
            scr(f"ain{l}", self.a_w_in_d[l], D, 4096)
            scr(f"ag{l}", self.a_w_group_d[l], 2048, 512)
            scr(f"aout{l}", self.a_w_out_d[l], 2048, D)
        scr("wkv", self.w_kv_d, D, 2048)
        for l in range(1, DEPTH):
            scr(f"ada{l}", self.ada_w_d[l], D, 3 * D)
        scr("kvada", self.kv_ada_w_d, D, 2048)
        for j in range(2):
            scr(f"bin{j}", self.b_w_in_d[j], D, 2048)
            scr(f"bout{j}", self.b_w_out_d[j], D, D)
        self.conv_res = {}
        self.conv_q = []
        self.mod_q = []
        self.ps_pool = 8
        self.deferred = None
        self.next_A = None
        self.next_gate = None
        self.fifo2 = []
        self.alloc()

    def alloc(self):
        nc = self.nc
        off = 0

        def take(n):
            nonlocal off
            o = off
            off += (n + 63) // 64 * 64
            return o

        o_x = take(NB * D * 4)
        o_kv = take(65536)
        o_flex = take(44032)
        o_cols = take(NCOLS * 4)
        o_identf = take(512)
        o_identb = take(256)
        o_onesb = take(256)
        o_onesf = take(512)
        o_bands = take(8 * 144 * 2)
        o_mod = take((4 * 48 + 32) * 4)
        o_small = take(1024)
        o_abf = take(2048)
        o_akv = take(2048)
        o_gf = take(4096)
        o_htm = take(4096)
        o_hT = take(8192)
        o_tmp = take(4096)
        o_junk = take(2048)
        o_mask = take(384)
        total = off
        assert total <= nc.sbuf_bytes_remaining, (total, nc.sbuf_bytes_remaining)
        self.arena = nc.alloc_sbuf_tensor("arena", [P, total // 2], BF16)

        def view(o, dtype, *shape):
            n = 1
            for d_ in shape:
                n *= d_
            esz = 4 if dtype == F32 else 2
            ap = self.arena[:, o // 2: o // 2 + n * esz // 2]
            if dtype == F32:
                ap = ap.bitcast(F32)
            if len(shape) == 2:
                ap = ap.rearrange("p (a b) -> p a b", a=shape[0])
            elif len(shape) == 3:
                ap = ap.rearrange("p (a b c) -> p a b c", a=shape[0], b=shape[1])
            return ap

        self.view = view
        self.x_sb = view(o_x, F32, NB, D)
        self.stage = view(o_x, F32, 8 * 144)
        self.kT = view(o_kv, BF16, 8, S)
        self.V = view(o_kv + 32768, BF16, NB, D)
        self.u_tm = view(o_kv, BF16, 4, 2048)
        self.szA = view(o_kv + 16384, BF16, 16, 512)
        self.gT = view(o_kv + 32768, BF16, 16, 512)
        self.pooledT = [view(o_kv + 49152 + i * 4096, BF16, 4, 512) for i in range(2)]
        self.halo = view(o_kv + 57344, BF16, 2048)
        self.sigA = [view(o_kv + 61440 + i * 2048, F32, 512) for i in range(2)]
        self.o_flex = o_flex
        self.ring_slots = {"A": [o_flex + i * 8192 for i in range(5)],
                           "B": [o_flex + i * 8192 for i in range(2)]}
        ob = o_flex + 16384
        self.yT = view(ob, BF16, 8, 512)
        self.q_h = [[view(ob + 8192 + (i * 2 + c) * 1024, BF16, 512) for c in range(2)]
                    for i in range(2)]
        self.pT = [[view(ob + 12288 + (i * 2 + c) * 1024, BF16, 512) for c in range(2)]
                   for i in range(2)]
        self.fsc = [view(ob + 16384 + i * 2048, F32, 512) for i in range(4)]
        self.f01 = view(ob + 16384, F32, 1024)
        self.sq = view(ob + 24576, BF16, 512)
        self.szb = [view(ob + 25600 + i * 1024, BF16, 512) for i in range(2)]
        assert ob + 27648 <= o_flex + 44032
        self.cols = view(o_cols, F32, NCOLS)
        self.ident_f = view(o_identf, F32, 128)
        self.ident_b = view(o_identb, BF16, 128)
        self.ones_b = view(o_onesb, BF16, 128)
        self.ones_f = view(o_onesf, F32, 128)
        self.bands = view(o_bands, BF16, 8, 144)
        self.modcol = [view(o_mod + l * 192, F32, 48) for l in range(4)]
        self.modkv = view(o_mod + 768, F32, 32)
        small = view(o_small, F32, 256)
        self.small = small
        self.A_bf = view(o_abf, BF16, D)
        self.Akv_bf = view(o_akv, BF16, D)
        self.G_f = view(o_gf, F32, D)
        self.h_tm = [view(o_htm + i * 2048, BF16, D) for i in range(2)]
        self.hT = view(o_hT, BF16, 8, 512)
        self.tmp = view(o_tmp, F32, D)
        self.junk = view(o_junk, BF16, D)
        self.maskrow = view(o_mask, BF16, 192)
        self.ps_all = nc.alloc_psum_tensor("ps_all", [P, 4096], F32)
        self.ps = [self.ps_all[:, i * 512:(i + 1) * 512] for i in range(8)]
        self.rot = 0
        self.ring_n = 0
        self.ring_mode = "A"
        self.ring_ch = [self.s.new_chan(f"ring{i}") for i in range(5)]
        self.ring_ch_sw = [self.s.new_chan(f"ringsw{i}") for i in range(5)]
        self.x_ch = [self.s.new_chan(f"xch{i}") for i in range(NB)]
        self.x_ch_hw = [self.s.new_chan(f"xchhw{i}") for i in range(NB)]
        self.const_ch = self.s.new_chan("constch")
        self.conv_ch = {}

    def bank(self, pool=8):
        b = self.rot % pool
        self.rot += 1
        return b

    def ring_get(self, name, r0, nk, dram_view, shape, split=None, dtype=BF16, eng="sp"):
        slots = self.ring_slots[self.ring_mode]
        slot = self.ring_n % len(slots)
        self.ring_n += 1
        v = self.view(slots[slot], dtype, *shape)
        res = ("ring", slot)
        if split is None:
            self.s.op(eng, lambda e, o=v, i=dram_view: e.dma_start(out=o, in_=i),
                      R=self.conv_res.get(name, []), W=[res],
                      chan=(self.ring_ch if eng == "sp" else self.ring_ch_sw)[slot])
        else:
            subs = []
            for i_, (ov, iv) in enumerate(split(v)):
                sr = ("ringpart", slot, i_)
                subs.append(self.s.op("sp", lambda e, o=ov, i=iv: e.dma_start(out=o, in_=i),
                                      R=self.conv_res[name], W=[sr], chan=self.ring_ch[slot],
                                      extra=[d for d in [self.s.lastw.get(res)] if d is not None]
                                      + list(self.s.readers.get(res, ()))))
            j = Op("none", None, None, False)
            j.deps = subs
            self.s.lastw[res] = j
            self.s.readers[res] = []
        return v, res

    def slab(self, name, r0, nk, c0, ncol):
        src = self.sc[name][r0:r0 + nk * P, c0:c0 + ncol].rearrange("(k p) n -> p k n", p=P)
        return self.ring_get((name, c0) if (name, c0) in self.conv_res else name, r0, nk, src, (nk, ncol))

    def mm(self, out, lhsT, rhs, start, stop, R, W):
        return self.s.op(
            "pe",
            lambda e, o=out, l=lhsT, r=rhs, a=start, b=stop: e.matmul(
                o, lhsT=l, rhs=r, start=a, stop=b, skip_group_check=True),
            R=R, W=W)

    def act(self, out, in_, func, R, W, bias=None, scale=None, accum=None):
        kw = {}
        if bias is not None:
            kw["bias"] = bias
        if scale is not None:
            kw["scale"] = scale
        if accum is not None:
            kw["accum_out"] = accum
        return self.s.op("act", lambda e, o=out, i=in_, f=func, kw=kw: e.activation(
            out=o, in_=i, func=f, **kw), R=R, W=W)

    def dve(self, fn, R, W):
        return self.s.op("dve", fn, R=R, W=W)

    def copy(self, eng, out, in_, R, W):
        if eng == "act":
            return self.act(out, in_, AF.Copy, R, W)
        return self.s.op(eng, lambda e, o=out, i=in_: e.tensor_copy(out=o, in_=i), R=R, W=W)

    def convert_weights(self, names, defer=False):
        for name in names:
            src, rows, cols = self.sc_src[name]
            res = []
            if name.startswith("ain"):
                for c0 in [2048, 2560, 3072, 3584, 0, 512, 1024, 1536]:
                    ch = self.s.new_chan(f"cv_{name}_{c0}")
                    rs = ("conv", name, c0)
                    self.conv_q.append(lambda o=self.sc[name][:, c0:c0 + 512], s_=src[:, c0:c0 + 512], rs=rs, ch=ch:
                                       self.s.op("pool", lambda e: e.dma_start(out=o, in_=s_), R=[], W=[rs], chan=ch))
                    self.conv_res[(name, c0)] = [rs]
                    res.append(rs)
            else:
                ch = self.s.new_chan("cv_" + name)
                step = max(128, (1 << 19) // cols)
                for i, r0 in enumerate(range(0, rows, step)):
                    r1 = min(rows, r0 + step)
                    rs = ("conv", name, i)
                    self.conv_q.append(lambda o=self.sc[name][r0:r1, :], s_=src[r0:r1, :], rs=rs, ch=ch:
                                       self.s.op("pool", lambda e: e.dma_start(out=o, in_=s_), R=[], W=[rs], chan=ch))
                    res.append(rs)
            self.conv_res[name] = res
        if not defer:
            self.issue_conv(1.0)

    def issue_conv(self, frac):
        n = int(round(len(self.conv_q) * frac)) if frac < 1.0 else len(self.conv_q)
        for _ in range(n):
            self.conv_q.pop(0)()

    def prologue(self):
        s = self.s
        sm = self.small
        s.op("sp", lambda e: e.dma_start(out=self.cols, in_=self.cols_d), W=["cols"],
             chan=self.s.new_chan("constch1"))
        s.op("sp", lambda e: e.dma_start(out=self.ident_f, in_=self.mats_d[:, 0:128]),
             W=["ident_f"], chan=self.s.new_chan("constch2"))
        self.copy("act", self.ident_b, self.ident_f, ["ident_f"], ["ident_b"])
        s.op("pool", lambda e: e.memset(self.ones_b, 1.0), W=["ones_b"])
        s.op("pool", lambda e: e.memset(self.ones_f, 1.0), W=["ones_f"])
        c = self.cols[:, C_CT:C_CT + 16]
        t0 = sm[:, 0:16]
        self.act(t0, c, AF.Exp, ["cols"], ["ctmp"], scale=-1.0)
        self.dve(lambda e: e.tensor_scalar_add(out=t0, in0=t0, scalar1=1.0), ["ctmp"], ["ctmp"])
        self.dve(lambda e: e.reciprocal(out=t0, in_=t0), ["ctmp"], ["ctmp"])
        self.condT = self.view_small_bf16()
        self.dve(lambda e: e.tensor_tensor(out=self.condT, in0=t0, in1=c, op=ALU.mult),
                 ["ctmp", "cols"], ["condT"])
        for j in range(2):
            lb = self.cols[:, C_LAMB + j * 256: C_LAMB + (j + 1) * 256]
            pr = sm[:, 128:192]
            for h_ in range(2):
                self.dve(lambda e, a=lb[:, h_ * 128:h_ * 128 + 64], b=lb[:, h_ * 128 + 64:h_ * 128 + 128]:
                         e.tensor_tensor(out=pr, in0=a, in1=b, op=ALU.mult), ["cols"], ["lprod"])
                self.dve(lambda e, o=sm[:, 114 + h_:115 + h_]: e.reduce_sum(out=o, in_=pr, axis=AX.X),
                         ["lprod"], [("lsum", h_)])
                self.act(sm[:, 116 + h_:117 + h_], sm[:, 114 + h_:115 + h_], AF.Exp,
                         [("lsum", h_)], [("lexp", h_)])
            lam = sm[:, 108 + j:109 + j]
            self.dve(lambda e, o=lam: e.tensor_tensor(out=o, in0=sm[:, 116:117], in1=sm[:, 117:118],
                                                      op=ALU.subtract),
                     [("lexp", 0), ("lexp", 1)], [("lam", j)])
            li = lambda_init_fn(2 + j)
            self.dve(lambda e, o=lam, li=li: e.tensor_scalar_add(out=o, in0=o, scalar1=float(li)),
                     [("lam", j)], [("lam", j)])
            self.dve(lambda e, o=sm[:, 110 + j:111 + j], i=lam: e.tensor_scalar_mul(out=o, in0=i, scalar1=-1.0),
                     [("lam", j)], [("neglam", j)])
            self.dve(lambda e, o=sm[:, 112 + j:113 + j], i=self.cols[:, C_SUBLN + j:C_SUBLN + j + 1], li=li:
                     e.tensor_scalar_mul(out=o, in0=i, scalar1=float(1.0 - li)), ["cols"], [("gsub", j)])

    def modcols(self, l, run_now=False):
        name = f"ada{l}" if l < DEPTH else "kvada"
        nch = 24 if l < DEPTH else 16
        dst = self.modcol[l] if l < DEPTH else self.modkv
        bcol = (self.cols[:, C_ADAB + l * 48:C_ADAB + (l + 1) * 48] if l < DEPTH
                else self.cols[:, C_KVADAB:C_KVADAB + 32])
        srcw = self.ada_w_d[l] if l < DEPTH else self.kv_ada_w_d

        def step(sl):
            b = self.bank(self.ps_pool)
            bres = ("ps", b)
            if l == 0:
                src = srcw[:, sl * 512:(sl + 1) * 512].rearrange("(k p) n -> p k n", p=P)
                w, wres = self.ring_get(None, 0, 8, src, (8, 512), eng="pool")
            else:
                w, wres = self.slab(name, 0, 8, sl * 512, 512)
            for cc in range(4):
                for k in range(8):
                    self.mm(self.ps[b][:, 2 * cc:2 * cc + 2], w[:, k, cc * 128:(cc + 1) * 128],
                            self.condT[:, 2 * k:2 * k + 2], k == 0, k == 7,
                            [wres, "condT"], [bres])
            self.dve(lambda e, o=dst[:, 8 * sl:8 * sl + 8], i=self.ps[b][:, 0:8], bc=bcol[:, 8 * sl:8 * sl + 8]:
                     e.tensor_tensor(out=o, in0=i, in1=bc, op=ALU.add), [bres, "cols"], [("mod", l)])

        for sl in range(nch // 4):
            self.mod_q.append(lambda sl=sl: step(sl))
        if run_now:
            self.mod_flush()

    def mod_step(self):
        if self.mod_q:
            self.mod_q.pop(0)()

    def mod_flush(self):
        while self.mod_q:
            self.mod_q.pop(0)()

    def view_small_bf16(self):
        v = self.small[:, 120:128].bitcast(BF16)
        return v

    def bcast_tile(self, col8, dst, dst_res, evac_eng):
        diag = self.tmp
        for k in range(8):
            self.dve(lambda e, o=diag[:, k * 128:(k + 1) * 128], sc=col8[:, k:k + 1]:
                     e.tensor_scalar_mul(out=o, in0=self.ident_f, scalar1=sc),
                     ["ident_f", "col8"], [("tmpk", k)])
        for half in range(2):
            b = self.bank(self.ps_pool)
            self.mm(self.ps[b][:, :], self.ones_f, diag[:, half * 512:(half + 1) * 512], True, True,
                    ["ones_f"] + [("tmpk", k) for k in range(half * 4, half * 4 + 4)], [("ps", b)])
            self.copy(evac_eng, dst[:, half * 512:(half + 1) * 512], self.ps[b][:, :], [("ps", b)],
                      [dst_res])

    def modA(self, l, s_):
        sm = self.small
        mc = self.modcol[l]
        mc3 = mc.rearrange("p (j s) -> p j s", s=2)
        shift = mc3[:, 0:8, s_]
        scale = mc3[:, 8:16, s_]
        gate = mc3[:, 16:24, s_]
        acol = sm[:, 84:92]
        gcol = sm[:, 92:100]
        npre = self.cols[:, C_NPRE + l * 8:C_NPRE + (l + 1) * 8]
        npost = self.cols[:, C_NPOST + l * 8:C_NPOST + (l + 1) * 8]
        self.shift_col = shift
        if l == 2:
            kc3 = self.modkv.rearrange("p (j s) -> p j s", s=2)
            kcol = sm[:, 100:108]
            kvn = self.cols[:, C_KVN:C_KVN + 8]
            self.dve(lambda e: e.scalar_tensor_tensor(out=kcol, in0=kc3[:, 8:16, s_], scalar=1.0, in1=kvn,
                                                      op0=ALU.add, op1=ALU.mult),
                     [("mod", 4), "cols"] + [("tmpk", k) for k in range(8)], ["col8"])
            self.bcast_tile(kcol, self.Akv_bf, "Akv_bf", "act")
            self.kvshift_col = kc3[:, 0:8, s_]
        self.dve(lambda e: e.scalar_tensor_tensor(out=acol, in0=scale, scalar=1.0, in1=npre,
                                                  op0=ALU.add, op1=ALU.mult),
                 [("mod", l), "cols"] + [("tmpk", k) for k in range(8)], ["col8"])
        self.bcast_tile(acol, self.A_bf, "A_bf", "act")

        def gate_tile():
            self.dve(lambda e: e.tensor_tensor(out=gcol, in0=gate, in1=npost, op=ALU.mult),
                     [("mod", l), "cols"] + [("tmpk", k) for k in range(8)], ["col8"])
            self.bcast_tile(gcol, self.G_f, "G_f", "act")

        return gate_tile

    def call_next_A(self):
        if self.next_A is not None:
            f = self.next_A
            self.next_A = None
            f()

    def run_deferred(self):
        if self.deferred is not None:
            d = self.deferred
            self.deferred = None
            d()

    def flush_done(self, keep):
        while len(self.done_fifo) > keep:
            gb = self.done_fifo.pop(0)
            if self.block_done is not None:
                self.block_done(gb)

    def sumsq_block(self, gb):
        self.act(self.junk, self.x_sb[:, gb, :], AF.Square, [("x", gb)], ["junk", ("ss", gb)],
                 accum=self.small[:, 16 + gb:17 + gb])

    def prenorm_stats(self, have_ss=False):
        sm = self.small
        if not have_ss:
            for gb in range(NB):
                self.sumsq_block(gb)
        self.act(sm[:, 32:48], sm[:, 16:32], AF.Ln, [("ss", gb) for gb in range(NB)], ["ln16"],
                 scale=1.0 / D, bias=self.eps_col(EPS))
        self.act(sm[:, 48:64], sm[:, 32:48], AF.Exp, ["ln16"], ["rstd16"], scale=-0.5)

    def eps_col(self, eps):
        return self.small[:, 200:201] if eps == EPS else self.small[:, 201:202]

    def make_h(self, t, gain, gain_res, shiftcol, shift_res):
        sm = self.small
        banks = [self.bank() for _ in range(4)]
        for tb in range(4):
            gb = t * 4 + tb
            hb = self.h_tm[tb % 2]
            hres = ("h_tm", tb % 2)
            self.dve(lambda e, o=hb, x_=self.x_sb[:, gb, :], r=sm[:, 48 + gb:49 + gb]:
                     e.scalar_tensor_tensor(out=o, in0=x_, scalar=r, in1=gain, op0=ALU.mult, op1=ALU.mult),
                     [("x", gb), "rstd16", gain_res], [hres])
            for k in range(8):
                b = banks[k // 2]
                dst = self.ps[b].bitcast(BF16)[:, (k % 2) * 512 + tb * 128:(k % 2) * 512 + (tb + 1) * 128]
                self.s.op("pe", lambda e, o=dst, i=hb[:, k * 128:(k + 1) * 128]:
                          e.transpose(o, i, self.ident_b), R=[hres, "ident_b"], W=[("ps", b)])
        for k in range(8):
            b = banks[k // 2]
            src = self.ps[b].bitcast(BF16)[:, (k % 2) * 512:(k % 2) * 512 + 512]
            self.act(self.hT[:, k, :], src, AF.Identity, [("ps", b), shift_res], [("hT", k)],
                     bias=shiftcol[:, k:k + 1])

    def postnorm_update(self, t, tb, b0):
        sm = self.small
        gb = t * 4 + tb
        b1 = b0 + 1
        for hf, b in enumerate((b0, b1)):
            self.act(self.junk[:, hf * 512:(hf + 1) * 512], self.ps[b][:, :], AF.Square, [("ps", b)],
                     ["junk", ("ssy", hf)], accum=sm[:, 64 + hf:65 + hf])
        self.dve(lambda e: e.tensor_tensor(out=sm[:, 72:73], in0=sm[:, 64:65], in1=sm[:, 65:66], op=ALU.add),
                 [("ssy", 0), ("ssy", 1)], ["sst"])
        self.act(sm[:, 76:77], sm[:, 72:73], AF.Ln, ["sst"], ["lny"], scale=1.0 / D, bias=self.eps_col(EPS))
        self.act(sm[:, 80:81], sm[:, 76:77], AF.Exp, ["lny"], ["rstdy"], scale=-0.5)
        for hf, b in enumerate((b0, b1)):
            self.dve(lambda e, o=self.tmp[:, hf * 512:(hf + 1) * 512], i=self.ps[b][:, :],
                     g=self.G_f[:, hf * 512:(hf + 1) * 512]:
                     e.scalar_tensor_tensor(out=o, in0=i, scalar=sm[:, 80:81], in1=g, op0=ALU.mult, op1=ALU.mult),
                     [("ps", b), "rstdy", "G_f"], [("tmpk", 4 * hf + q) for q in range(4)])
        self.s.op("pool", lambda e, x_=self.x_sb[:, gb, :]: e.tensor_tensor(out=x_, in0=x_, in1=self.tmp, op=ALU.add),
                  R=[("tmpk", k) for k in range(8)] + [("x", gb)], W=[("x", gb)])
        if self.block_store is not None:
            self.block_store(gb)
        self.done_fifo.append(gb)
        self.flush_done(self.done_lag)

    def sigmoid_from_psum(self, zps, zres, scratch, sres):
        self.act(scratch, zps, AF.Exp, [zres], [sres], scale=-1.0)
        self.dve(lambda e: e.tensor_scalar_add(out=scratch, in0=scratch, scalar1=1.0), [sres], [sres])
        self.dve(lambda e: e.reciprocal(out=scratch, in_=scratch), [sres], [sres])

    def layerA(self, l, s_):
        self.ring_mode = "A"
        self.ps_pool = 8
        self.prenorm_stats(self.have_ss)
        self.make_h(0, self.A_bf, "A_bf", self.shift_col, ("mod", l))
        for t in range(NT):
            for zs in range(4):
                w, wres = self.slab(f"ain{l}", 0, 8, 2048 + zs * 512, 512)
                for cc in range(4):
                    j = zs * 4 + cc
                    b = self.bank()
                    for k in range(8):
                        self.mm(self.ps[b][:, :], w[:, k, cc * 128:(cc + 1) * 128], self.hT[:, k, :],
                                k == 0, k == 7, [wres, ("hT", k)], [("ps", b)])
                    self.act(self.szA[:, j, :], self.ps[b][:, :], AF.Silu, [("ps", b)], [("szA", j)])
                self.run_deferred()
                self.mod_step()
            if t > 0:
                self.s.op("pool", lambda e: e.tensor_copy(out=self.halo, in_=self.u_tm[:, 3, :]),
                          R=[("u_tm", 3, g) for g in range(4)], W=["halo"])
            for g in range(4):
                w, wres = self.slab(f"ain{l}", 0, 8, g * 512, 512)
                for tb in range(4):
                    b = self.bank()
                    for k in range(8):
                        self.mm(self.ps[b][:, :], self.hT[:, k, tb * 128:(tb + 1) * 128], w[:, k, :],
                                k == 0, k == 7, [wres, ("hT", k)], [("ps", b)])
                    self.copy("act" if tb % 2 == 0 else "dve", self.u_tm[:, tb, g * 512:(g + 1) * 512],
                              self.ps[b][:, :], [("ps", b)], [("u_tm", tb, g)])
            if t + 1 < NT:
                self.make_h(t + 1, self.A_bf, "A_bf", self.shift_col, ("mod", l))
            else:
                self.call_next_A()
            def pool_group(g):
                pg = self.pooledT[g % 2]
                pres = ("pooledT", g % 2)
                band = self.bands[:, g, :]
                bandf = self.bands[:, 4 + g, :]
                for c in range(4):
                    cc = g * 4 + c
                    b = self.bank()
                    first = True
                    if t > 0:
                        self.mm(self.ps[b][:, 0:16], self.halo[:, cc * 128:(cc + 1) * 128], band[:, 128:144],
                                True, False, ["halo", "bands"], [("ps", b)])
                        first = False
                    for tb in range(4):
                        ncol = 144 if tb < 3 else 128
                        bm = bandf if (t == 0 and tb == 0) else band
                        self.mm(self.ps[b][:, tb * 128:tb * 128 + ncol], self.u_tm[:, tb, cc * 128:(cc + 1) * 128],
                                bm[:, 0:ncol], first, tb == 3, [("u_tm", tb, g), "bands"], [("ps", b)])
                        first = False
                    self.copy("dve", pg[:, c, :], self.ps[b][:, :], [("ps", b)], [pres])

            def wg_group(g):
                pg = self.pooledT[g % 2]
                pres = ("pooledT", g % 2)
                w, wres = self.slab(f"ag{l}", g * 512, 4, 0, 512)
                for c in range(4):
                    cc = g * 4 + c
                    b = self.bank()
                    for k in range(4):
                        self.mm(self.ps[b][:, :], w[:, k, c * 128:(c + 1) * 128], pg[:, k, :],
                                k == 0, k == 3, [wres, pres], [("ps", b)])
                    self.dve(lambda e, o=self.gT[:, cc, :], i=self.ps[b][:, :],
                             sc=self.cols[:, C_ASC + l * 16 + cc:C_ASC + l * 16 + cc + 1], z=self.szA[:, cc, :]:
                             e.scalar_tensor_tensor(out=o, in0=i, scalar=sc, in1=z, op0=ALU.mult, op1=ALU.mult),
                             [("ps", b), "cols", ("szA", cc)], [("gT", cc)])

            pool_group(0)
            pool_group(1)
            wg_group(0)
            pool_group(2)
            wg_group(1)
            pool_group(3)
            wg_group(2)
            wg_group(3)
            for half in range(2):
                ws = [self.slab(f"aout{l}", kh * 1024, 8, half * 512, 512) for kh in range(2)]
                for tb in range(4):
                    b = 2 * tb + half
                    for kc in range(16):
                        w, wres = ws[kc // 8]
                        self.mm(self.ps[b][:, :], self.gT[:, kc, tb * 128:(tb + 1) * 128], w[:, kc % 8, :],
                                kc == 0, kc == 15, [wres, ("gT", kc)], [("ps", b)])
                    if half == 1:
                        self.postnorm_update(t, tb, 2 * tb)
            self.rot = 0
            self.issue_conv({0: 0.25, 1: 0.45, 2: 1.0, 3: 1.0}[t])
            if t == 1 and self.after_tile1 is not None:
                self.after_tile1()

    def layerB(self, l, s_):
        j = l - 2
        sm = self.small
        self.ring_mode = "B"
        self.ps_pool = 4
        self.ring_n = 0
        for i_ in range(2):
            self.s.op("pool", lambda e, o=self.q_h[i_][0][64:128, :]: e.memset(o, 0.0), W=["qzero"])
            self.s.op("pool", lambda e, o=self.q_h[i_][1][0:64, :]: e.memset(o, 0.0), W=["qzero"])
        self.prenorm_stats(self.have_ss)
        neglam = sm[:, 110 + j:111 + j]
        gsub = sm[:, 112 + j:113 + j]
        for t in range(NT):
            if l == 2:
                if t == 0:
                    self.make_h(t, self.Akv_bf, "Akv_bf", self.kvshift_col, ("mod", 4))
                for hs in range(2):
                    w, wres = self.slab("wkv", 0, 8, hs * 512, 512)
                    for hh in range(4):
                        hd = hs * 4 + hh
                        b = self.bank()
                        for k in range(8):
                            self.mm(self.ps[b][:, :], w[:, k, hh * 128:(hh + 1) * 128], self.hT[:, k, :],
                                    k == 0, k == 7, [wres, ("hT", k)], [("ps", b)])
                        self.copy("dve", self.kT[:, hd, t * 512:(t + 1) * 512], self.ps[b][:, :],
                                  [("ps", b)], [("kT", hd, t)])
                for hs in range(2):
                    w, wres = self.slab("wkv", 0, 8, 1024 + hs * 512, 512)
                    for tb in range(4):
                        b = self.bank()
                        for k in range(8):
                            self.mm(self.ps[b][:, :], self.hT[:, k, tb * 128:(tb + 1) * 128], w[:, k, :],
                                    k == 0, k == 7, [wres, ("hT", k)], [("ps", b)])
                        self.copy("dve", self.V[:, t * 4 + tb, hs * 512:(hs + 1) * 512], self.ps[b][:, :],
                                  [("ps", b)], [("V", t * 4 + tb)])
            if l == 2 or t == 0:
                self.make_h(t, self.A_bf, "A_bf", self.shift_col, ("mod", l))
            if t == NT - 1:
                self.call_next_A()
            pending = None
            pending_b = None
            nkb = 4 * t + 4

            def head_pre(hd):
                srcq = self.sc[f"bin{j}"][:, hd * 128:(hd + 1) * 128].rearrange("(k p) e -> p k e", p=P)
                srcz = self.sc[f"bin{j}"][:, 1024 + hd * 128:1024 + (hd + 1) * 128].rearrange("(k p) e -> p k e", p=P)
                w, wres = self.ring_get(f"bin{j}", 0, 8, None, (8, 2, 128),
                                        split=lambda v, a=srcq, b_=srcz: [(v[:, :, 0, :], a), (v[:, :, 1, :], b_)])
                qb = self.q_h[hd % 2]
                qres = ("q_h", hd % 2)
                b = self.bank(4)
                for k in range(8):
                    self.mm(self.ps[b][:, :], w[:, k, 0, :], self.hT[:, k, :], k == 0, k == 7,
                            [wres, ("hT", k)], [("ps", b)])
                self.copy("dve", qb[0][0:64, :], self.ps[b][0:64, :], [("ps", b), "qzero"], [qres])
                self.copy("dve", qb[1][64:128, :], self.ps[b][64:128, :], [("ps", b), "qzero"], [qres])
                zb = self.bank(4)
                for k in range(8):
                    self.mm(self.ps[zb][:, :], w[:, k, 1, :], self.hT[:, k, :], k == 0, k == 7,
                            [wres, ("hT", k)], [("ps", zb)])
                self.copy("dve", self.szb[hd % 2], self.ps[zb][:, :], [("ps", zb)], [("szb", hd % 2)])
                self.run_deferred()
                self.mod_step()

            head_pre(0)
            for hd in range(8):
                qb = self.q_h[hd % 2]
                qres = ("q_h", hd % 2)
                szb = self.szb[hd % 2]
                szr = ("szb", hd % 2)
                sgt = self.tmp[:, (hd % 2) * 512:(hd % 2) * 512 + 512]
                sgr = [("tmpk", 4 * (hd % 2) + q_) for q_ in range(4)]

                def emit_S(kb):
                    c0 = 0 if kb < 4 * t else (kb - 4 * t) * 128
                    diag = kb >= 4 * t
                    pbuf = self.pT[kb % 2]
                    for c in range(2):
                        b = self.bank(4)
                        self.mm(self.ps[b][:, c0:512], self.kT[:, hd, kb * 128:(kb + 1) * 128],
                                qb[c][:, c0:512], True, not diag,
                                [("kT", hd, kb // 4), qres], [("ps", b)])
                        if diag:
                            self.mm(self.ps[b][:, c0:c0 + 64], self.maskrow[0:1, 0:128], self.maskrow[0:1, 128:192],
                                    False, True, ["maskrow"], [("ps", b)])
                        self.act(pbuf[c][:, c0:512], self.ps[b][:, c0:512], AF.Exp, [("ps", b)],
                                 [("pT", kb % 2, c)], scale=0.125)

                emit_S(0)
                for kb in range(nkb):
                    if kb + 1 < nkb:
                        emit_S(kb + 1)
                    c0 = 0 if kb < 4 * t else (kb - 4 * t) * 128
                    pbuf = self.pT[kb % 2]
                    for c in range(2):
                        pres = ("pT", kb % 2, c)
                        vh = self.V[:, kb, hd * 128:(hd + 1) * 128]
                        for which, (ob, lhs_full) in enumerate(((4 + c, vh), (6 + c, self.ones_b))):
                            R_ = [pres, ("V", kb) if which == 0 else "ones_b"]
                            self.mm(self.ps[ob][:, c0:512], lhs_full, pbuf[c][:, c0:512], kb == 0, kb == nkb - 1,
                                    R_, [("ps", ob)])
                    if kb == 0:
                        self.act(sgt, szb, AF.Exp, [szr], sgr, scale=-1.0)
                    elif kb == 1:
                        self.act(sgt, sgt, AF.Ln, sgr, sgr, bias=self.small[:, 202:203])
                    elif kb == 2:
                        if pending is not None:
                            pending_b = pending()
                            pending = None
                    if kb == 3 and pending_b is not None:
                        pending_b()
                        pending_b = None
                    if kb == min(4, nkb - 1):
                        self.act(sgt, sgt, AF.Exp, sgr, sgr, scale=-1.0)
                        self.dve(lambda e, o=szb, g_=sgt: e.tensor_tensor(out=o, in0=o, in1=g_, op=ALU.mult),
                                 [szr] + sgr, [szr])
                if pending is not None:
                    pending()()
                    pending = None
                if pending_b is not None:
                    pending_b()
                    pending_b = None
                stage1b = self.head_epilogue(hd, neglam, gsub)
                if hd < 7:
                    head_pre(hd + 1)
                if hd == 6 and t + 1 < NT:
                    if l == 2:
                        self.make_h(t + 1, self.Akv_bf, "Akv_bf", self.kvshift_col, ("mod", 4))
                    else:
                        self.make_h(t + 1, self.A_bf, "A_bf", self.shift_col, ("mod", l))
                pending = stage1b()
            wsl = [self.slab(f"bout{j}", 0, 8, half * 512, 512) for half in range(2)]
            pair = {0: 0, 1: 2, 2: 6, 3: 4}
            for tb in range(4):
                if tb == 2:
                    self.rot = 4
                    pending(8)()
                    pending = None
                for half in range(2):
                    w, wres = wsl[half]
                    b = pair[tb] + half
                    for e_ in range(7):
                        self.mm(self.ps[b][:, :], self.yT[:, e_, tb * 128:(tb + 1) * 128], w[:, e_, :],
                                e_ == 0, False, [wres, ("yT", e_)], [("ps", b)])
            for tb in range(4):
                for half in range(2):
                    w, wres = wsl[half]
                    b = pair[tb] + half
                    self.mm(self.ps[b][:, :], self.yT[:, 7, tb * 128:(tb + 1) * 128], w[:, 7, :],
                            False, True, [wres, ("yT", 7)], [("ps", b)])
                self.postnorm_update(t, tb, pair[tb])
            self.issue_conv({0: 0.25, 1: 0.45, 2: 1.0, 3: 1.0}[t])
            if t == 1 and self.after_tile1 is not None:
                self.after_tile1()
            self.rot = 0

    def head_epilogue(self, hd, neglam, gsub):
        f0, f1, f2, f3 = self.fsc
        f01 = self.f01
        o0, o1 = self.ps[4], self.ps[5]
        l01 = self.ps_all[:, 3072:4096]
        self.act(f01, l01, AF.Ln, [("ps", 6), ("ps", 7)], [("f", 0), ("f", 1)])
        self.copy("dve", f2, o0[:, :], [("ps", 4)], [("f", 2)])
        self.copy("dve", f3, o1[:, :], [("ps", 5)], [("f", 3)])
        self.act(f01, f01, AF.Exp, [("f", 0), ("f", 1)], [("f", 0), ("f", 1)], scale=-1.0)

        def stage1b():
            self.dve(lambda e: e.tensor_tensor(out=f2, in0=f2, in1=f0, op=ALU.mult),
                     [("f", 2), ("f", 0)], [("f", 2)])
            self.dve(lambda e: e.scalar_tensor_tensor(out=f3, in0=f3, scalar=neglam, in1=f1,
                                                      op0=ALU.mult, op1=ALU.mult),
                     [("f", 3), ("f", 1), ("neglam", 0), ("neglam", 1)], [("f", 3)])
            self.dve(lambda e: e.tensor_tensor(out=f2, in0=f2, in1=f3, op=ALU.add),
                     [("f", 2), ("f", 3)], [("f", 2)])
            self.dve(lambda e: e.tensor_tensor(out=self.sq, in0=f2, in1=f2, op=ALU.mult), [("f", 2)], ["sq"])

            def stage2(pool=4):
                mb = self.bank(pool)
                self.mm(self.ps[mb][:, :], self.ones_b, self.sq, True, True, ["ones_b", "sq"], [("ps", mb)])
                self.act(f0, self.ps[mb][:, :], AF.Ln, [("ps", mb)], [("f", 0)], scale=1.0 / 128,
                         bias=self.eps_col(SUBLN_EPS))

                def stage2b():
                    self.act(f0, f0, AF.Exp, [("f", 0)], [("f", 0)], scale=-0.5)
                    self.dve(lambda e: e.scalar_tensor_tensor(out=f2, in0=f2, scalar=gsub, in1=f0,
                                                              op0=ALU.mult, op1=ALU.mult),
                             [("f", 2), ("f", 0), ("gsub", 0), ("gsub", 1)], [("f", 2)])
                    self.dve(lambda e: e.tensor_tensor(out=self.yT[:, hd, :], in0=f2, in1=self.szb[hd % 2],
                                                       op=ALU.mult),
                             [("f", 2), ("szb", hd % 2)], [("yT", hd)])

                return stage2b

            return stage2

        return stage1b

    def build(self):
        s = self.s
        s.op("sp", lambda e: e.dma_start(out=self.stage, in_=self.mats_d[:, 128:NMATS]),
             W=[("x", 0), ("x", 1)], chan=self.const_ch)
        self.copy("dve", self.bands.rearrange("p a b -> p (a b)"), self.stage,
                  [("x", 0), ("x", 1)], ["bands"])
        self.convert_weights(["ain0", "ag0", "aout0", "ada1"])
        s.op("pool", lambda e: e.memset(self.small[:, 200:201], EPS), W=["epsc"])
        s.op("pool", lambda e: e.memset(self.small[:, 201:202], SUBLN_EPS), W=["epsc2"])
        s.op("pool", lambda e: e.memset(self.small[:, 202:203], 1.0), W=["onec"])
        s.op("pool", lambda e: e.memset(self.maskrow[0:1, 0:64], 0.0), W=["maskrow"])
        s.op("pool", lambda e: e.memset(self.maskrow[0:1, 64:128], 1.0), W=["maskrow"])
        s.op("pool", lambda e: e.memset(self.maskrow[0:1, 128:192], -30000.0), W=["maskrow"])
        stores = []
        self.have_ss = False
        self.block_done = None
        for gb in range(NB):
            s.op("sp", lambda e, o=self.x_sb[:, gb, :], i=self.x_d[0, gb * P:(gb + 1) * P, :]:
                 e.dma_start(out=o, in_=i), R=[], W=[("x", gb)], chan=self.x_ch_hw[gb])
        self.prologue()
        self.modcols(0, run_now=True)
        self.next_gate = self.modA(0, 0)
        stages = [(s_, l) for s_ in range(SEQ_PER_CORE) for l in range(self.n_layers)]
        for si, (s_, l) in enumerate(stages):
            self.deferred = self.next_gate
            self.next_gate = None
            if si + 1 < len(stages):
                def nxtA(ns=stages[si + 1]):
                    self.mod_flush()
                    self.next_gate = self.modA(ns[1], ns[0])
                self.next_A = nxtA
            self.after_tile0 = None
            self.after_tile1 = None
            if s_ == 0:
                nxt = {0: ["ada2", "kvada", "ain1", "ag1", "aout1"], 1: ["ada3", "wkv", "bin0", "bout0"],
                       2: ["bin1", "bout1"]}.get(l)
                if nxt:
                    self.convert_weights(nxt, defer=True)
                if l + 1 < self.n_layers:
                    self.modcols(l + 1)
                    if l + 1 == 2:
                        self.modcols(DEPTH)
            last = l + 1 == self.n_layers
            self.done_fifo = []
            self.fifo2 = []
            self.block_store = None
            if not last:
                self.block_done = self.sumsq_block
                self.done_lag = 1
            else:
                def store(gb, s_=s_):
                    stores.append(s.op(
                        "pool", lambda e, i=self.x_sb[:, gb, :], o=self.out_d[s_, gb * P:(gb + 1) * P, :]:
                        e.dma_start(out=o, in_=i), R=[("x", gb)], W=[], chan=self.x_ch[gb]))
                self.block_store = store
                if s_ + 1 < SEQ_PER_CORE:
                    def load_next(gb, s_=s_):
                        s.op("pool", lambda e, o=self.x_sb[:, gb, :], i=self.x_d[s_ + 1, gb * P:(gb + 1) * P, :]:
                             e.dma_start(out=o, in_=i), R=[], W=[("x", gb)], chan=self.x_ch[gb])
                        self.fifo2.append(gb)
                        while len(self.fifo2) > 4:
                            self.sumsq_block(self.fifo2.pop(0))
                    self.block_done = load_next
                else:
                    self.block_done = None
                self.done_lag = 4
            if l < 2:
                self.layerA(l, s_)
            else:
                if l == 2:
                    s.barrier()
                self.layerB(l, s_)
            self.flush_done(0)
            while self.fifo2:
                self.sumsq_block(self.fifo2.pop(0))
            self.run_deferred()
            self.call_next_A()
            self.mod_flush()
            self.have_ss = True
            if self.n_layers > 2 and l + 1 == self.n_layers and s_ + 1 < SEQ_PER_CORE:
                s.barrier()
        fin = Op("pool", None, None, False)
        fin.deps = stores
        s.ops.append(fin)
        s.streams["pool"].append(fin)
        s.emit()
        return self.nc


def band_mats():
    m = np.zeros((P, 8, 144), np.float32)
    tin = np.arange(P)[:, None]
    for g, w in enumerate(POOL_W):
        for first in range(2):
            a = np.zeros((P, 144), np.float32)
            jj = np.arange(128)[None, :]
            dlt = jj - tin
            cnt = np.minimum(jj + 1, w) if first else np.full_like(jj, w)
            a[:, :128] = np.where((dlt >= 0) & (dlt < w), 1.0 / cnt, 0.0) - (dlt == 0)
            j2 = np.arange(16)[None, :]
            d2 = 128 + j2 - tin
            a[:, 128:] = np.where((d2 >= 0) & (d2 < w), 1.0 / w, 0.0)
            m[:, first * 4 + g, :] = a
    return m


def make_cols(core, c, ada_b, norm_pre, norm_post, a_scale, kv_norm, kv_ada_b, b_lambda, b_subln):
    cols = np.zeros((P, NCOLS), np.float32)
    cc = np.asarray(c[core * 2:core * 2 + 2], np.float32)
    cols[:, C_CT:C_CT + 16] = cc.reshape(2, 8, P).transpose(2, 1, 0).reshape(P, 16)
    cols[:, C_NPRE:C_NPRE + 32] = np.asarray(norm_pre).reshape(4, 8, P).transpose(2, 0, 1).reshape(P, 32)
    cols[:, C_NPOST:C_NPOST + 32] = np.asarray(norm_post).reshape(4, 8, P).transpose(2, 0, 1).reshape(P, 32)
    cols[:, C_KVN:C_KVN + 8] = np.asarray(kv_norm).reshape(8, P).T
    ab = np.asarray(ada_b).reshape(4, 24, P).transpose(2, 0, 1)
    cols[:, C_ADAB:C_ADAB + 192] = np.repeat(ab[:, :, :, None], 2, axis=3).reshape(P, 192)
    kb = np.asarray(kv_ada_b).reshape(16, P).T
    cols[:, C_KVADAB:C_KVADAB + 32] = np.repeat(kb[:, :, None], 2, axis=2).reshape(P, 32)
    cols[:, C_ASC:C_ASC + 32] = np.asarray(a_scale).reshape(2, 16, P).transpose(2, 0, 1).reshape(P, 32)
    cols[:, C_SUBLN:C_SUBLN + 2] = np.asarray(b_subln).reshape(2, P).T
    cols[:, C_LAMB:C_LAMB + 512] = np.broadcast_to(np.asarray(b_lambda).reshape(1, 512), (P, 512))
    return cols


_NC_CACHE = {}


def kernel(x, c, ada_w, ada_b, norm_pre, norm_post, a_w_in, a_w_group, a_scale, a_w_out,
           kv_norm, kv_ada_w, kv_ada_b, w_kv, b_w_in, b_lambda, b_subln, b_w_out, _n_layers=DEPTH):
    f = lambda a: np.ascontiguousarray(np.asarray(a, dtype=np.float32))
    x = f(x)
    mats = np.concatenate([np.eye(P, dtype=np.float32), band_mats().reshape(P, 8 * 144)], axis=1)
    shared = {
        "mats": mats, "ada_w": f(ada_w), "a_w_in": f(a_w_in),
        "a_w_group": f(a_w_group).reshape(2, 2048, 512), "a_w_out": f(a_w_out),
        "kv_ada_w": f(kv_ada_w), "w_kv": f(w_kv), "b_w_in": f(b_w_in), "b_w_out": f(b_w_out),
    }
    in_maps = []
    for core in range(NCORES):
        m = dict(shared)
        m["x"] = x[core * 2:core * 2 + 2]
        m["cols"] = make_cols(core, f(c), f(ada_b), f(norm_pre), f(norm_post), f(a_scale),
                              f(kv_norm), f(kv_ada_b), f(b_lambda), f(b_subln))
        in_maps.append(m)
    nc = Builder(_n_layers).build()
    res = run_bass_kernel_spmd(nc, in_maps, core_ids=list(range(NCORES)))
    out = np.concatenate([np.asarray(r["out"]) for r in res.results], axis=0)
    return out.astype(np.float32)
```

```python
import numpy as np
import concourse.bass as bass
import concourse.mybir as mybir
from concourse.bass_utils import run_bass_kernel_spmd

F32 = mybir.dt.float32
BF16 = mybir.dt.bfloat16
AF = mybir.ActivationFunctionType
ALU = mybir.AluOpType
AX = mybir.AxisListType

P = 128
D = 1024
S = 2048
NB = 16
NT = 4
DEPTH = 4
NCORES = 8
SEQ_PER_CORE = 2
EPS = 1e-6
SUBLN_EPS = 1e-5
POOL_W = (2, 4, 8, 16)

C_CT = 0
C_NPRE = 16
C_NPOST = 48
C_KVN = 80
C_ADAB = 88
C_KVADAB = 280
C_ASC = 312
C_SUBLN = 344
C_LAMB = 346
NCOLS = 864
NMATS = 128 + 8 * 144


def lambda_init_fn(layer_idx):
    import math
    return 0.8 - 0.6 * math.exp(-0.3 * layer_idx)


class Chan:
    def __init__(self, sem, step):
        self.sem = sem
        self.step = step
        self.count = 0


class Op:
    __slots__ = ("eng", "fn", "deps", "chan", "val", "need", "dma")

    def __init__(self, eng, fn, chan, dma):
        self.eng = eng
        self.fn = fn
        self.chan = chan
        self.dma = dma
        self.deps = []
        self.val = None
        self.need = dma


class Sched:
    ENGS = ("pe", "act", "dve", "pool", "sp")

    def __init__(self, nc):
        self.nc = nc
        self.ops = []
        self.streams = {e: [] for e in self.ENGS}
        self.lastw = {}
        self.readers = {}
        self.echan = {}
        for e in ("pe", "act", "dve", "pool"):
            self.echan[e] = Chan(nc.alloc_semaphore(name="e_" + e), 1)

    def new_chan(self, name):
        return Chan(self.nc.alloc_semaphore(name=name), 16)

    def op(self, eng, fn, R=(), W=(), chan=None, extra=()):
        dma = chan is not None
        o = Op(eng, fn, chan if dma else self.echan.get(eng), dma)
        deps = set(extra)
        for r in R:
            w = self.lastw.get(r)
            if w is not None:
                deps.add(w)
        for w_ in W:
            w = self.lastw.get(w_)
            if w is not None:
                deps.add(w)
            for rd in self.readers.get(w_, ()):
                deps.add(rd)
        for r in R:
            self.readers.setdefault(r, []).append(o)
        for w_ in W:
            self.lastw[w_] = o
            self.readers[w_] = []
        deps.discard(o)
        while any(d.fn is None for d in deps):
            nd = set()
            for d in deps:
                if d.fn is None:
                    nd.update(d.deps)
                else:
                    nd.add(d)
            deps = nd
        if eng == "pe":
            deps = [d for d in deps if not (d.eng == "pe" and not d.dma)]
        o.deps = list(deps)
        self.ops.append(o)
        self.streams[eng].append(o)
        return o

    def barrier(self):
        last = {}
        for o in self.ops:
            if o.fn is not None:
                last[o.chan] = o
        tails = list(last.values())
        for e in self.ENGS:
            w = Op(e, None, None, False)
            w.deps = [t for t in tails if not (t.eng == e and not t.dma and e == "pe")]
            self.ops.append(w)
            self.streams[e].append(w)
        self.lastw = {}
        self.readers = {}

    def emit(self):
        for o in self.ops:
            for d in o.deps:
                d.need = True
        for o in self.ops:
            if o.fn is not None and o.need:
                o.chan.count += o.chan.step
                o.val = o.chan.count
        nc = self.nc
        with nc.Block() as block:
            decos = {"pe": block.tensor, "act": block.scalar, "dve": block.vector,
                     "pool": block.gpsimd, "sp": block.sync}
            for name in self.ENGS:
                stream = self.streams[name]

                def body(e, stream=stream):
                    seen = {}
                    for o in stream:
                        waits = {}
                        for d in o.deps:
                            if waits.get(d.chan, 0) < d.val:
                                waits[d.chan] = d.val
                        for ch, v in waits.items():
                            if seen.get(ch, 0) < v:
                                e.wait_ge(ch.sem, v)
                                seen[ch] = v
                        if o.fn is not None:
                            ins = o.fn(e)
                            if o.val is not None:
                                ins.then_inc(o.chan.sem, o.chan.step)

                decos[name](body)


class Builder:
    def __init__(self, n_layers=DEPTH):
        self.n_layers = n_layers
        nc = bass.Bass("TRN2", target_bir_lowering=False)
        self.nc = nc
        self.s = Sched(nc)
        dt = nc.dram_tensor
        self.x_d = dt("x", [SEQ_PER_CORE, S, D], F32, kind="ExternalInput").ap()
        self.cols_d = dt("cols", [P, NCOLS], F32, kind="ExternalInput").ap()
        self.mats_d = dt("mats", [P, NMATS], F32, kind="ExternalInput").ap()
        self.ada_w_d = dt("ada_w", [DEPTH, D, 3 * D], F32, kind="ExternalInput").ap()
        self.a_w_in_d = dt("a_w_in", [2, D, 4096], F32, kind="ExternalInput").ap()
        self.a_w_group_d = dt("a_w_group", [2, 2048, 512], F32, kind="ExternalInput").ap()
        self.a_w_out_d = dt("a_w_out", [2, 2048, D], F32, kind="ExternalInput").ap()
        self.kv_ada_w_d = dt("kv_ada_w", [D, 2048], F32, kind="ExternalInput").ap()
        self.w_kv_d = dt("w_kv", [D, 2048], F32, kind="ExternalInput").ap()
        self.b_w_in_d = dt("b_w_in", [2, D, 2048], F32, kind="ExternalInput").ap()
        self.b_w_out_d = dt("b_w_out", [2, D, D], F32, kind="ExternalInput").ap()
        self.out_d = dt("out", [SEQ_PER_CORE, S, D], F32, kind="ExternalOutput").ap()
        self.sc = {}
        self.sc_src = {}

        def scr(name, src, rows, cols):
            self.sc[name] = dt("sc_" + name, [rows, cols], BF16).ap()
            self.sc_src[name] = (src, rows, cols)

        for l in range(2):
            scr(f"ain{l}", self.a_w_in_d[l], D, 4096)
            scr(f"ag{l}", self.a_w_group_d[l], 2048, 512)
            scr(f"aout{l}", self.a_w_out_d[l], 2048, D)
        scr("wkv", self.w_kv_d, D, 2048)
        for l in range(1, DEPTH):
            scr(f"ada{l}", self.ada_w_d[l], D, 3 * D)
        scr("kvada", self.kv_ada_w_d, D, 2048)
        for j in range(2):
            scr(f"bin{j}", self.b_w_in_d[j], D, 2048)
            scr(f"bout{j}", self.b_w_out_d[j], D, D)
        self.conv_res = {}
        self.conv_q = []
        self.mod_q = []
        self.ps_pool = 8
        self.deferred = None
        self.next_A = None
        self.next_gate = None
        self.fifo2 = []
        self.alloc()

    def alloc(self):
        nc = self.nc
        off = 0

        def take(n):
            nonlocal off
            o = off
            off += (n + 63) // 64 * 64
            return o

        o_x = take(NB * D * 4)
        o_kv = take(65536)
        o_flex = take(44032)
        o_cols = take(NCOLS * 4)
        o_identf = take(512)
        o_identb = take(256)
        o_onesb = take(256)
        o_onesf = take(512)
        o_bands = take(8 * 144 * 2)
        o_mod = take((4 * 48 + 32) * 4)
        o_small = take(1024)
        o_abf = take(2048)
        o_akv = take(2048)
        o_gf = take(4096)
        o_htm = take(4096)
        o_hT = take(8192)
        o_tmp = take(4096)
        o_junk = take(2048)
        o_mask = take(384)
        total = off
        assert total <= nc.sbuf_bytes_remaining, (total, nc.sbuf_bytes_remaining)
        self.arena = nc.alloc_sbuf_tensor("arena", [P, total // 2], BF16)

        def view(o, dtype, *shape):
            n = 1
            for d_ in shape:
                n *= d_
            esz = 4 if dtype == F32 else 2
            ap = self.arena[:, o // 2: o // 2 + n * esz // 2]
            if dtype == F32:
                ap = ap.bitcast(F32)
            if len(shape) == 2:
                ap = ap.rearrange("p (a b) -> p a b", a=shape[0])
            elif len(shape) == 3:
                ap = ap.rearrange("p (a b c) -> p a b c", a=shape[0], b=shape[1])
            return ap

        self.view = view
        self.x_sb = view(o_x, F32, NB, D)
        self.stage = view(o_x, F32, 8 * 144)
        self.kT = view(o_kv, BF16, 8, S)
        self.V = view(o_kv + 32768, BF16, NB, D)
        self.u_tm = view(o_kv, BF16, 4, 2048)
        self.szA = view(o_kv + 16384, BF16, 16, 512)
        self.gT = view(o_kv + 32768, BF16, 16, 512)
        self.pooledT = [view(o_kv + 49152 + i * 4096, BF16, 4, 512) for i in range(2)]
        self.halo = view(o_kv + 57344, BF16, 2048)
        self.sigA = [view(o_kv + 61440 + i * 2048, F32, 512) for i in range(2)]
        self.o_flex = o_flex
        self.ring_slots = {"A": [o_flex + i * 8192 for i in range(5)],
                           "B": [o_flex + i * 8192 for i in range(2)]}
        ob = o_flex + 16384
        self.yT = view(ob, BF16, 8, 512)
        self.q_h = [[view(ob + 8192 + (i * 2 + c) * 1024, BF16, 512) for c in range(2)]
                    for i in range(2)]
        self.pT = [[view(ob + 12288 + (i * 2 + c) * 1024, BF16, 512) for c in range(2)]
                   for i in range(2)]
        self.fsc = [view(ob + 16384 + i * 2048, F32, 512) for i in range(4)]
        self.f01 = view(ob + 16384, F32, 1024)
        self.sq = view(ob + 24576, BF16, 512)
        self.szb = [view(ob + 25600 + i * 1024, BF16, 512) for i in range(2)]
        assert ob + 27648 <= o_flex + 44032
        self.cols = view(o_cols, F32, NCOLS)
        self.ident_f = view(o_identf, F32, 128)
        self.ident_b = view(o_identb, BF16, 128)
        self.ones_b = view(o_onesb, BF16, 128)
        self.ones_f = view(o_onesf, F32, 128)
        self.bands = view(o_bands, BF16, 8, 144)
        self.modcol = [view(o_mod + l * 192, F32, 48) for l in range(4)]
        self.modkv = view(o_mod + 768, F32, 32)
        small = view(o_small, F32, 256)
        self.small = small
        self.A_bf = view(o_abf, BF16, D)
        self.Akv_bf = view(o_akv, BF16, D)
        self.G_f = view(o_gf, F32, D)
        self.h_tm = [view(o_htm + i * 2048, BF16, D) for i in range(2)]
        self.hT = view(o_hT, BF16, 8, 512)
        self.tmp = view(o_tmp, F32, D)
        self.junk = view(o_junk, BF16, D)
        self.maskrow = view(o_mask, BF16, 192)
        self.ps_all = nc.alloc_psum_tensor("ps_all", [P, 4096], F32)
        self.ps = [self.ps_all[:, i * 512:(i + 1) * 512] for i in range(8)]
        self.rot = 0
        self.ring_n = 0
        self.ring_mode = "A"
        self.ring_ch = [self.s.new_chan(f"ring{i}") for i in range(5)]
        self.ring_ch_sw = [self.s.new_chan(f"ringsw{i}") for i in range(5)]
        self.x_ch = [self.s.new_chan(f"xch{i}") for i in range(NB)]
        self.x_ch_hw = [self.s.new_chan(f"xchhw{i}") for i in range(NB)]
        self.const_ch = self.s.new_chan("constch")
        self.conv_ch = {}

    def bank(self, pool=8):
        b = self.rot % pool
        self.rot += 1
        return b

    def ring_get(self, name, r0, nk, dram_view, shape, split=None, dtype=BF16, eng="sp"):
        slots = self.ring_slots[self.ring_mode]
        slot = self.ring_n % len(slots)
        self.ring_n += 1
        v = self.view(slots[slot], dtype, *shape)
        res = ("ring", slot)
        if split is None:
            self.s.op(eng, lambda e, o=v, i=dram_view: e.dma_start(out=o, in_=i),
                      R=self.conv_res.get(name, []), W=[res],
                      chan=(self.ring_ch if eng == "sp" else self.ring_ch_sw)[slot])
        else:
            subs = []
            for i_, (ov, iv) in enumerate(split(v)):
                sr = ("ringpart", slot, i_)
                subs.append(self.s.op("sp", lambda e, o=ov, i=iv: e.dma_start(out=o, in_=i),
                                      R=self.conv_res[name], W=[sr], chan=self.ring_ch[slot],
                                      extra=[d for d in [self.s.lastw.get(res)] if d is not None]
                                      + list(self.s.readers.get(res, ()))))
            j = Op("none", None, None, False)
            j.deps = subs
            self.s.lastw[res] = j
            self.s.readers[res] = []
        return v, res

    def slab(self, name, r0, nk, c0, ncol):
        src = self.sc[name][r0:r0 + nk * P, c0:c0 + ncol].rearrange("(k p) n -> p k n", p=P)
        return self.ring_get((name, c0) if (name, c0) in self.conv_res else name, r0, nk, src, (nk, ncol))

    def mm(self, out, lhsT, rhs, start, stop, R, W):
        return self.s.op(
            "pe",
            lambda e, o=out, l=lhsT, r=rhs, a=start, b=stop: e.matmul(
                o, lhsT=l, rhs=r, start=a, stop=b, skip_group_check=True),
            R=R, W=W)

    def act(self, out, in_, func, R, W, bias=None, scale=None, accum=None):
        kw = {}
        if bias is not None:
            kw["bias"] = bias
        if scale is not None:
            kw["scale"] = scale
        if accum is not None:
            kw["accum_out"] = accum
        return self.s.op("act", lambda e, o=out, i=in_, f=func, kw=kw: e.activation(
            out=o, in_=i, func=f, **kw), R=R, W=W)

    def dve(self, fn, R, W):
        return self.s.op("dve", fn, R=R, W=W)

    def copy(self, eng, out, in_, R, W):
        if eng == "act":
            return self.act(out, in_, AF.Copy, R, W)
        return self.s.op(eng, lambda e, o=out, i=in_: e.tensor_copy(out=o, in_=i), R=R, W=W)

    def convert_weights(self, names, defer=False):
        for name in names:
            src, rows, cols = self.sc_src[name]
            res = []
            if name.startswith("ain"):
                for c0 in [2048, 2560, 3072, 3584, 0, 512, 1024, 1536]:
                    ch = self.s.new_chan(f"cv_{name}_{c0}")
                    rs = ("conv", name, c0)
                    self.conv_q.append(lambda o=self.sc[name][:, c0:c0 + 512], s_=src[:, c0:c0 + 512], rs=rs, ch=ch:
                                       self.s.op("pool", lambda e: e.dma_start(out=o, in_=s_), R=[], W=[rs], chan=ch))
                    self.conv_res[(name, c0)] = [rs]
                    res.append(rs)
            else:
                ch = self.s.new_chan("cv_" + name)
                step = max(128, (1 << 19) // cols)
                for i, r0 in enumerate(range(0, rows, step)):
                    r1 = min(rows, r0 + step)
                    rs = ("conv", name, i)
                    self.conv_q.append(lambda o=self.sc[name][r0:r1, :], s_=src[r0:r1, :], rs=rs, ch=ch:
                                       self.s.op("pool", lambda e: e.dma_start(out=o, in_=s_), R=[], W=[rs], chan=ch))
                    res.append(rs)
            self.conv_res[name] = res
        if not defer:
            self.issue_conv(1.0)

    def issue_conv(self, frac):
        n = int(round(len(self.conv_q) * frac)) if frac < 1.0 else len(self.conv_q)
        for _ in range(n):
            self.conv_q.pop(0)()

    def prologue(self):
        s = self.s
        sm = self.small
        s.op("sp", lambda e: e.dma_start(out=self.cols, in_=self.cols_d), W=["cols"],
             chan=self.s.new_chan("constch1"))
        s.op("sp", lambda e: e.dma_start(out=self.ident_f, in_=self.mats_d[:, 0:128]),
             W=["ident_f"], chan=self.s.new_chan("constch2"))
        self.copy("act", self.ident_b, self.ident_f, ["ident_f"], ["ident_b"])
        s.op("pool", lambda e: e.memset(self.ones_b, 1.0), W=["ones_b"])
        s.op("pool", lambda e: e.memset(self.ones_f, 1.0), W=["ones_f"])
        c = self.cols[:, C_CT:C_CT + 16]
        t0 = sm[:, 0:16]
        self.act(t0, c, AF.Exp, ["cols"], ["ctmp"], scale=-1.0)
        self.dve(lambda e: e.tensor_scalar_add(out=t0, in0=t0, scalar1=1.0), ["ctmp"], ["ctmp"])
        self.dve(lambda e: e.reciprocal(out=t0, in_=t0), ["ctmp"], ["ctmp"])
        self.condT = self.view_small_bf16()
        self.dve(lambda e: e.tensor_tensor(out=self.condT, in0=t0, in1=c, op=ALU.mult),
                 ["ctmp", "cols"], ["condT"])
        for j in range(2):
            lb = self.cols[:, C_LAMB + j * 256: C_LAMB + (j + 1) * 256]
            pr = sm[:, 128:192]
            for h_ in range(2):
                self.dve(lambda e, a=lb[:, h_ * 128:h_ * 128 + 64], b=lb[:, h_ * 128 + 64:h_ * 128 + 128]:
                         e.tensor_tensor(out=pr, in0=a, in1=b, op=ALU.mult), ["cols"], ["lprod"])
                self.dve(lambda e, o=sm[:, 114 + h_:115 + h_]: e.reduce_sum(out=o, in_=pr, axis=AX.X),
                         ["lprod"], [("lsum", h_)])
                self.act(sm[:, 116 + h_:117 + h_], sm[:, 114 + h_:115 + h_], AF.Exp,
                         [("lsum", h_)], [("lexp", h_)])
            lam = sm[:, 108 + j:109 + j]
            self.dve(lambda e, o=lam: e.tensor_tensor(out=o, in0=sm[:, 116:117], in1=sm[:, 117:118],
                                                      op=ALU.subtract),
                     [("lexp", 0), ("lexp", 1)], [("lam", j)])
            li = lambda_init_fn(2 + j)
            self.dve(lambda e, o=lam, li=li: e.tensor_scalar_add(out=o, in0=o, scalar1=float(li)),
                     [("lam", j)], [("lam", j)])
            self.dve(lambda e, o=sm[:, 110 + j:111 + j], i=lam: e.tensor_scalar_mul(out=o, in0=i, scalar1=-1.0),
                     [("lam", j)], [("neglam", j)])
            self.dve(lambda e, o=sm[:, 112 + j:113 + j], i=self.cols[:, C_SUBLN + j:C_SUBLN + j + 1], li=li:
                     e.tensor_scalar_mul(out=o, in0=i, scalar1=float(1.0 - li)), ["cols"], [("gsub", j)])

    def modcols(self, l, run_now=False):
        name = f"ada{l}" if l < DEPTH else "kvada"
        nch = 24 if l < DEPTH else 16
        dst = self.modcol[l] if l < DEPTH else self.modkv
        bcol = (self.cols[:, C_ADAB + l * 48:C_ADAB + (l + 1) * 48] if l < DEPTH
                else self.cols[:, C_KVADAB:C_KVADAB + 32])
        srcw = self.ada_w_d[l] if l < DEPTH else self.kv_ada_w_d

        def step(sl):
            b = self.bank(self.ps_pool)
            bres = ("ps", b)
            if l == 0:
                src = srcw[:, sl * 512:(sl + 1) * 512].rearrange("(k p) n -> p k n", p=P)
                w, wres = self.ring_get(None, 0, 8, src, (8, 512), eng="pool")
            else:
                w, wres = self.slab(name, 0, 8, sl * 512, 512)
            for cc in range(4):
                for k in range(8):
                    self.mm(self.ps[b][:, 2 * cc:2 * cc + 2], w[:, k, cc * 128:(cc + 1) * 128],
                            self.condT[:, 2 * k:2 * k + 2], k == 0, k == 7,
                            [wres, "condT"], [bres])
            self.dve(lambda e, o=dst[:, 8 * sl:8 * sl + 8], i=self.ps[b][:, 0:8], bc=bcol[:, 8 * sl:8 * sl + 8]:
                     e.tensor_tensor(out=o, in0=i, in1=bc, op=ALU.add), [bres, "cols"], [("mod", l)])

        for sl in range(nch // 4):
            self.mod_q.append(lambda sl=sl: step(sl))
        if run_now:
            self.mod_flush()

    def mod_step(self):
        if self.mod_q:
            self.mod_q.pop(0)()

    def mod_flush(self):
        while self.mod_q:
            self.mod_q.pop(0)()

    def view_small_bf16(self):
        v = self.small[:, 120:128].bitcast(BF16)
        return v

    def bcast_tile(self, col8, dst, dst_res, evac_eng):
        diag = self.tmp
        for k in range(8):
            self.dve(lambda e, o=diag[:, k * 128:(k + 1) * 128], sc=col8[:, k:k + 1]:
                     e.tensor_scalar_mul(out=o, in0=self.ident_f, scalar1=sc),
                     ["ident_f", "col8"], [("tmpk", k)])
        for half in range(2):
            b = self.bank(self.ps_pool)
            self.mm(self.ps[b][:, :], self.ones_f, diag[:, half * 512:(half + 1) * 512], True, True,
                    ["ones_f"] + [("tmpk", k) for k in range(half * 4, half * 4 + 4)], [("ps", b)])
            self.copy(evac_eng, dst[:, half * 512:(half + 1) * 512], self.ps[b][:, :], [("ps", b)],
                      [dst_res])

    def modA(self, l, s_):
        sm = self.small
        mc = self.modcol[l]
        mc3 = mc.rearrange("p (j s) -> p j s", s=2)
        shift = mc3[:, 0:8, s_]
        scale = mc3[:, 8:16, s_]
        gate = mc3[:, 16:24, s_]
        acol = sm[:, 84:92]
        gcol = sm[:, 92:100]
        npre = self.cols[:, C_NPRE + l * 8:C_NPRE + (l + 1) * 8]
        npost = self.cols[:, C_NPOST + l * 8:C_NPOST + (l + 1) * 8]
        self.shift_col = shift
        if l == 2:
            kc3 = self.modkv.rearrange("p (j s) -> p j s", s=2)
            kcol = sm[:, 100:108]
            kvn = self.cols[:, C_KVN:C_KVN + 8]
            self.dve(lambda e: e.scalar_tensor_tensor(out=kcol, in0=kc3[:, 8:16, s_], scalar=1.0, in1=kvn,
                                                      op0=ALU.add, op1=ALU.mult),
                     [("mod", 4), "cols"] + [("tmpk", k) for k in range(8)], ["col8"])
            self.bcast_tile(kcol, self.Akv_bf, "Akv_bf", "act")
            self.kvshift_col = kc3[:, 0:8, s_]
        self.dve(lambda e: e.scalar_tensor_tensor(out=acol, in0=scale, scalar=1.0, in1=npre,
                                                  op0=ALU.add, op1=ALU.mult),
                 [("mod", l), "cols"] + [("tmpk", k) for k in range(8)], ["col8"])
        self.bcast_tile(acol, self.A_bf, "A_bf", "act")

        def gate_tile():
            self.dve(lambda e: e.tensor_tensor(out=gcol, in0=gate, in1=npost, op=ALU.mult),
                     [("mod", l), "cols"] + [("tmpk", k) for k in range(8)], ["col8"])
            self.bcast_tile(gcol, self.G_f, "G_f", "act")

        return gate_tile

    def call_next_A(self):
        if self.next_A is not None:
            f = self.next_A
            self.next_A = None
            f()

    def run_deferred(self):
        if self.deferred is not None:
            d = self.deferred
            self.deferred = None
            d()

    def flush_done(self, keep):
        while len(self.done_fifo) > keep:
            gb = self.done_fifo.pop(0)
            if self.block_done is not None:
                self.block_done(gb)

    def sumsq_block(self, gb):
        self.act(self.junk, self.x_sb[:, gb, :], AF.Square, [("x", gb)], ["junk", ("ss", gb)],
                 accum=self.small[:, 16 + gb:17 + gb])

    def prenorm_stats(self, have_ss=False):
        sm = self.small
        if not have_ss:
            for gb in range(NB):
                self.sumsq_block(gb)
        self.act(sm[:, 32:48], sm[:, 16:32], AF.Ln, [("ss", gb) for gb in range(NB)], ["ln16"],
                 scale=1.0 / D, bias=self.eps_col(EPS))
        self.act(sm[:, 48:64], sm[:, 32:48], AF.Exp, ["ln16"], ["rstd16"], scale=-0.5)

    def eps_col(self, eps):
        return self.small[:, 200:201] if eps == EPS else self.small[:, 201:202]

    def make_h(self, t, gain, gain_res, shiftcol, shift_res):
        sm = self.small
        banks = [self.bank() for _ in range(4)]
        for tb in range(4):
            gb = t * 4 + tb
            hb = self.h_tm[tb % 2]
            hres = ("h_tm", tb % 2)
            self.dve(lambda e, o=hb, x_=self.x_sb[:, gb, :], r=sm[:, 48 + gb:49 + gb]:
                     e.scalar_tensor_tensor(out=o, in0=x_, scalar=r, in1=gain, op0=ALU.mult, op1=ALU.mult),
                     [("x", gb), "rstd16", gain_res], [hres])
            for k in range(8):
                b = banks[k // 2]
                dst = self.ps[b].bitcast(BF16)[:, (k % 2) * 512 + tb * 128:(k % 2) * 512 + (tb + 1) * 128]
                self.s.op("pe", lambda e, o=dst, i=hb[:, k * 128:(k + 1) * 128]:
                          e.transpose(o, i, self.ident_b), R=[hres, "ident_b"], W=[("ps", b)])
        for k in range(8):
            b = banks[k // 2]
            src = self.ps[b].bitcast(BF16)[:, (k % 2) * 512:(k % 2) * 512 + 512]
            self.act(self.hT[:, k, :], src, AF.Identity, [("ps", b), shift_res], [("hT", k)],
                     bias=shiftcol[:, k:k + 1])

    def postnorm_update(self, t, tb, b0):
        sm = self.small
        gb = t * 4 + tb
        b1 = b0 + 1
        for hf, b in enumerate((b0, b1)):
            self.act(self.junk[:, hf * 512:(hf + 1) * 512], self.ps[b][:, :], AF.Square, [("ps", b)],
                     ["junk", ("ssy", hf)], accum=sm[:, 64 + hf:65 + hf])
        self.dve(lambda e: e.tensor_tensor(out=sm[:, 72:73], in0=sm[:, 64:65], in1=sm[:, 65:66], op=ALU.add),
                 [("ssy", 0), ("ssy", 1)], ["sst"])
        self.act(sm[:, 76:77], sm[:, 72:73], AF.Ln, ["sst"], ["lny"], scale=1.0 / D, bias=self.eps_col(EPS))
        self.act(sm[:, 80:81], sm[:, 76:77], AF.Exp, ["lny"], ["rstdy"], scale=-0.5)
        for hf, b in enumerate((b0, b1)):
            self.dve(lambda e, o=self.tmp[:, hf * 512:(hf + 1) * 512], i=self.ps[b][:, :],
                     g=self.G_f[:, hf * 512:(hf + 1) * 512]:
                     e.scalar_tensor_tensor(out=o, in0=i, scalar=sm[:, 80:81], in1=g, op0=ALU.mult, op1=ALU.mult),
                     [("ps", b), "rstdy", "G_f"], [("tmpk", 4 * hf + q) for q in range(4)])
        self.s.op("pool", lambda e, x_=self.x_sb[:, gb, :]: e.tensor_tensor(out=x_, in0=x_, in1=self.tmp, op=ALU.add),
                  R=[("tmpk", k) for k in range(8)] + [("x", gb)], W=[("x", gb)])
        if self.block_store is not None:
            self.block_store(gb)
        self.done_fifo.append(gb)
        self.flush_done(self.done_lag)

    def sigmoid_from_psum(self, zps, zres, scratch, sres):
        self.act(scratch, zps, AF.Exp, [zres], [sres], scale=-1.0)
        self.dve(lambda e: e.tensor_scalar_add(out=scratch, in0=scratch, scalar1=1.0), [sres], [sres])
        self.dve(lambda e: e.reciprocal(out=scratch, in_=scratch), [sres], [sres])

    def layerA(self, l, s_):
        self.ring_mode = "A"
        self.ps_pool = 8
        self.prenorm_stats(self.have_ss)
        self.make_h(0, self.A_bf, "A_bf", self.shift_col, ("mod", l))
        for t in range(NT):
            for zs in range(4):
                w, wres = self.slab(f"ain{l}", 0, 8, 2048 + zs * 512, 512)
                for cc in range(4):
                    j = zs * 4 + cc
                    b = self.bank()
                    for k in range(8):
                        self.mm(self.ps[b][:, :], w[:, k, cc * 128:(cc + 1) * 128], self.hT[:, k, :],
                                k == 0, k == 7, [wres, ("hT", k)], [("ps", b)])
                    self.act(self.szA[:, j, :], self.ps[b][:, :], AF.Silu, [("ps", b)], [("szA", j)])
                self.run_deferred()
                self.mod_step()
            if t > 0:
                self.s.op("pool", lambda e: e.tensor_copy(out=self.halo, in_=self.u_tm[:, 3, :]),
                          R=[("u_tm", 3, g) for g in range(4)], W=["halo"])
            for g in range(4):
                w, wres = self.slab(f"ain{l}", 0, 8, g * 512, 512)
                for tb in range(4):
                    b = self.bank()
                    for k in range(8):
                        self.mm(self.ps[b][:, :], self.hT[:, k, tb * 128:(tb + 1) * 128], w[:, k, :],
                                k == 0, k == 7, [wres, ("hT", k)], [("ps", b)])
                    self.copy("act" if tb % 2 == 0 else "dve", self.u_tm[:, tb, g * 512:(g + 1) * 512],
                              self.ps[b][:, :], [("ps", b)], [("u_tm", tb, g)])
            if t + 1 < NT:
                self.make_h(t + 1, self.A_bf, "A_bf", self.shift_col, ("mod", l))
            else:
                self.call_next_A()
            def pool_group(g):
                pg = self.pooledT[g % 2]
                pres = ("pooledT", g % 2)
                band = self.bands[:, g, :]
                bandf = self.bands[:, 4 + g, :]
                for c in range(4):
                    cc = g * 4 + c
                    b = self.bank()
                    first = True
                    if t > 0:
                        self.mm(self.ps[b][:, 0:16], self.halo[:, cc * 128:(cc + 1) * 128], band[:, 128:144],
                                True, False, ["halo", "bands"], [("ps", b)])
                        first = False
                    for tb in range(4):
                        ncol = 144 if tb < 3 else 128
                        bm = bandf if (t == 0 and tb == 0) else band
                        self.mm(self.ps[b][:, tb * 128:tb * 128 + ncol], self.u_tm[:, tb, cc * 128:(cc + 1) * 128],
                                bm[:, 0:ncol], first, tb == 3, [("u_tm", tb, g), "bands"], [("ps", b)])
                        first = False
                    self.copy("dve", pg[:, c, :], self.ps[b][:, :], [("ps", b)], [pres])

            def wg_group(g):
                pg = self.pooledT[g % 2]
                pres = ("pooledT", g % 2)
                w, wres = self.slab(f"ag{l}", g * 512, 4, 0, 512)
                for c in range(4):
                    cc = g * 4 + c
                    b = self.bank()
                    for k in range(4):
                        self.mm(self.ps[b][:, :], w[:, k, c * 128:(c + 1) * 128], pg[:, k, :],
                                k == 0, k == 3, [wres, pres], [("ps", b)])
                    self.dve(lambda e, o=self.gT[:, cc, :], i=self.ps[b][:, :],
                             sc=self.cols[:, C_ASC + l * 16 + cc:C_ASC + l * 16 + cc + 1], z=self.szA[:, cc, :]:
                             e.scalar_tensor_tensor(out=o, in0=i, scalar=sc, in1=z, op0=ALU.mult, op1=ALU.mult),
                             [("ps", b), "cols", ("szA", cc)], [("gT", cc)])

            pool_group(0)
            pool_group(1)
            wg_group(0)
            pool_group(2)
            wg_group(1)
            pool_group(3)
            wg_group(2)
            wg_group(3)
            for half in range(2):
                ws = [self.slab(f"aout{l}", kh * 1024, 8, half * 512, 512) for kh in range(2)]
                for tb in range(4):
                    b = 2 * tb + half
                    for kc in range(16):
                        w, wres = ws[kc // 8]
                        self.mm(self.ps[b][:, :], self.gT[:, kc, tb * 128:(tb + 1) * 128], w[:, kc % 8, :],
                                kc == 0, kc == 15, [wres, ("gT", kc)], [("ps", b)])
                    if half == 1:
                        self.postnorm_update(t, tb, 2 * tb)
            self.rot = 0
            self.issue_conv({0: 0.25, 1: 0.45, 2: 1.0, 3: 1.0}[t])
            if t == 1 and self.after_tile1 is not None:
                self.after_tile1()

    def layerB(self, l, s_):
        j = l - 2
        sm = self.small
        self.ring_mode = "B"
        self.ps_pool = 4
        self.ring_n = 0
        for i_ in range(2):
            self.s.op("pool", lambda e, o=self.q_h[i_][0][64:128, :]: e.memset(o, 0.0), W=["qzero"])
            self.s.op("pool", lambda e, o=self.q_h[i_][1][0:64, :]: e.memset(o, 0.0), W=["qzero"])
        self.prenorm_stats(self.have_ss)
        neglam = sm[:, 110 + j:111 + j]
        gsub = sm[:, 112 + j:113 + j]
        for t in range(NT):
            if l == 2:
                if t == 0:
                    self.make_h(t, self.Akv_bf, "Akv_bf", self.kvshift_col, ("mod", 4))
                for hs in range(2):
                    w, wres = self.slab("wkv", 0, 8, hs * 512, 512)
                    for hh in range(4):
                        hd = hs * 4 + hh
                        b = self.bank()
                        for k in range(8):
                            self.mm(self.ps[b][:, :], w[:, k, hh * 128:(hh + 1) * 128], self.hT[:, k, :],
                                    k == 0, k == 7, [wres, ("hT", k)], [("ps", b)])
                        self.copy("dve", self.kT[:, hd, t * 512:(t + 1) * 512], self.ps[b][:, :],
                                  [("ps", b)], [("kT", hd, t)])
                for hs in range(2):
                    w, wres = self.slab("wkv", 0, 8, 1024 + hs * 512, 512)
                    for tb in range(4):
                        b = self.bank()
                        for k in range(8):
                            self.mm(self.ps[b][:, :], self.hT[:, k, tb * 128:(tb + 1) * 128], w[:, k, :],
                                    k == 0, k == 7, [wres, ("hT", k)], [("ps", b)])
                        self.copy("dve", self.V[:, t * 4 + tb, hs * 512:(hs + 1) * 512], self.ps[b][:, :],
                                  [("ps", b)], [("V", t * 4 + tb)])
            if l == 2 or t == 0:
                self.make_h(t, self.A_bf, "A_bf", self.shift_col, ("mod", l))
            if t == NT - 1:
                self.call_next_A()
            pending = None
            pending_b = None
            nkb = 4 * t + 4

            def head_pre(hd):
                srcq = self.sc[f"bin{j}"][:, hd * 128:(hd + 1) * 128].rearrange("(k p) e -> p k e", p=P)
                srcz = self.sc[f"bin{j}"][:, 1024 + hd * 128:1024 + (hd + 1) * 128].rearrange("(k p) e -> p k e", p=P)
                w, wres = self.ring_get(f"bin{j}", 0, 8, None, (8, 2, 128),
                                        split=lambda v, a=srcq, b_=srcz: [(v[:, :, 0, :], a), (v[:, :, 1, :], b_)])
                qb = self.q_h[hd % 2]
                qres = ("q_h", hd % 2)
                b = self.bank(4)
                for k in range(8):
                    self.mm(self.ps[b][:, :], w[:, k, 0, :], self.hT[:, k, :], k == 0, k == 7,
                            [wres, ("hT", k)], [("ps", b)])
                self.copy("dve", qb[0][0:64, :], self.ps[b][0:64, :], [("ps", b), "qzero"], [qres])
                self.copy("dve", qb[1][64:128, :], self.ps[b][64:128, :], [("ps", b), "qzero"], [qres])
                zb = self.bank(4)
                for k in range(8):
                    self.mm(self.ps[zb][:, :], w[:, k, 1, :], self.hT[:, k, :], k == 0, k == 7,
                            [wres, ("hT", k)], [("ps", zb)])
                self.copy("dve", self.szb[hd % 2], self.ps[zb][:, :], [("ps", zb)], [("szb", hd % 2)])
                self.run_deferred()
                self.mod_step()

            head_pre(0)
            for hd in range(8):
                qb = self.q_h[hd % 2]
                qres = ("q_h", hd % 2)
                szb = self.szb[hd % 2]
                szr = ("szb", hd % 2)
                sgt = self.tmp[:, (hd % 2) * 512:(hd % 2) * 512 + 512]
                sgr = [("tmpk", 4 * (hd % 2) + q_) for q_ in range(4)]

                def emit_S(kb):
                    c0 = 0 if kb < 4 * t else (kb - 4 * t) * 128
                    diag = kb >= 4 * t
                    pbuf = self.pT[kb % 2]
                    for c in range(2):
                        b = self.bank(4)
                        self.mm(self.ps[b][:, c0:512], self.kT[:, hd, kb * 128:(kb + 1) * 128],
                                qb[c][:, c0:512], True, not diag,
                                [("kT", hd, kb // 4), qres], [("ps", b)])
                        if diag:
                            self.mm(self.ps[b][:, c0:c0 + 64], self.maskrow[0:1, 0:128], self.maskrow[0:1, 128:192],
                                    False, True, ["maskrow"], [("ps", b)])
                        self.act(pbuf[c][:, c0:512], self.ps[b][:, c0:512], AF.Exp, [("ps", b)],
                                 [("pT", kb % 2, c)], scale=0.125)

                emit_S(0)
                for kb in range(nkb):
                    if kb + 1 < nkb:
                        emit_S(kb + 1)
                    c0 = 0 if kb < 4 * t else (kb - 4 * t) * 128
                    pbuf = self.pT[kb % 2]
                    for c in range(2):
                        pres = ("pT", kb % 2, c)
                        vh = self.V[:, kb, hd * 128:(hd + 1) * 128]
                        for which, (ob, lhs_full) in enumerate(((4 + c, vh), (6 + c, self.ones_b))):
                            R_ = [pres, ("V", kb) if which == 0 else "ones_b"]
                            self.mm(self.ps[ob][:, c0:512], lhs_full, pbuf[c][:, c0:512], kb == 0, kb == nkb - 1,
                                    R_, [("ps", ob)])
                    if kb == 0:
                        self.act(sgt, szb, AF.Exp, [szr], sgr, scale=-1.0)
                    elif kb == 1:
                        self.act(sgt, sgt, AF.Ln, sgr, sgr, bias=self.small[:, 202:203])
                    elif kb == 2:
                        if pending is not None:
                            pending_b = pending()
                            pending = None
                    if kb == 3 and pending_b is not None:
                        pending_b()
                        pending_b = None
                    if kb == min(5, nkb - 1):
                        self.act(sgt, sgt, AF.Exp, sgr, sgr, scale=-1.0)
                        self.dve(lambda e, o=szb, g_=sgt: e.tensor_tensor(out=o, in0=o, in1=g_, op=ALU.mult),
                                 [szr] + sgr, [szr])
                if pending is not None:
                    pending()()
                    pending = None
                if pending_b is not None:
                    pending_b()
                    pending_b = None
                stage1b = self.head_epilogue(hd, neglam, gsub)
                if hd < 7:
                    head_pre(hd + 1)
                if hd == 6 and t + 1 < NT:
                    if l == 2:
                        self.make_h(t + 1, self.Akv_bf, "Akv_bf", self.kvshift_col, ("mod", 4))
                    else:
                        self.make_h(t + 1, self.A_bf, "A_bf", self.shift_col, ("mod", l))
                pending = stage1b()
            wsl = [self.slab(f"bout{j}", 0, 8, half * 512, 512) for half in range(2)]
            pair = {0: 0, 1: 2, 2: 6, 3: 4}
            for tb in range(4):
                if tb == 2:
                    self.rot = 4
                    pending(8)()
                    pending = None
                for half in range(2):
                    w, wres = wsl[half]
                    b = pair[tb] + half
                    for e_ in range(7):
                        self.mm(self.ps[b][:, :], self.yT[:, e_, tb * 128:(tb + 1) * 128], w[:, e_, :],
                                e_ == 0, False, [wres, ("yT", e_)], [("ps", b)])
            for tb in range(4):
                for half in range(2):
                    w, wres = wsl[half]
                    b = pair[tb] + half
                    self.mm(self.ps[b][:, :], self.yT[:, 7, tb * 128:(tb + 1) * 128], w[:, 7, :],
                            False, True, [wres, ("yT", 7)], [("ps", b)])
                self.postnorm_update(t, tb, pair[tb])
            self.issue_conv({0: 0.25, 1: 0.45, 2: 1.0, 3: 1.0}[t])
            if t == 1 and self.after_tile1 is not None:
                self.after_tile1()
            self.rot = 0

    def head_epilogue(self, hd, neglam, gsub):
        f0, f1, f2, f3 = self.fsc
        f01 = self.f01
        o0, o1 = self.ps[4], self.ps[5]
        l01 = self.ps_all[:, 3072:4096]
        self.act(f01, l01, AF.Ln, [("ps", 6), ("ps", 7)], [("f", 0), ("f", 1)])
        self.copy("dve", f2, o0[:, :], [("ps", 4)], [("f", 2)])
        self.copy("dve", f3, o1[:, :], [("ps", 5)], [("f", 3)])
        self.act(f01, f01, AF.Exp, [("f", 0), ("f", 1)], [("f", 0), ("f", 1)], scale=-1.0)

        def stage1b():
            self.dve(lambda e: e.tensor_tensor(out=f2, in0=f2, in1=f0, op=ALU.mult),
                     [("f", 2), ("f", 0)], [("f", 2)])
            self.dve(lambda e: e.scalar_tensor_tensor(out=f3, in0=f3, scalar=neglam, in1=f1,
                                                      op0=ALU.mult, op1=ALU.mult),
                     [("f", 3), ("f", 1), ("neglam", 0), ("neglam", 1)], [("f", 3)])
            self.dve(lambda e: e.tensor_tensor(out=f2, in0=f2, in1=f3, op=ALU.add),
                     [("f", 2), ("f", 3)], [("f", 2)])
            self.dve(lambda e: e.tensor_tensor(out=self.sq, in0=f2, in1=f2, op=ALU.mult), [("f", 2)], ["sq"])

            def stage2(pool=4):
                mb = self.bank(pool)
                self.mm(self.ps[mb][:, :], self.ones_b, self.sq, True, True, ["ones_b", "sq"], [("ps", mb)])
                self.act(f0, self.ps[mb][:, :], AF.Ln, [("ps", mb)], [("f", 0)], scale=1.0 / 128,
                         bias=self.eps_col(SUBLN_EPS))

                def stage2b():
                    self.act(f0, f0, AF.Exp, [("f", 0)], [("f", 0)], scale=-0.5)
                    self.dve(lambda e: e.scalar_tensor_tensor(out=f2, in0=f2, scalar=gsub, in1=f0,
                                                              op0=ALU.mult, op1=ALU.mult),
                             [("f", 2), ("f", 0), ("gsub", 0), ("gsub", 1)], [("f", 2)])
                    self.dve(lambda e: e.tensor_tensor(out=self.yT[:, hd, :], in0=f2, in1=self.szb[hd % 2],
                                                       op=ALU.mult),
                             [("f", 2), ("szb", hd % 2)], [("yT", hd)])

                return stage2b

            return stage2

        return stage1b

    def build(self):
        s = self.s
        s.op("sp", lambda e: e.dma_start(out=self.stage, in_=self.mats_d[:, 128:NMATS]),
             W=[("x", 0), ("x", 1)], chan=self.const_ch)
        self.copy("dve", self.bands.rearrange("p a b -> p (a b)"), self.stage,
                  [("x", 0), ("x", 1)], ["bands"])
        self.convert_weights(["ain0", "ag0", "aout0", "ada1"])
        s.op("pool", lambda e: e.memset(self.small[:, 200:201], EPS), W=["epsc"])
        s.op("pool", lambda e: e.memset(self.small[:, 201:202], SUBLN_EPS), W=["epsc2"])
        s.op("pool", lambda e: e.memset(self.small[:, 202:203], 1.0), W=["onec"])
        s.op("pool", lambda e: e.memset(self.maskrow[0:1, 0:64], 0.0), W=["maskrow"])
        s.op("pool", lambda e: e.memset(self.maskrow[0:1, 64:128], 1.0), W=["maskrow"])
        s.op("pool", lambda e: e.memset(self.maskrow[0:1, 128:192], -30000.0), W=["maskrow"])
        stores = []
        self.have_ss = False
        self.block_done = None
        for gb in range(NB):
            s.op("sp", lambda e, o=self.x_sb[:, gb, :], i=self.x_d[0, gb * P:(gb + 1) * P, :]:
                 e.dma_start(out=o, in_=i), R=[], W=[("x", gb)], chan=self.x_ch_hw[gb])
        self.prologue()
        self.modcols(0, run_now=True)
        self.next_gate = self.modA(0, 0)
        stages = [(s_, l) for s_ in range(SEQ_PER_CORE) for l in range(self.n_layers)]
        for si, (s_, l) in enumerate(stages):
            self.deferred = self.next_gate
            self.next_gate = None
            if si + 1 < len(stages):
                def nxtA(ns=stages[si + 1]):
                    self.mod_flush()
                    self.next_gate = self.modA(ns[1], ns[0])
                self.next_A = nxtA
            self.after_tile0 = None
            self.after_tile1 = None
            if s_ == 0:
                nxt = {0: ["ada2", "kvada", "ain1", "ag1", "aout1"], 1: ["ada3", "wkv", "bin0", "bout0"],
                       2: ["bin1", "bout1"]}.get(l)
                if nxt:
                    self.convert_weights(nxt, defer=True)
                if l + 1 < self.n_layers:
                    self.modcols(l + 1)
                    if l + 1 == 2:
                        self.modcols(DEPTH)
            last = l + 1 == self.n_layers
            self.done_fifo = []
            self.fifo2 = []
            self.block_store = None
            if not last:
                self.block_done = self.sumsq_block
                self.done_lag = 1
            else:
                def store(gb, s_=s_):
                    stores.append(s.op(
                        "pool", lambda e, i=self.x_sb[:, gb, :], o=self.out_d[s_, gb * P:(gb + 1) * P, :]:
                        e.dma_start(out=o, in_=i), R=[("x", gb)], W=[], chan=self.x_ch[gb]))
                self.block_store = store
                if s_ + 1 < SEQ_PER_CORE:
                    def load_next(gb, s_=s_):
                        s.op("pool", lambda e, o=self.x_sb[:, gb, :], i=self.x_d[s_ + 1, gb * P:(gb + 1) * P, :]:
                             e.dma_start(out=o, in_=i), R=[], W=[("x", gb)], chan=self.x_ch[gb])
                        self.fifo2.append(gb)
                        while len(self.fifo2) > 4:
                            self.sumsq_block(self.fifo2.pop(0))
                    self.block_done = load_next
                else:
                    self.block_done = None
                self.done_lag = 4
            if l < 2:
                self.layerA(l, s_)
            else:
                if l == 2:
                    s.barrier()
                self.layerB(l, s_)
            self.flush_done(0)
            while self.fifo2:
                self.sumsq_block(self.fifo2.pop(0))
            self.run_deferred()
            self.call_next_A()
            self.mod_flush()
            self.have_ss = True
            if self.n_layers > 2 and l + 1 == self.n_layers and s_ + 1 < SEQ_PER_CORE:
                s.barrier()
        fin = Op("pool", None, None, False)
        fin.deps = stores
        s.ops.append(fin)
        s.streams["pool"].append(fin)
        s.emit()
        return self.nc


def band_mats():
    m = np.zeros((P, 8, 144), np.float32)
    tin = np.arange(P)[:, None]
    for g, w in enumerate(POOL_W):
        for first in range(2):
            a = np.zeros((P, 144), np.float32)
            jj = np.arange(128)[None, :]
            dlt = jj - tin
            cnt = np.minimum(jj + 1, w) if first else np.full_like(jj, w)
            a[:, :128] = np.where((dlt >= 0) & (dlt < w), 1.0 / cnt, 0.0) - (dlt == 0)
            j2 = np.arange(16)[None, :]
            d2 = 128 + j2 - tin
            a[:, 128:] = np.where((d2 >= 0) & (d2 < w), 1.0 / w, 0.0)
            m[:, first * 4 + g, :] = a
    return m


def make_cols(core, c, ada_b, norm_pre, norm_post, a_scale, kv_norm, kv_ada_b, b_lambda, b_subln):
    cols = np.zeros((P, NCOLS), np.float32)
    cc = np.asarray(c[core * 2:core * 2 + 2], np.float32)
    cols[:, C_CT:C_CT + 16] = cc.reshape(2, 8, P).transpose(2, 1, 0).reshape(P, 16)
    cols[:, C_NPRE:C_NPRE + 32] = np.asarray(norm_pre).reshape(4, 8, P).transpose(2, 0, 1).reshape(P, 32)
    cols[:, C_NPOST:C_NPOST + 32] = np.asarray(norm_post).reshape(4, 8, P).transpose(2, 0, 1).reshape(P, 32)
    cols[:, C_KVN:C_KVN + 8] = np.asarray(kv_norm).reshape(8, P).T
    ab = np.asarray(ada_b).reshape(4, 24, P).transpose(2, 0, 1)
    cols[:, C_ADAB:C_ADAB + 192] = np.repeat(ab[:, :, :, None], 2, axis=3).reshape(P, 192)
    kb = np.asarray(kv_ada_b).reshape(16, P).T
    cols[:, C_KVADAB:C_KVADAB + 32] = np.repeat(kb[:, :, None], 2, axis=2).reshape(P, 32)
    cols[:, C_ASC:C_ASC + 32] = np.asarray(a_scale).reshape(2, 16, P).transpose(2, 0, 1).reshape(P, 32)
    cols[:, C_SUBLN:C_SUBLN + 2] = np.asarray(b_subln).reshape(2, P).T
    cols[:, C_LAMB:C_LAMB + 512] = np.broadcast_to(np.asarray(b_lambda).reshape(1, 512), (P, 512))
    return cols


_NC_CACHE = {}


def kernel(x, c, ada_w, ada_b, norm_pre, norm_post, a_w_in, a_w_group, a_scale, a_w_out,
           kv_norm, kv_ada_w, kv_ada_b, w_kv, b_w_in, b_lambda, b_subln, b_w_out, _n_layers=DEPTH):
    f = lambda a: np.ascontiguousarray(np.asarray(a, dtype=np.float32))
    x = f(x)
    mats = np.concatenate([np.eye(P, dtype=np.float32), band_mats().reshape(P, 8 * 144)], axis=1)
    shared = {
        "mats": mats, "ada_w": f(ada_w), "a_w_in": f(a_w_in),
        "a_w_group": f(a_w_group).reshape(2, 2048, 512), "a_w_out": f(a_w_out),
        "kv_ada_w": f(kv_ada_w), "w_kv": f(w_kv), "b_w_in": f(b_w_in), "b_w_out": f(b_w_out),
    }
    in_maps = []
    for core in range(NCORES):
        m = dict(shared)
        m["x"] = x[core * 2:core * 2 + 2]
        m["cols"] = make_cols(core, f(c), f(ada_b), f(norm_pre), f(norm_post), f(a_scale),
                              f(kv_norm), f(kv_ada_b), f(b_lambda), f(b_subln))
        in_maps.append(m)
    nc = Builder(_n_layers).build()
    res = run_bass_kernel_spmd(nc, in_maps, core_ids=list(range(NCORES)))
    out = np.concatenate([np.asarray(r["out"]) for r in res.results], axis=0)
    return out.astype(np.float32)
```
